# Optimizing a Trainium2 kernel written in Bass

```python
import math
import numpy as np
import jax
import jax.numpy as jnp
from jax import lax


D_MODEL = 1024
BATCH = 16
SEQ = 2048
DEPTH = 2

GRID_W = 64
CTX_LEN = 256
N_BRANCH = 4
BRANCH_W = 512
QBLK = 128
ROPE_THETA = 10000.0
EPS = 1e-6
NA_HEADS = 8
NA_HEAD_DIM = 64
NA_ROW_WIN = 8
NA_COL_WIN = 16
MLA_HEADS = 8
MLA_Q_RANK = 384
MLA_KV_RANK = 256
MLA_NOPE = 64
MLA_ROPE = 32
MLA_V = 64
GQA_Q_HEADS = 8
GQA_KV_HEADS = 2
GQA_HEAD_DIM = 64
SSM_HEADS = 8
SSM_HEAD_DIM = 64
SSM_GROUPS = 2
SSM_STATE = 128
SSM_CONV = 5
SSM_CHUNK = 128
SSM_INNER = SSM_HEADS * SSM_HEAD_DIM
SSM_BC = SSM_GROUPS * SSM_STATE
SSM_CONV_CH = SSM_INNER + 2 * SSM_BC
FFN_HIDDEN = ((8 * D_MODEL + 3 * 256 - 1) // (3 * 256)) * 256

KEY_SPLITS = (('na_k', NA_HEADS * NA_HEAD_DIM), ('na_v', NA_HEADS * NA_HEAD_DIM),
              ('mla_ckv', MLA_KV_RANK), ('mla_kr', MLA_ROPE),
              ('gqa_k', GQA_KV_HEADS * GQA_HEAD_DIM), ('gqa_v', GQA_KV_HEADS * GQA_HEAD_DIM),
              ('ssm_x', SSM_INNER), ('ssm_B', SSM_BC), ('ssm_dt', 2 * SSM_HEADS))
QUERY_SPLITS = (('na_q', NA_HEADS * NA_HEAD_DIM), ('mla_cq', MLA_Q_RANK),
                ('gqa_q', GQA_Q_HEADS * GQA_HEAD_DIM), ('ssm_C', SSM_BC), ('ssm_z', SSM_INNER))
KEY_COLS = sum(w for _, w in KEY_SPLITS)
MIX_COLS = KEY_COLS + sum(w for _, w in QUERY_SPLITS)
IN_COLS = MIX_COLS + N_BRANCH * D_MODEL

kernel_name = "hybrid_gated_na_mla_gqa_ssd_prefix_dit"


def rmsnorm(x, g):
    xf = x.astype(jnp.float32)
    y = xf * lax.rsqrt(jnp.mean(xf * xf, axis=-1, keepdims=True) + EPS) * g.astype(jnp.float32)
    return y.astype(x.dtype)


def split_cols(p, with_query):
    names = KEY_SPLITS + (QUERY_SPLITS if with_query else ())
    out, off = {}, 0
    for name, w in names:
        out[name] = p[..., off:off + w]
        off += w
    return out


def to_heads(a, nh):
    b, t, w = a.shape
    return a.reshape(b, t, nh, w // nh).transpose(0, 2, 1, 3)


def merge_heads(o):
    b, g, r, t, d = o.shape
    return o.transpose(0, 3, 1, 2, 4).reshape(b, t, g * r * d)


def rope_tables(n, dim):
    t = jnp.arange(n)
    row = (t // GRID_W).astype(jnp.float32)
    col = (t % GRID_W).astype(jnp.float32)
    quarter = dim // 4
    inv = ROPE_THETA ** (-jnp.arange(quarter, dtype=jnp.float32) / quarter)
    ang = jnp.concatenate([row[:, None] * inv, col[:, None] * inv], axis=-1)
    return jnp.cos(ang), jnp.sin(ang)


def apply_rope(x, cos, sin):
    half = x.shape[-1] // 2
    x1, x2 = x[..., :half], x[..., half:]
    cos = cos.astype(x.dtype)
    sin = sin.astype(x.dtype)
    return jnp.concatenate([x1 * cos - x2 * sin, x1 * sin + x2 * cos], axis=-1)


def block_attention(q, k, v, scale):
    b, g, r, t, dk = q.shape
    nb = t // QBLK
    qb = jnp.moveaxis(q.reshape(b, g, r, nb, QBLK, dk), 3, 0)

    def one(qi):
        s = jnp.einsum('bgrqd,bgud->bgrqu', qi, k).astype(jnp.float32) * scale
        p = jax.nn.softmax(s, axis=-1).astype(v.dtype)
        return jnp.einsum('bgrqu,bgud->bgrqd', p, v)

    o = lax.map(one, qb)
    return jnp.moveaxis(o, 0, 3).reshape(b, g, r, t, v.shape[-1])


def na_mixer(pl, pc, rpb, need_ctx):
    bsz, s, _ = pl['na_q'].shape
    rows = s // GRID_W
    kr = min(NA_ROW_WIN, rows)
    nh, dh, kcw = NA_HEADS, NA_HEAD_DIM, NA_COL_WIN
    scale = dh ** -0.5

    def grid(a):
        return a.reshape(bsz, rows, GRID_W, nh, dh).transpose(0, 3, 1, 2, 4)

    qg, kg, vg = grid(pl['na_q']), grid(pl['na_k']), grid(pl['na_v'])
    kc, vc = to_heads(pc['na_k'], nh), to_heads(pc['na_v'], nh)
    col_start = np.clip(np.arange(GRID_W) - kcw // 2, 0, GRID_W - kcw)
    col_idx = col_start[:, None] + np.arange(kcw)[None, :]
    col_off = col_idx - np.arange(GRID_W)[:, None] + kcw - 1
    rpb_cols = rpb[:, :, col_off].astype(jnp.float32)
    n_loc = kr * kcw

    def one_row(r):
        rs = jnp.clip(r - kr // 2, 0, rows - kr)
        q_r = lax.dynamic_index_in_dim(qg, r, axis=2, keepdims=False)
        k_win = lax.dynamic_slice_in_dim(kg, rs, kr, axis=2)[:, :, :, col_idx]
        v_win = lax.dynamic_slice_in_dim(vg, rs, kr, axis=2)[:, :, :, col_idx]
        bias = jnp.take(rpb_cols, rs + jnp.arange(kr) - r + NA_ROW_WIN - 1, axis=1)
        s_loc = jnp.einsum('bhwd,bhiwjd->bhwij', q_r, k_win).astype(jnp.float32) * scale + jnp.transpose(bias, (0, 2, 1, 3))
        s_ctx = jnp.einsum('bhwd,bhld->bhwl', q_r, kc).astype(jnp.float32) * scale
        p = jax.nn.softmax(jnp.concatenate([s_loc.reshape(bsz, nh, GRID_W, n_loc), s_ctx], axis=-1), axis=-1).astype(vg.dtype)
        return (jnp.einsum('bhwij,bhiwjd->bhwd', p[..., :n_loc].reshape(bsz, nh, GRID_W, kr, kcw), v_win)
                + jnp.einsum('bhwl,bhld->bhwd', p[..., n_loc:], vc))

    o = lax.map(one_row, jnp.arange(rows))
    y = o.transpose(1, 0, 3, 2, 4).reshape(bsz, s, nh * dh)
    yc = None
    if need_ctx:
        yc = merge_heads(block_attention(to_heads(pc['na_q'], nh)[:, :, None], kc, vc, scale))
    return y, yc


def mla_mixer(pl, pc, g_q, g_kv, w_uq, w_ukv, rope, need_ctx):
    cos, sin = rope
    scale = (MLA_NOPE + MLA_ROPE) ** -0.5

    def queries(p, rotate):
        q = to_heads(rmsnorm(p['mla_cq'], g_q) @ w_uq, MLA_HEADS)
        q_nope, q_pe = q[..., :MLA_NOPE], q[..., MLA_NOPE:]
        if rotate:
            q_pe = apply_rope(q_pe, cos, sin)
        return jnp.concatenate([q_nope, q_pe], axis=-1)[:, :, None]

    def keys_values(p, rotate):
        kv = to_heads(rmsnorm(p['mla_ckv'], g_kv) @ w_ukv, MLA_HEADS)
        k_nope, v = kv[..., :MLA_NOPE], kv[..., MLA_NOPE:]
        k_pe = p['mla_kr'][:, None]
        if rotate:
            k_pe = apply_rope(k_pe, cos, sin)
        k = jnp.concatenate([k_nope, jnp.broadcast_to(k_pe, k_nope.shape[:-1] + (MLA_ROPE,))], axis=-1)
        return k, v

    kl, vl = keys_values(pl, True)
    kc, vc = keys_values(pc, False)
    y = merge_heads(block_attention(queries(pl, True), jnp.concatenate([kc, kl], axis=2),
                                    jnp.concatenate([vc, vl], axis=2), scale))
    yc = None
    if need_ctx:
        yc = merge_heads(block_attention(queries(pc, False), kc, vc, scale))
    return y, yc


def gqa_mixer(pl, pc, g_q, g_k, rope, need_ctx):
    cos, sin = rope
    rep = GQA_Q_HEADS // GQA_KV_HEADS
    scale = GQA_HEAD_DIM ** -0.5

    def queries(p, rotate):
        q = rmsnorm(to_heads(p['gqa_q'], GQA_Q_HEADS), g_q)
        if rotate:
            q = apply_rope(q, cos, sin)
        b, _, t, d = q.shape
        return q.reshape(b, GQA_KV_HEADS, rep, t, d)

    def keys_values(p, rotate):
        k = rmsnorm(to_heads(p['gqa_k'], GQA_KV_HEADS), g_k)
        if rotate:
            k = apply_rope(k, cos, sin)
        return k, to_heads(p['gqa_v'], GQA_KV_HEADS)

    kl, vl = keys_values(pl, True)
    kc, vc = keys_values(pc, False)
    y = merge_heads(block_attention(queries(pl, True), jnp.concatenate([kc, kl], axis=2),
                                    jnp.concatenate([vc, vl], axis=2), scale))
    yc = None
    if need_ctx:
        yc = merge_heads(block_attention(queries(pc, False), kc, vc, scale))
    return y, yc


def dwconv_centred(x, w, b):
    k = w.shape[0]
    y = lax.conv_general_dilated(x, w[:, None, :], window_strides=(1,), padding=[(k // 2, k // 2)],
                                 dimension_numbers=('NWC', 'WIO', 'NWC'), feature_group_count=x.shape[-1])
    return y + b


def ssd(x, dt, a, bm, cm, h0):
    bsz, t, nh, hp = x.shape
    ng, ns = bm.shape[2], bm.shape[3]
    r = nh // ng
    q = SSM_CHUNK
    nc = t // q
    f32 = jnp.float32
    xc = x.astype(f32).reshape(bsz, nc, q, ng, r, hp)
    dtc = dt.reshape(bsz, nc, q, ng, r)
    bc = bm.astype(f32).reshape(bsz, nc, q, ng, ns)
    a_cum = jnp.cumsum(dtc * a.reshape(ng, r), axis=2)
    a_tot = a_cum[:, :, -1]
    xdt = xc * dtc[..., None]
    s_loc = jnp.einsum('bcsgn,bcsgr,bcsgrp->bcgrpn', bc, jnp.exp(a_tot[:, :, None] - a_cum), xdt)

    def step(h, inp):
        s_c, at = inp
        return jnp.exp(at)[..., None, None] * h + s_c, h

    h_fin, h_in = lax.scan(step, h0.reshape(bsz, ng, r, hp, ns),
                           (jnp.moveaxis(s_loc, 1, 0), jnp.moveaxis(a_tot, 1, 0)))
    h_fin = h_fin.reshape(bsz, nh, hp, ns)
    if cm is None:
        return None, h_fin
    cc = cm.astype(f32).reshape(bsz, nc, q, ng, ns)
    h_in = jnp.moveaxis(h_in, 0, 1)
    lower = np.tril(np.ones((q, q), dtype=bool))[:, :, None, None]
    seg = a_cum[:, :, :, None] - a_cum[:, :, None, :]
    lmat = jnp.exp(jnp.where(lower, seg, -jnp.inf))
    cb = jnp.einsum('bcqgn,bcsgn->bcqsg', cc, bc)
    y = (jnp.einsum('bcqsg,bcqsgr,bcsgrp->bcqgrp', cb, lmat, xdt)
         + jnp.einsum('bcqgn,bcgrpn->bcqgrp', cc, h_in) * jnp.exp(a_cum)[..., None])
    return y.reshape(bsz, t, nh, hp).astype(x.dtype), h_fin


def ssm_mixer(pl, pc, conv_w, conv_b, a_log, dt_bias, d_skip, g_norm, need_ctx):
    a = -jnp.exp(a_log.astype(jnp.float32))
    fl = lambda arr: None if arr is None else jnp.flip(arr, axis=1)

    def prep(p, with_c):
        chans = [p['ssm_x'], p['ssm_B']] + ([p['ssm_C']] if with_c else [])
        xbc = jnp.concatenate(chans, axis=-1)
        nch = xbc.shape[-1]
        xbc = jax.nn.silu(dwconv_centred(xbc, conv_w[:, :nch], conv_b[:nch]))
        b, t, _ = xbc.shape
        xs = xbc[..., :SSM_INNER].reshape(b, t, SSM_HEADS, SSM_HEAD_DIM)
        bm = xbc[..., SSM_INNER:SSM_INNER + SSM_BC].reshape(b, t, SSM_GROUPS, SSM_STATE)
        cm = xbc[..., SSM_INNER + SSM_BC:].reshape(b, t, SSM_GROUPS, SSM_STATE) if with_c else None
        dt = jax.nn.softplus(p['ssm_dt'].astype(jnp.float32) + dt_bias.reshape(-1).astype(jnp.float32))
        return xs, bm, cm, dt[..., :SSM_HEADS], dt[..., SSM_HEADS:]

    def combine(xs, y_f, y_b, z):
        b, t = xs.shape[:2]
        y = (y_f + y_b + d_skip[:, None] * xs).reshape(b, t, SSM_INNER)
        return rmsnorm(y * jax.nn.silu(z), g_norm)

    xc, bc, cc, dtc_f, dtc_b = prep(pc, need_ctx)
    h0 = jnp.zeros((xc.shape[0], SSM_HEADS, SSM_HEAD_DIM, SSM_STATE), jnp.float32)
    yc_f, hc_f = ssd(xc, dtc_f, a[0], bc, cc, h0)
    yc_b, hc_b = ssd(fl(xc), fl(dtc_b), a[1], fl(bc), fl(cc), h0)
    xs, bm, cm, dt_f, dt_b = prep(pl, True)
    y_f, _ = ssd(xs, dt_f, a[0], bm, cm, hc_f)
    y_b, _ = ssd(fl(xs), fl(dt_b), a[1], fl(bm), fl(cm), hc_b)
    y = combine(xs, y_f, fl(y_b), pl['ssm_z'])
    yc = combine(xc, yc_f, fl(yc_b), pc['ssm_z']) if need_ctx else None
    return y, yc


def merge_branches(ys, gate_logits, w_branch, w_out):
    b, t, _ = gate_logits.shape
    g = jax.nn.sigmoid(gate_logits).reshape(b, t, N_BRANCH, D_MODEL)
    proj = jnp.einsum('btke,ked->btkd', jnp.stack(ys, axis=2), w_branch)
    return jnp.sum(g * proj, axis=2) @ w_out


def swiglu(h, w1, w3, w2):
    return (jax.nn.silu(h @ w1) * (h @ w3)) @ w2


def layer(x, xc, c, c_ctx, w_ada, b_ada, g_pre1, g_post1, g_pre2, g_post2, w_in, na_rpb,
          mla_g_q, mla_g_kv, mla_w_uq, mla_w_ukv, gqa_g_q, gqa_g_k, ssm_conv_w, ssm_conv_b,
          ssm_a_log, ssm_dt_bias, ssm_d, ssm_g_norm, w_branch, w_out, ffn_w1, ffn_w3, ffn_w2,
          rope_mla, rope_gqa, need_ctx):
    d = D_MODEL
    sh1, sc1, gt1, sh2, sc2, gt2 = jnp.split((jax.nn.silu(c) @ w_ada + b_ada)[:, None, :], 6, axis=-1)
    n_mod = 6 if need_ctx else 2
    mc = jnp.split(jax.nn.silu(c_ctx) @ w_ada[:, :n_mod * d] + b_ada[:n_mod * d], n_mod, axis=-1)
    h = rmsnorm(x, g_pre1) * (1 + sc1) + sh1
    hc = rmsnorm(xc, g_pre1) * (1 + mc[1]) + mc[0]
    p = h @ w_in
    pc = hc @ (w_in if need_ctx else w_in[:, :KEY_COLS])
    pl_, pc_ = split_cols(p, True), split_cols(pc, need_ctx)
    y_na, yc_na = na_mixer(pl_, pc_, na_rpb, need_ctx)
    y_mla, yc_mla = mla_mixer(pl_, pc_, mla_g_q, mla_g_kv, mla_w_uq, mla_w_ukv, rope_mla, need_ctx)
    y_gqa, yc_gqa = gqa_mixer(pl_, pc_, gqa_g_q, gqa_g_k, rope_gqa, need_ctx)
    y_ssm, yc_ssm = ssm_mixer(pl_, pc_, ssm_conv_w, ssm_conv_b, ssm_a_log, ssm_dt_bias, ssm_d, ssm_g_norm, need_ctx)
    mix = merge_branches([y_na, y_mla, y_gqa, y_ssm], p[..., MIX_COLS:], w_branch, w_out)
    x = x + gt1 * rmsnorm(mix, g_post1)
    h2 = rmsnorm(x, g_pre2) * (1 + sc2) + sh2
    x = x + gt2 * rmsnorm(swiglu(h2, ffn_w1, ffn_w3, ffn_w2), g_post2)
    if need_ctx:
        mix_c = merge_branches([yc_na, yc_mla, yc_gqa, yc_ssm], pc[..., MIX_COLS:], w_branch, w_out)
        xc = xc + mc[2] * rmsnorm(mix_c, g_post1)
        hc2 = rmsnorm(xc, g_pre2) * (1 + mc[4]) + mc[3]
        xc = xc + mc[5] * rmsnorm(swiglu(hc2, ffn_w1, ffn_w3, ffn_w2), g_post2)
    return x, xc


def setup_inputs(seed: int = 0) -> dict:
    key = jax.random.key(seed)
    ks = iter(jax.random.split(key, 40))
    f32 = jnp.float32

    def nrm(shape, s):
        return jax.random.normal(next(ks), shape, f32) * s

    L, D = DEPTH, D_MODEL
    x = nrm((BATCH, SEQ, D), 1.0)
    c = nrm((BATCH, D), 1.0)
    ctx = nrm((BATCH, CTX_LEN, D), 1.0)
    c_ctx = nrm((D,), 1.0)
    w_ada = nrm((L, D, 6 * D), 0.5 * D ** -0.5)
    b_ada = nrm((L, 6 * D), 0.02)
    g_pre1 = 1.0 + nrm((L, D), 0.05)
    g_post1 = 1.0 + nrm((L, D), 0.05)
    g_pre2 = 1.0 + nrm((L, D), 0.05)
    g_post2 = 1.0 + nrm((L, D), 0.05)
    w_in = nrm((L, D, IN_COLS), D ** -0.5)
    na_rpb = nrm((L, NA_HEADS, 2 * NA_ROW_WIN - 1, 2 * NA_COL_WIN - 1), 0.1)
    mla_g_q = 1.0 + nrm((L, MLA_Q_RANK), 0.05)
    mla_g_kv = 1.0 + nrm((L, MLA_KV_RANK), 0.05)
    mla_w_uq = nrm((L, MLA_Q_RANK, MLA_HEADS * (MLA_NOPE + MLA_ROPE)), MLA_Q_RANK ** -0.5)
    mla_w_ukv = nrm((L, MLA_KV_RANK, MLA_HEADS * (MLA_NOPE + MLA_V)), MLA_KV_RANK ** -0.5)
    gqa_g_q = 1.0 + nrm((L, GQA_HEAD_DIM), 0.05)
    gqa_g_k = 1.0 + nrm((L, GQA_HEAD_DIM), 0.05)
    ssm_conv_w = nrm((L, SSM_CONV, SSM_CONV_CH), SSM_CONV ** -0.5)
    ssm_conv_b = nrm((L, SSM_CONV_CH), 0.02)
    ssm_a_log = jnp.log(jax.random.uniform(next(ks), (L, 2, SSM_HEADS), f32, minval=1.0, maxval=16.0))
    dt0 = jnp.exp(jax.random.uniform(next(ks), (L, 2, SSM_HEADS), f32, minval=math.log(1e-3), maxval=math.log(1e-1)))
    ssm_dt_bias = dt0 + jnp.log(-jnp.expm1(-dt0))
    ssm_d = 1.0 + nrm((L, SSM_HEADS), 0.1)
    ssm_g_norm = 1.0 + nrm((L, SSM_INNER), 0.05)
    w_branch = nrm((L, N_BRANCH, BRANCH_W, D), BRANCH_W ** -0.5)
    w_out = nrm((L, D, D), D ** -0.5)
    ffn_w1 = nrm((L, D, FFN_HIDDEN), D ** -0.5)
    ffn_w3 = nrm((L, D, FFN_HIDDEN), D ** -0.5)
    ffn_w2 = nrm((L, FFN_HIDDEN, D), FFN_HIDDEN ** -0.5)
    return {"x": x, "c": c, "ctx": ctx, "c_ctx": c_ctx, "w_ada": w_ada, "b_ada": b_ada,
            "g_pre1": g_pre1, "g_post1": g_post1, "g_pre2": g_pre2, "g_post2": g_post2,
            "w_in": w_in, "na_rpb": na_rpb, "mla_g_q": mla_g_q, "mla_g_kv": mla_g_kv,
            "mla_w_uq": mla_w_uq, "mla_w_ukv": mla_w_ukv, "gqa_g_q": gqa_g_q, "gqa_g_k": gqa_g_k,
            "ssm_conv_w": ssm_conv_w, "ssm_conv_b": ssm_conv_b, "ssm_a_log": ssm_a_log,
            "ssm_dt_bias": ssm_dt_bias, "ssm_d": ssm_d, "ssm_g_norm": ssm_g_norm,
            "w_branch": w_branch, "w_out": w_out, "ffn_w1": ffn_w1, "ffn_w3": ffn_w3, "ffn_w2": ffn_w2}


def reference(x, c, ctx, c_ctx, w_ada, b_ada, g_pre1, g_post1, g_pre2, g_post2, w_in, na_rpb,
              mla_g_q, mla_g_kv, mla_w_uq, mla_w_ukv, gqa_g_q, gqa_g_k, ssm_conv_w, ssm_conv_b,
              ssm_a_log, ssm_dt_bias, ssm_d, ssm_g_norm, w_branch, w_out, ffn_w1, ffn_w3, ffn_w2):
    s = x.shape[1]
    rope_mla = rope_tables(s, MLA_ROPE)
    rope_gqa = rope_tables(s, GQA_HEAD_DIM)
    xc = ctx
    for l in range(DEPTH):
        x, xc = layer(x, xc, c, c_ctx, w_ada[l], b_ada[l], g_pre1[l], g_post1[l], g_pre2[l], g_post2[l],
                      w_in[l], na_rpb[l], mla_g_q[l], mla_g_kv[l], mla_w_uq[l], mla_w_ukv[l],
                      gqa_g_q[l], gqa_g_k[l], ssm_conv_w[l], ssm_conv_b[l], ssm_a_log[l], ssm_dt_bias[l],
                      ssm_d[l], ssm_g_norm[l], w_branch[l], w_out[l], ffn_w1[l], ffn_w3[l], ffn_w2[l],
                      rope_mla, rope_gqa, l < DEPTH - 1)
    return x
```

```python
from contextlib import ExitStack
import numpy as np
from concourse.bass_utils import run_bass_kernel_spmd
import numpy as np
import concourse.bass as bass
import concourse.mybir as mybir
F32 = mybir.dt.float32
BF16 = mybir.dt.bfloat16
AF = mybir.ActivationFunctionType
ALU = mybir.AluOpType
AX = mybir.AxisListType

class Tok:
    __slots__ = ("w", "r", "name", "excl")
    def __init__(self, name="", excl=False):
        self.w = {}
        self.r = {}
        self.name = name
        self.excl = excl

class Q:
    def __init__(self, fw, name, attr, sem):
        self.fw = fw; self.name = name; self.attr = attr; self.sem = sem
        self.key = name
        self.count = 0
        self.seen = {}
        self.ops = []
        self.pending = False
        self.dsems = []
        self.dtarget = {}
        self.dnext = 0

class FW:
    def __init__(self, nc, stack, n_dma_sems=6):
        self.nc = nc
        self.q = {}
        self.semh = {}
        for name, attr in (("pe", "tensor"), ("act", "scalar"), ("dve", "vector"), ("pool", "gpsimd"), ("sp", "sync")):
            s = stack.enter_context(nc.semaphore("s_" + name))
            self.q[name] = Q(self, name, attr, s)
            self.semh[name] = s
        for qn in ("sp", "pool"):
            q = self.q[qn]
            for i in range(n_dma_sems):
                key = f"d_{qn}{i}"
                s = stack.enter_context(nc.semaphore(key))
                self.semh[key] = s
                q.dsems.append(key)
                q.dtarget[key] = 0
        self.n_instr = 0

    def _wait(self, q, key, val):
        if q.seen.get(key, 0) < val:
            q.ops.append(("w", key, val))
            q.seen[key] = val

    def _deps(self, q, reads, writes):
        for t in reads:
            for k, v in t.w.items():
                self._dep1(q, k, v)
            if t.excl:
                for k, v in t.r.items():
                    if k != q.key:
                        self._dep1(q, k, v)
        for t in writes:
            for k, v in t.w.items():
                self._dep1(q, k, v)
            for k, v in t.r.items():
                self._dep1(q, k, v)

    def _dep1(self, q, k, v):
        if k == q.key and v > q.count:
            return
        self._wait(q, k, v)

    def op(self, qn, fn, reads=(), writes=(), signal=True):
        q = self.q[qn]
        self._deps(q, reads, writes)
        if signal:
            q.count += 1
            q.ops.append(("i", fn, True))
            q.pending = False
        else:
            q.ops.append(("i", fn, False))
            q.pending = True
        ev = (q.key, q.count if signal else q.count + 1)
        self._mark(ev, reads, writes)
        self.n_instr += 1

    def _mark(self, ev, reads, writes):
        k, v = ev
        for t in writes:
            t.w = {k: v}
            t.r = {}
        for t in reads:
            if t.r.get(k, 0) < v:
                t.r[k] = v

    def dma(self, qn, out, in_, reads=(), writes=(), **kw):
        q = self.q[qn]
        self._deps(q, reads, writes)
        key = q.dsems[q.dnext]
        q.dnext = (q.dnext + 1) % len(q.dsems)
        self._wait(q, key, q.dtarget[key])
        q.dtarget[key] += 16
        q.ops.append(("d", out, in_, key, kw))
        self._mark((key, q.dtarget[key]), reads, writes)
        self.n_instr += 1

    def barrier(self):
        cur = {}
        for q in self.q.values():
            assert not q.pending, q.name
            cur[q.key] = q.count
            for k in q.dsems:
                cur[k] = q.dtarget[k]
        for q in self.q.values():
            for k, v in cur.items():
                if k == q.key:
                    continue
                if v > 0:
                    self._wait(q, k, v)

    def replay(self):
        nc = self.nc
        fwself = self
        with nc.Block() as block:
            def mk(q):
                def body(eng):
                    for o in q.ops:
                        if o[0] == "w":
                            eng.wait_ge(fwself.semh[o[1]], o[2])
                        elif o[0] == "i":
                            ins = o[1](eng)
                            if o[2]:
                                ins.then_inc(q.sem, 1)
                        else:
                            eng.dma_start(out=o[1], in_=o[2], **o[4]).then_inc(fwself.semh[o[3]], 16)
                return body
            block.tensor(mk(self.q["pe"]))
            block.scalar(mk(self.q["act"]))
            block.vector(mk(self.q["dve"]))
            block.gpsimd(mk(self.q["pool"]))
            block.sync(mk(self.q["sp"]))

D = 1024
NTOK = 2304
NT = 18
EPS = 1e-6
C_NAK, C_NAV, C_CKV, C_KR, C_GK, C_GV, C_SX, C_SB, C_SDT = 0, 512, 1024, 1280, 1312, 1440, 1568, 2080, 2336
C_NAQ, C_CQ, C_GQ, C_SC, C_SZ, C_GATE = 2352, 2864, 3248, 3760, 4016, 4528
FFH = 2816
NEG = -30000.0


def na_chunks(t):
    if t <= 1:
        return 1 + t, [0, 1, 2, 3]
    if t >= 14:
        return 3 + (t - 14), [12, 13, 14, 15]
    return 0, [t - 2, t - 1, t, t + 1, t + 2]


def build_program(debug=False, layers=(0, 1), stages=("mod", "norm1", "gqa", "mla", "na", "ssm", "merge", "ffn")):
    nc = bass.Bass("TRN2", target_bir_lowering=False)
    dt_in = lambda name, shape, dt=F32: nc.dram_tensor(name, list(shape), dt, kind="ExternalInput").ap()
    kind_scr = "ExternalOutput" if debug else "Internal"
    dt_scr = lambda name, shape, dt=F32: nc.dram_tensor(name, list(shape), dt, kind=kind_scr).ap()
    x_in = dt_in("x", [2, 2048, D])
    ctx_in = dt_in("ctx", [2, 256, D])
    cvec = dt_in("cvec", [3, D])
    w_ada = dt_in("w_ada", [2, D, 6 * D])
    b_ada = dt_in("b_ada", [2, 6 * D])
    g4 = dt_in("g4", [2, 4, D])
    w_in = dt_in("w_in", [2, D, 8624])
    nab = dt_in("nab", [2, 5, 128, 8, 5, 128])
    mla_g_q = dt_in("mla_g_q", [2, 384])
    mla_g_kv = dt_in("mla_g_kv", [2, 256])
    mla_w_uq = dt_in("mla_w_uq", [2, 384, 768])
    mla_w_ukv = dt_in("mla_w_ukv", [2, 256, 1024])
    gqa_g = dt_in("gqa_g", [2, 2, 64])
    conv_wT = dt_in("conv_wT", [2, D, 5])
    conv_b = dt_in("conv_b", [2, D])
    ssm_small = dt_in("ssm_small", [2, 40])
    ssm_g_norm = dt_in("ssm_g_norm", [2, 512])
    w_branch = dt_in("w_branch", [2, 4, 512, D])
    w_out = dt_in("w_out", [2, D, D])
    ffn_w1 = dt_in("ffn_w1", [2, D, FFH])
    ffn_w3 = dt_in("ffn_w3", [2, D, FFH])
    ffn_w2 = dt_in("ffn_w2", [2, FFH, D])
    consts = dt_in("consts", [128, 14, 128])
    ropeg = dt_in("ropeg", [128, 2, 16, 32])
    ropem = dt_in("ropem", [128, 2, 16, 16])
    out = nc.dram_tensor("out", [2, 2048, D], F32, kind="ExternalOutput").ap()
    combd = dt_scr("combd", [2, 3, 6, D])
    yTd = dt_scr("yTd", [4, 512, NTOK], BF16)
    xmid = dt_scr("xmid", [2, NTOK, D])
    xres = dt_scr("xres", [2, NTOK, D])

    ST = stages
    with ExitStack() as top:
        fw = FW(nc, top)
        uid = [0]

        def sb(es, shape, dt=F32, name="t"):
            uid[0] += 1
            t = es.enter_context(nc.sbuf_tensor(f"{name}{uid[0]}", list(shape), dt))
            return t

        ps = [top.enter_context(nc.psum_tensor(f"ps{i}", [128, 512], F32)) for i in range(8)]
        tp = [Tok(f"ps{i}", excl=True) for i in range(8)]
        psb = [p[:].bitcast(BF16) for p in ps]
        cst = sb(top, [128, 14, 128], F32, "cst")
        ident = sb(top, [128, 128], BF16, "ident")
        rg = sb(top, [128, 2, 16, 32], F32, "rg")
        rm = sb(top, [128, 2, 16, 16], F32, "rm")
        t_c = Tok("consts")
        fw.dma("sp", cst[:], consts, writes=[t_c])
        fw.dma("sp", rg[:], ropeg, writes=[t_c])
        fw.dma("sp", rm[:], ropem, writes=[t_c])
        fw.op("dve", lambda e: e.tensor_copy(out=ident[:], in_=cst[:, 0, :]), [t_c], [t_c])
        identf = cst[:, 0, :]
        tri = [cst[:, 1, :], cst[:, 2, :]]
        mneg4 = [cst[:, 3:7, :].rearrange("p a q -> p (a q)"), cst[:, 7:11, :].rearrange("p a q -> p (a q)")]
        onesf = cst[:, 11, :]
        ntri = [cst[:, 12, :], cst[:, 13, :]]
        t_comb = [Tok(f"comb{l}") for l in range(2)]
        t_yT = {}
        t_xmid = {}
        t_xres = {}

        def tk(dct, key):
            if key not in dct:
                dct[key] = Tok(str(key))
            return dct[key]

        def V(fn, r, w):
            fw.op("dve", fn, r, w)

        def A(fn, r, w):
            fw.op("act", fn, r, w)

        def G(fn, r, w):
            fw.op("pool", fn, r, w)

        def MM(o, lhsT, rhs, r, w, start=True, stop=True, sig=True):
            fw.op("pe", lambda e: e.matmul(o, lhsT=lhsT, rhs=rhs, start=start, stop=stop), r, w, signal=sig)

        def TR(o, in_, idn, r, w, sig=True):
            fw.op("pe", lambda e: e.transpose(out=o, in_=in_, identity=idn), r, w, signal=sig)

        def load_w(dst, src, tok, q="pool"):
            fw.dma(q, dst, src.rearrange("(k p) n -> p k n", p=128), writes=[tok])

        def bcast_load(dst, src_row, tok, parts=128):
            fw.dma("sp", dst, src_row.partition_broadcast(parts), writes=[tok])

        def rsqrt_mean(ap, n, r_w):
            A(lambda e: e.activation(out=ap, in_=ap, func=AF.Ln, scale=1.0 / n, bias=EPS), r_w, r_w)
            A(lambda e: e.activation(out=ap, in_=ap, func=AF.Exp, scale=-0.5), r_w, r_w)

        def xsrc(l, b, ti):
            if l == 0:
                if ti < 2:
                    return ctx_in[b, ti * 128:(ti + 1) * 128, :], []
                return x_in[b, (ti - 2) * 128:(ti - 1) * 128, :], []
            return xres[b, ti * 128:(ti + 1) * 128, :], [tk(t_xres, (b, ti))]

        def rope(dst3, src3, cos, sin, H, half, tmp, toks_r, tok_tmp, tok_dst):
            cb = cos.unsqueeze(1).broadcast_to([128, H, half])
            sn = sin.unsqueeze(1).broadcast_to([128, H, half])
            x1 = src3[:, :, 0:half]
            x2 = src3[:, :, half:2 * half]
            V(lambda e: e.tensor_tensor(out=tmp[:, 0], in0=x1, in1=cb, op=ALU.mult), toks_r, [tok_tmp])
            V(lambda e: e.tensor_tensor(out=tmp[:, 1], in0=x2, in1=sn, op=ALU.mult), toks_r, [tok_tmp])
            G(lambda e: e.tensor_tensor(out=tmp[:, 2], in0=x1, in1=sn, op=ALU.mult), toks_r, [tok_tmp])
            G(lambda e: e.tensor_tensor(out=tmp[:, 3], in0=x2, in1=cb, op=ALU.mult), toks_r, [tok_tmp])
            V(lambda e: e.tensor_tensor(out=dst3[:, :, 0:half], in0=tmp[:, 0], in1=tmp[:, 1], op=ALU.subtract), [tok_tmp], [tok_dst])
            V(lambda e: e.tensor_tensor(out=dst3[:, :, half:2 * half], in0=tmp[:, 2], in1=tmp[:, 3], op=ALU.add), [tok_tmp], [tok_dst])

        def stage_mod(l):
            with ExitStack() as es:
                cin = sb(es, [3, D]); t_cin = Tok()
                cs = sb(es, [3, D], BF16)
                csT = sb(es, [128, 8, 4], BF16); t_csT = Tok()
                modr = sb(es, [3, 6 * D]); t_mod = Tok()
                bad = sb(es, [3, 6 * D]); t_bad = Tok()
                g4t = sb(es, [3, 4, D]); t_g4 = Tok()
                comb = sb(es, [3, 6, D]); t_cb = Tok()
                wts = [sb(es, [128, 8, 512], BF16) for _ in range(2)]
                t_w = [Tok(), Tok()]
                fw.dma("sp", cin[:], cvec, writes=[t_cin])
                bcast_load(bad[:], b_ada[l, :], t_bad, 3)
                for j in range(4):
                    bcast_load(g4t[:, j, :], g4[l, j, :], t_g4, 3)
                A(lambda e: e.activation(out=cs[:], in_=cin[:], func=AF.Silu), [t_cin], [t_cin])
                for kc in range(8):
                    TR(psb[7][:, kc * 4:kc * 4 + 3], cs[:, kc * 128:(kc + 1) * 128], ident[0:3, 0:3], [t_cin, t_c], [tp[7]], sig=(kc == 7))
                V(lambda e: e.tensor_copy(out=csT[:, :, 0:3], in_=psb[7][:, 0:32].rearrange("p (k f) -> p k f", f=4)[:, :, 0:3]), [tp[7]], [t_csT])
                for n in range(12):
                    wt = wts[n % 2]
                    load_w(wt[:], w_ada[l, :, n * 512:(n + 1) * 512], t_w[n % 2])
                    pt = ps[n % 2]
                    for kc in range(8):
                        MM(pt[0:3, :], csT[:, kc, 0:3], wt[:, kc, :], [t_csT, t_w[n % 2]], [tp[n % 2]], start=(kc == 0), stop=(kc == 7), sig=(kc == 7))
                    V(lambda e, n=n, pt=pt: e.tensor_tensor(out=modr[:, n * 512:(n + 1) * 512], in0=pt[0:3, :], in1=bad[:, n * 512:(n + 1) * 512], op=ALU.add),
                      [tp[n % 2], t_bad], [t_mod])
                m = lambda j: modr[:, j * D:(j + 1) * D]
                V(lambda e: e.scalar_tensor_tensor(out=comb[:, 0, :], in0=m(1), scalar=1.0, in1=g4t[:, 0, :], op0=ALU.add, op1=ALU.mult), [t_mod, t_g4], [t_cb])
                V(lambda e: e.tensor_copy(out=comb[:, 1, :], in_=m(0)), [t_mod], [t_cb])
                V(lambda e: e.tensor_tensor(out=comb[:, 2, :], in0=m(2), in1=g4t[:, 1, :], op=ALU.mult), [t_mod, t_g4], [t_cb])
                V(lambda e: e.scalar_tensor_tensor(out=comb[:, 3, :], in0=m(4), scalar=1.0, in1=g4t[:, 2, :], op0=ALU.add, op1=ALU.mult), [t_mod, t_g4], [t_cb])
                V(lambda e: e.tensor_copy(out=comb[:, 4, :], in_=m(3)), [t_mod], [t_cb])
                V(lambda e: e.tensor_tensor(out=comb[:, 5, :], in0=m(5), in1=g4t[:, 3, :], op=ALU.mult), [t_mod, t_g4], [t_cb])
                fw.dma("sp", combd[l], comb[:], reads=[t_cb], writes=[t_comb[l]])
                fw.barrier()

        def norm_mod_T(es, xt, t_x, Ab, Bb, t_ab, hdst, t_hdst, wk):
            junk, ss, t1, hb, t_wk = wk
            A(lambda e: e.activation(out=junk[:], in_=xt, func=AF.Square, accum_out=ss[:]), [t_x], [t_wk])
            rsqrt_mean(ss[:], D, [t_wk])
            V(lambda e: e.scalar_tensor_tensor(out=t1[:], in0=xt, scalar=ss[:, 0:1], in1=Ab, op0=ALU.mult, op1=ALU.mult), [t_x, t_wk, t_ab], [t_wk])
            G(lambda e: e.tensor_tensor(out=hb[:], in0=t1[:], in1=Bb, op=ALU.add), [t_wk, t_ab], [t_wk])
            for kc in range(8):
                TR(psb[7][:, kc * 128:(kc + 1) * 128], hb[:, kc * 128:(kc + 1) * 128], ident[:], [t_wk, t_c], [tp[7]], sig=(kc == 7))
            A(lambda e: e.activation(out=hdst, in_=psb[7].rearrange("p (k t) -> p k t", k=8), func=AF.Copy), [tp[7]], [t_hdst])

        def load_comb(es, l, b, j0):
            tl = {}
            t_ab = Tok()
            for rname, row in (("b", b), ("c", 2)):
                for j in range(j0, j0 + 3):
                    t = sb(es, [128, D])
                    fw.dma("sp", t[:], combd[l, row, j, :].partition_broadcast(128), reads=[t_comb[l]], writes=[t_ab])
                    tl[(rname, j)] = t
            return tl, t_ab

        def stage_norm1(l, b, hT, t_hT):
            with ExitStack() as es:
                tl, t_ab = load_comb(es, l, b, 0)
                xb = [sb(es, [128, D]) for _ in range(2)]
                t_xb = [Tok(), Tok()]
                wks = []
                for i in range(2):
                    wks.append((sb(es, [128, D], BF16), sb(es, [128, 1]), sb(es, [128, D]), sb(es, [128, D], BF16), Tok()))
                for ti in range(NT):
                    src, rt = xsrc(l, b, ti)
                    fw.dma("sp", xb[ti % 2][:], src, reads=rt, writes=[t_xb[ti % 2]])
                    rn = "c" if ti < 2 else "b"
                    norm_mod_T(es, xb[ti % 2][:], t_xb[ti % 2], tl[(rn, 0)][:], tl[(rn, 1)][:], t_ab,
                               hT[:, :, ti * 128:(ti + 1) * 128], t_hT[ti], wks[ti % 2])
                fw.barrier()

        def proj_tok(pt, hT, t_hT, ti, wt, t_w, tps):
            for kc in range(8):
                MM(pt, hT[:, kc, ti * 128:(ti + 1) * 128], wt[:, kc, :], [t_hT[ti], t_w], tps, start=(kc == 0), stop=(kc == 7), sig=(kc == 7))

        def emit_yT(es, k, yb, t_yb, tok0, nq, yTs, t_yTs):
            for j in range(nq // 128):
                bk = 7
                for ec in range(4):
                    TR(psb[bk][:, ec * 128:(ec + 1) * 128], yb[:, j, ec * 128:(ec + 1) * 128], ident[:], [t_yb, t_c], [tp[bk]], sig=(ec == 3))
                V(lambda e, j=j, bk=bk: e.tensor_copy(out=yTs[:, :, j * 128:(j + 1) * 128], in_=psb[bk][:, 0:512].rearrange("p (c t) -> p c t", c=4)),
                  [tp[bk]], [t_yTs])
            fw.dma("sp", yTd[k].rearrange("(c p) t -> p c t", p=128)[:, :, tok0:tok0 + nq], yTs[:, :, 0:nq], reads=[t_yTs],
                   writes=[tk(t_yT, (k, tok0))])

        CUR = {}

        def warm(n=16):
            hT_, t_hT_ = CUR["hT"], CUR["t_hT"]
            for i in range(n):
                MM(ps[6][:, 0:512], ident[:], hT_[:, 0, 0:512], [t_c] + t_hT_[0:4], [tp[6]], sig=(i == n - 1))

        def attend(QT_fn, KT_fn, V_fn, t_q, t_kv, scale, nq, kchunks, yb, t_yb, pb, t_pb, rc, t_rc):
            warm()
            nj = nq // 128
            last = len(kchunks) - 1
            items = [(h, ci, kc) for h in range(8) for ci, kc in enumerate(kchunks)]

            SBK = (0, 1, 6)

            def emitS(idx):
                h, ci, kc = items[idx]
                sbk = SBK[idx % 3]
                MM(ps[sbk][:, 0:nq], KT_fn(h, kc), QT_fn(h), [t_q, t_kv], [tp[sbk]])

            def emitRest(idx):
                h, ci, kc = items[idx]
                sbk = SBK[idx % 3]
                P = pb[idx % 3]
                tP = t_pb[idx % 3]
                A(lambda e: e.activation(out=P[:, 0:nq], in_=ps[sbk][:, 0:nq], func=AF.Exp, scale=scale), [tp[sbk]], [tP])
                for j in range(nj):
                    MM(ps[2 + j][:, 0:65], P[:, j * 128:(j + 1) * 128], V_fn(h, kc), [tP, t_kv], [tp[2 + j]],
                       start=(ci == 0), stop=(ci == last), sig=(ci == last))
                if ci == last:
                    for j in range(nj):
                        r = rc[j % 4]
                        tr_ = t_rc[j % 4]
                        V(lambda e, j=j, r=r: e.reciprocal(out=r[:], in_=ps[2 + j][:, 64:65]), [tp[2 + j]], [tr_])
                        V(lambda e, j=j, r=r: e.tensor_scalar(out=yb[:, j, h * 64:(h + 1) * 64], in0=ps[2 + j][:, 0:64], scalar1=r[:, 0:1], scalar2=None, op0=ALU.mult),
                          [tp[2 + j], tr_], [t_yb])

            emitS(0)
            if len(items) > 1:
                emitS(1)
            for idx in range(len(items)):
                if idx + 2 < len(items):
                    emitS(idx + 2)
                emitRest(idx)

        QBLOCKS = [(0, 256, [0, 1])] + [(256 + 512 * j, 512, list(range(NT))) for j in range(4)]

        def attn_work(es):
            yb = sb(es, [128, 4, 512], BF16); t_yb = Tok()
            yTs = sb(es, [128, 4, 512], BF16); t_yTs = Tok()
            pb = [sb(es, [128, 512], BF16) for _ in range(3)]
            t_pb = [Tok() for _ in range(3)]
            rc = [sb(es, [128, 1]) for _ in range(4)]
            t_rc = [Tok() for _ in range(4)]
            return yb, t_yb, yTs, t_yTs, pb, t_pb, rc, t_rc

        def stage_gqa(l, b, hT, t_hT):
            with ExitStack() as es:
                wq = sb(es, [128, 8, 512], BF16); wkv = sb(es, [128, 8, 256], BF16); t_w = Tok()
                load_w(wq[:], w_in[l, :, C_GQ:C_GQ + 512], t_w)
                load_w(wkv[:], w_in[l, :, C_GK:C_GK + 256], t_w)
                gq = sb(es, [128, 64]); gk = sb(es, [128, 64]); t_g = Tok()
                bcast_load(gq[:], gqa_g[l, 0, :], t_g)
                bcast_load(gk[:], gqa_g[l, 1, :], t_g)
                KT = sb(es, [128, 2, NTOK], BF16); Vg = sb(es, [128, NT, 2, 80], BF16); t_kv = Tok()
                G(lambda e: e.memset(Vg[:, :, :, 64:65], 1.0), [], [t_kv])
                qf = sb(es, [128, 512]); sq = sb(es, [128, 512]); ssq = sb(es, [128, 8]); qn = sb(es, [128, 512]); t_qw = Tok()
                tmp = sb(es, [128, 4, 8, 32]); t_tmp = Tok()
                qb = sb(es, [128, 512], BF16); t_qb = Tok()
                kd = sb(es, [128, 2, 2, 64], BF16); t_kd = Tok()
                QTb = sb(es, [128, 4, 512], BF16); t_q = Tok()
                work = attn_work(es)

                def normrope(src_ps, tps, H, gt, ti, dst3, t_dst):
                    n = H * 64
                    A(lambda e: e.activation(out=qf[:, 0:n], in_=src_ps, func=AF.Copy), tps, [t_qw])
                    V(lambda e: e.tensor_tensor(out=sq[:, 0:n], in0=qf[:, 0:n], in1=qf[:, 0:n], op=ALU.mult), [t_qw], [t_qw])
                    V(lambda e: e.tensor_reduce(out=ssq[:, 0:H], in_=sq[:, 0:n].rearrange("p (h d) -> p h d", h=H), axis=AX.X, op=ALU.add), [t_qw], [t_qw])
                    rsqrt_mean(ssq[:, 0:H], 64, [t_qw])
                    q3 = qf[:, 0:n].rearrange("p (h d) -> p h d", h=H)
                    n3 = qn[:, 0:n].rearrange("p (h d) -> p h d", h=H)
                    V(lambda e: e.tensor_tensor(out=n3, in0=q3, in1=ssq[:, 0:H].unsqueeze(2).broadcast_to([128, H, 64]), op=ALU.mult), [t_qw], [t_qw])
                    if ti >= 2 and 'noRope' not in ST:
                        V(lambda e: e.tensor_tensor(out=n3, in0=n3, in1=gt[:].unsqueeze(1).broadcast_to([128, H, 64]), op=ALU.mult), [t_qw, t_g], [t_qw])
                        rope(dst3, n3, rg[:, 0, ti - 2, :], rg[:, 1, ti - 2, :], H, 32, tmp[:, :, 0:H, :], [t_qw, t_c], t_tmp, t_dst)
                    else:
                        V(lambda e: e.tensor_tensor(out=dst3, in0=n3, in1=gt[:].unsqueeze(1).broadcast_to([128, H, 64]), op=ALU.mult), [t_qw, t_g], [t_dst])

                for ti in range(NT):
                    bk = 6
                    proj_tok(ps[bk][:, 0:256], hT, t_hT, ti, wkv, t_w, [tp[bk]])
                    V(lambda e, ti=ti, bk=bk: e.tensor_copy(out=Vg[:, ti, :, 0:64], in_=ps[bk][:, 128:256].rearrange("p (g d) -> p g d", g=2)), [tp[bk]], [t_kv])
                    if 'gqaK0' in ST:
                        continue
                    normrope(ps[bk][:, 0:128], [tp[bk]], 2, gk, ti, kd[:, :, 0, :], t_kd)
                    if 'gqaK1' in ST:
                        continue
                    G(lambda e: e.tensor_copy(out=kd[:, :, 1, :], in_=kd[:, :, 0, :]), [t_kd], [t_kd])
                    for g in range(2):
                        TR(psb[7][:, g * 128:(g + 1) * 128], kd[:, g, :, :].rearrange("p c d -> p (c d)"), ident[:], [t_kd, t_c], [tp[7]], sig=(g == 1))
                    A(lambda e, ti=ti, bk=bk: e.activation(out=KT[:, :, ti * 128:(ti + 1) * 128], in_=psb[7][:, 0:256].rearrange("p (g t) -> p g t", g=2), func=AF.Copy),
                      [tp[7]], [t_kv])
                for (tok0, nq, kch) in QBLOCKS:
                    if 'gqaK' in ST:
                        continue
                    for jt in range(nq // 128):
                        ti = tok0 // 128 + jt
                        bk = 6
                        proj_tok(ps[bk][:, 0:512], hT, t_hT, ti, wq, t_w, [tp[bk]])
                        normrope(ps[bk][:, 0:512], [tp[bk]], 8, gq, ti, qb[:].rearrange("p (h d) -> p h d", h=8), t_qb)
                        for pr in range(4):
                            TR(psb[7][:, pr * 128:(pr + 1) * 128], qb[:, pr * 128:(pr + 1) * 128], ident[:], [t_qb, t_c], [tp[7]], sig=(pr == 3))
                        A(lambda e, jt=jt, bk=bk: e.activation(out=QTb[:, :, jt * 128:(jt + 1) * 128], in_=psb[7][:, 0:512].rearrange("p (c t) -> p c t", c=4), func=AF.Copy),
                          [tp[7]], [t_q])
                    yb, t_yb, yTs, t_yTs, pb, t_pb, rc, t_rc = work
                    if 'noattn' in ST:
                        continue
                    attend(lambda h: QTb[(h % 2) * 64:(h % 2) * 64 + 64, h // 2, 0:nq],
                           lambda h, kc: KT[(h % 2) * 64:(h % 2) * 64 + 64, h // 4, kc * 128:(kc + 1) * 128],
                           lambda h, kc: Vg[:, kc, h // 4, 0:65],
                           t_q, t_kv, 0.125, nq, kch, yb, t_yb, pb, t_pb, rc, t_rc)
                    emit_yT(es, 2, yb, t_yb, tok0, nq, yTs, t_yTs)
                fw.barrier()

        def stage_mla(l, b, hT, t_hT):
            with ExitStack() as es:
                wkv = sb(es, [128, 8, 288], BF16); wq = sb(es, [128, 8, 384], BF16); t_w = Tok()
                load_w(wkv[:], w_in[l, :, C_CKV:C_CKV + 288], t_w)
                load_w(wq[:], w_in[l, :, C_CQ:C_CQ + 384], t_w)
                wuq = sb(es, [128, 3, 768], BF16); wukv = sb(es, [128, 2, 1024], BF16)
                load_w(wuq[:], mla_w_uq[l], t_w)
                load_w(wukv[:], mla_w_ukv[l], t_w)
                gq = sb(es, [128, 384]); gkv = sb(es, [128, 256]); t_g = Tok()
                bcast_load(gq[:], mla_g_q[l, :], t_g)
                bcast_load(gkv[:], mla_g_kv[l, :], t_g)
                KT = sb(es, [96, 8, NTOK], BF16); Vm = sb(es, [128, NT, 8, 80], BF16); t_kv = Tok()
                G(lambda e: e.memset(Vm[:, :, :, 64:65], 1.0), [], [t_kv])
                cf = sb(es, [128, 416]); junk = sb(es, [128, 384], BF16); ss = sb(es, [128, 1]); cn = sb(es, [128, 384], BF16); t_cw = Tok()
                cT = sb(es, [128, 3, 128], BF16); t_cT = Tok()
                kpe = sb(es, [128, 1, 32]); tmp = sb(es, [128, 4, 8, 16]); t_tmp = Tok(); t_kpe = Tok()
                kfull = sb(es, [128, 8, 96], BF16); t_kf = Tok()
                qfull = sb(es, [128, 8, 96], BF16); t_qf = Tok()
                qpe = sb(es, [128, 8, 32]); t_qpe = Tok()
                QTb = sb(es, [96, 8, 512], BF16); t_q = Tok()
                work = attn_work(es)

                def lowrank(src_ps, tps, n, gt):
                    A(lambda e: e.activation(out=cf[:, 0:n], in_=src_ps, func=AF.Copy), tps, [t_cw])
                    A(lambda e: e.activation(out=junk[:, 0:n], in_=cf[:, 0:n], func=AF.Square, accum_out=ss[:]), [t_cw], [t_cw])
                    rsqrt_mean(ss[:], n, [t_cw])
                    V(lambda e: e.scalar_tensor_tensor(out=cn[:, 0:n], in0=cf[:, 0:n], scalar=ss[:, 0:1], in1=gt[:, 0:n], op0=ALU.mult, op1=ALU.mult), [t_cw, t_g], [t_cw])
                    for c in range(n // 128):
                        TR(psb[7][:, c * 128:(c + 1) * 128], cn[:, c * 128:(c + 1) * 128], ident[:], [t_cw, t_c], [tp[7]], sig=(c == n // 128 - 1))
                    V(lambda e: e.tensor_copy(out=cT[:, 0:n // 128, :], in_=psb[7][:, 0:n].rearrange("p (c t) -> p c t", t=128)), [tp[7]], [t_cT])

                for ti in range(NT):
                    proj_tok(ps[6][:, 0:288], hT, t_hT, ti, wkv, t_w, [tp[6]])
                    if ti >= 2:
                        A(lambda e: e.activation(out=cf[:, 384:416], in_=ps[6][:, 256:288], func=AF.Copy), [tp[6]], [t_kpe])
                        rope(kpe[:], cf[:, 384:416].rearrange("p (h d) -> p h d", h=1), rm[:, 0, ti - 2, :], rm[:, 1, ti - 2, :], 1, 16, tmp[:, :, 0:1, :], [t_kpe, t_c], t_tmp, t_kpe)
                    else:
                        A(lambda e: e.activation(out=kpe[:, 0, :], in_=ps[6][:, 256:288], func=AF.Copy), [tp[6]], [t_kpe])
                    lowrank(ps[6][:, 0:256], [tp[6]], 256, gkv)
                    for half in range(2):
                        bk = 4 + half
                        for c in range(2):
                            MM(ps[bk][:, 0:512], cT[:, c, :], wukv[:, c, half * 512:(half + 1) * 512], [t_cT, t_w], [tp[bk]], start=(c == 0), stop=(c == 1), sig=(c == 1))
                        kv3 = ps[bk][:, 0:512].rearrange("p (h d) -> p h d", h=4)
                        V(lambda e, kv3=kv3, half=half: e.tensor_copy(out=kfull[:, half * 4:(half + 1) * 4, 0:64], in_=kv3[:, :, 0:64]), [tp[bk]], [t_kf])
                        A(lambda e, kv3=kv3, half=half, ti=ti: e.activation(out=Vm[:, ti, half * 4:(half + 1) * 4, 0:64], in_=kv3[:, :, 64:128], func=AF.Copy), [tp[bk]], [t_kv])
                    G(lambda e: e.tensor_copy(out=kfull[:, :, 64:96], in_=kpe[:].broadcast_to([128, 8, 32])), [t_kpe], [t_kf])
                    for h in range(8):
                        TR(psb[7][0:96, h * 128:(h + 1) * 128], kfull[:, h, :], ident[:], [t_kf, t_c], [tp[7]], sig=(h == 7))
                    A(lambda e, ti=ti: e.activation(out=KT[:, :, ti * 128:(ti + 1) * 128], in_=psb[7][0:96, :].rearrange("p (h t) -> p h t", h=8), func=AF.Copy), [tp[7]], [t_kv])
                for (tok0, nq, kch) in QBLOCKS:
                    for jt in range(nq // 128):
                        ti = tok0 // 128 + jt
                        proj_tok(ps[6][:, 0:384], hT, t_hT, ti, wq, t_w, [tp[6]])
                        lowrank(ps[6][:, 0:384], [tp[6]], 384, gq)
                        for c in range(3):
                            MM(ps[4][:, 0:512], cT[:, c, :], wuq[:, c, 0:512], [t_cT, t_w], [tp[4]], start=(c == 0), stop=(c == 2), sig=(c == 2))
                        for c in range(3):
                            MM(ps[5][:, 0:256], cT[:, c, :], wuq[:, c, 512:768], [t_cT, t_w], [tp[5]], start=(c == 0), stop=(c == 2), sig=(c == 2))
                        for h in range(8):
                            c0 = h * 96
                            for (a, bnd, dst_off) in ((c0, c0 + 64, 0),):
                                pass
                        def colcopy(dst_fn, c_lo, c_hi, eng):
                            segs = []
                            if c_lo < 512:
                                segs.append((4, c_lo, min(c_hi, 512), c_lo))
                            if c_hi > 512:
                                segs.append((5, max(c_lo, 512) - 512, c_hi - 512, max(c_lo, 512)))
                            for (bk, lo, hi, g0) in segs:
                                eng(lambda e, bk=bk, lo=lo, hi=hi, g0=g0: e.tensor_copy(out=dst_fn(g0 - c_lo, g0 - c_lo + hi - lo), in_=ps[bk][:, lo:hi]), [tp[bk]], None)
                        for h in range(8):
                            c0 = h * 96
                            segs = []
                            for (lo, hi, kind) in ((c0, c0 + 64, "n"), (c0 + 64, c0 + 96, "p")):
                                parts = []
                                if lo < 512:
                                    parts.append((4, lo, min(hi, 512), 0))
                                if hi > 512:
                                    parts.append((5, max(lo, 512) - 512, hi - 512, max(lo, 512) - lo))
                                for (bk, a, bb, off) in parts:
                                    if kind == "n":
                                        V(lambda e, bk=bk, a=a, bb=bb, off=off, h=h: e.tensor_copy(out=qfull[:, h, off:off + bb - a], in_=ps[bk][:, a:bb]), [tp[bk]], [t_qf])
                                    else:
                                        dst = qpe if ti >= 2 else None
                                        if ti >= 2:
                                            V(lambda e, bk=bk, a=a, bb=bb, off=off, h=h: e.tensor_copy(out=qpe[:, h, off:off + bb - a], in_=ps[bk][:, a:bb]), [tp[bk]], [t_qpe])
                                        else:
                                            V(lambda e, bk=bk, a=a, bb=bb, off=off, h=h: e.tensor_copy(out=qfull[:, h, 64 + off:64 + off + bb - a], in_=ps[bk][:, a:bb]), [tp[bk]], [t_qf])
                        if ti >= 2:
                            rope(qfull[:, :, 64:96], qpe[:], rm[:, 0, ti - 2, :], rm[:, 1, ti - 2, :], 8, 16, tmp[:], [t_qpe, t_c], t_tmp, t_qf)
                        for h in range(8):
                            TR(psb[7][0:96, h * 128:(h + 1) * 128], qfull[:, h, :], ident[:], [t_qf, t_c], [tp[7]], sig=(h == 7))
                        A(lambda e, jt=jt: e.activation(out=QTb[:, :, jt * 128:(jt + 1) * 128], in_=psb[7][0:96, :].rearrange("p (h t) -> p h t", h=8), func=AF.Copy), [tp[7]], [t_q])
                    yb, t_yb, yTs, t_yTs, pb, t_pb, rc, t_rc = work
                    attend(lambda h: QTb[:, h, 0:nq],
                           lambda h, kc: KT[:, h, kc * 128:(kc + 1) * 128],
                           lambda h, kc: Vm[:, kc, h, 0:65],
                           t_q, t_kv, 96.0 ** -0.5, nq, kch, yb, t_yb, pb, t_pb, rc, t_rc)
                    emit_yT(es, 1, yb, t_yb, tok0, nq, yTs, t_yTs)
                fw.barrier()

        def stage_na(l, b, hT, t_hT):
            with ExitStack() as es:
                wk = sb(es, [128, 8, 512], BF16); wv = sb(es, [128, 8, 512], BF16); wq = sb(es, [128, 8, 512], BF16); t_w = Tok()
                load_w(wk[:], w_in[l, :, C_NAK:C_NAK + 512], t_w)
                load_w(wv[:], w_in[l, :, C_NAV:C_NAV + 512], t_w)
                load_w(wq[:], w_in[l, :, C_NAQ:C_NAQ + 512], t_w)
                KT = sb(es, [128, 4, NTOK], BF16); Vn = sb(es, [128, NT, 8, 80], BF16); t_kv = Tok()
                G(lambda e: e.memset(Vn[:, :, :, 64:65], 1.0), [], [t_kv])
                QT = sb(es, [128, 4, NTOK], BF16); t_q = Tok()
                blocks = [(i * 512, min(512, NTOK - i * 512)) for i in range(5)]
                cnt = 0
                for (wt, dstT, tdst) in ((wk, KT, t_kv), (wq, QT, t_q)):
                    for pr in range(4):
                        for (t0, n) in blocks:
                            bk = 5 + cnt % 2
                            cnt += 1
                            for kc in range(8):
                                MM(ps[bk][:, 0:n], wt[:, kc, pr * 128:(pr + 1) * 128], hT[:, kc, t0:t0 + n], [t_w] + t_hT[t0 // 128:(t0 + n) // 128], [tp[bk]],
                                   start=(kc == 0), stop=(kc == 7), sig=(kc == 7))
                            A(lambda e, bk=bk, dstT=dstT, pr=pr, t0=t0, n=n: e.activation(out=dstT[:, pr, t0:t0 + n], in_=ps[bk][:, 0:n], func=AF.Copy), [tp[bk]], [tdst])
                for ti in range(NT):
                    bk = 5 + ti % 2
                    proj_tok(ps[bk][:, 0:512], hT, t_hT, ti, wv, t_w, [tp[bk]])
                    V(lambda e, ti=ti, bk=bk: e.tensor_copy(out=Vn[:, ti, :, 0:64], in_=ps[bk][:, 0:512].rearrange("p (h d) -> p h d", h=8)), [tp[bk]], [t_kv])
                yb, t_yb, yTs, t_yTs, pb, t_pb, rc, t_rc = attn_work(es)
                attend(lambda h: QT[(h % 2) * 64:(h % 2) * 64 + 64, h // 2, 0:256],
                       lambda h, kc: KT[(h % 2) * 64:(h % 2) * 64 + 64, h // 2, kc * 128:(kc + 1) * 128],
                       lambda h, kc: Vn[:, kc, h, 0:65],
                       t_q, t_kv, 0.125, 256, [0, 1], yb, t_yb, pb, t_pb, rc, t_rc)
                emit_yT(es, 0, yb, t_yb, 0, 256, yTs, t_yTs)
                bt = [sb(es, [128, 5, 128]) for _ in range(3)]; t_bt = [Tok() for _ in range(3)]
                bres = sb(es, [128, 8, 5, 128]); t_bres = Tok()
                fw.dma("sp", bres[:], nab[l, 0], writes=[t_bres])
                Sb = [sb(es, [128, 5, 128]) for _ in range(2)]; t_Sb = [Tok() for _ in range(2)]
                Pn = [sb(es, [128, 7, 128], BF16) for _ in range(2)]; t_Pn = [Tok() for _ in range(2)]
                iters = [(t, h) for t in range(16) for h in range(8)]

                def na_S(it):
                    t, h = iters[it]
                    cls, chunks = na_chunks(t)
                    nloc = len(chunks)
                    q0 = 256 + t * 128
                    par = (h % 2) * 64
                    if cls != 0:
                        fw.dma("sp", bt[it % 3][:], nab[l, cls, :, h, :, :], writes=[t_bt[it % 3]])
                    pa = 0 + 2 * (it % 2); pbk = 1 + 2 * (it % 2)
                    qv = QT[par:par + 64, h // 2, q0:q0 + 128]
                    for s_, c in enumerate(chunks[:4]):
                        kt0 = 256 + c * 128
                        MM(ps[pa][:, s_ * 128:(s_ + 1) * 128], KT[par:par + 64, h // 2, kt0:kt0 + 128], qv, [t_q, t_kv], [tp[pa]], sig=(s_ == 3))
                    for s_ in range(2):
                        MM(ps[pbk][:, s_ * 128:(s_ + 1) * 128], KT[par:par + 64, h // 2, s_ * 128:(s_ + 1) * 128], qv, [t_q, t_kv], [tp[pbk]], sig=(s_ == 1 and nloc == 4))
                    if nloc == 5:
                        kt0 = 256 + chunks[4] * 128
                        MM(ps[pbk][:, 256:384], KT[par:par + 64, h // 2, kt0:kt0 + 128], qv, [t_q, t_kv], [tp[pbk]])

                def na_rest(it):
                    t, h = iters[it]
                    cls, chunks = na_chunks(t)
                    nloc = len(chunks)
                    if cls == 0:
                        B_ = bres[:, h]; tB = t_bres
                    else:
                        B_ = bt[it % 3][:]; tB = t_bt[it % 3]
                    pa = 0 + 2 * (it % 2); pbk = 1 + 2 * (it % 2)
                    S_ = Sb[it % 2]; tS = t_Sb[it % 2]
                    P_ = Pn[it % 2]; tP = t_Pn[it % 2]
                    V(lambda e: e.scalar_tensor_tensor(out=S_[:, 0:4, :], in0=ps[pa][:, 0:512].rearrange("p (s q) -> p s q", s=4), scalar=0.125,
                                                       in1=B_[:, 0:4, :], op0=ALU.mult, op1=ALU.add), [tp[pa], tB], [tS])
                    if nloc == 5:
                        V(lambda e: e.scalar_tensor_tensor(out=S_[:, 4, :], in0=ps[pbk][:, 256:384], scalar=0.125,
                                                           in1=B_[:, 4, :], op0=ALU.mult, op1=ALU.add), [tp[pbk], tB], [tS])
                    A(lambda e: e.activation(out=P_[:, 5:7, :], in_=ps[pbk][:, 0:256].rearrange("p (s q) -> p s q", s=2), func=AF.Exp, scale=0.125), [tp[pbk]], [tP])
                    A(lambda e: e.activation(out=P_[:, 0:nloc, :], in_=S_[:, 0:nloc, :], func=AF.Exp), [tS], [tP])
                    ob = 4 + it % 2
                    seq = [(5, 0), (6, 1)] + [(s_, 2 + c) for s_, c in enumerate(chunks)]
                    for i, (s_, kti) in enumerate(seq):
                        MM(ps[ob][:, 0:65], P_[:, s_, :], Vn[:, kti, h, 0:65], [tP, t_kv], [tp[ob]], start=(i == 0), stop=(i == len(seq) - 1), sig=(i == len(seq) - 1))
                    r = rc[it % 4]; tr_ = t_rc[it % 4]
                    V(lambda e: e.reciprocal(out=r[:], in_=ps[ob][:, 64:65]), [tp[ob]], [tr_])
                    V(lambda e: e.tensor_scalar(out=yb[:, t % 4, h * 64:(h + 1) * 64], in0=ps[ob][:, 0:64], scalar1=r[:, 0:1], scalar2=None, op0=ALU.mult),
                      [tp[ob], tr_], [t_yb])
                    if h == 7 and t % 4 == 3:
                        emit_yT(es, 0, yb, t_yb, 256 + (t - 3) * 128, 512, yTs, t_yTs)

                warm()
                na_S(0)
                for it in range(len(iters)):
                    if it + 1 < len(iters):
                        na_S(it + 1)
                    na_rest(it)
                fw.barrier()

        def stage_ssm(l, b, hT, t_hT):
            with ExitStack() as es:
                t_w = Tok()
                wdt = sb(es, [128, 8, 16], BF16); load_w(wdt[:], w_in[l, :, C_SDT:C_SDT + 16], t_w)
                wz = sb(es, [128, 8, 512], BF16); load_w(wz[:], w_in[l, :, C_SZ:C_SZ + 512], t_w)
                cw = sb(es, [128, 8, 5]); cb = sb(es, [128, 8]); t_cw = Tok()
                fw.dma("sp", cw[:], conv_wT[l].rearrange("(c p) j -> p c j", p=128), writes=[t_cw])
                fw.dma("sp", cb[:], conv_b[l].rearrange("(c p) -> p c", p=128), writes=[t_cw], allow_slow_non_contiguous=True)
                sm = sb(es, [128, 40]); t_sm = Tok()
                bcast_load(sm[:], ssm_small[l, :], t_sm)
                gn = sb(es, [128, 512]); bcast_load(gn[:], ssm_g_norm[l, :], t_sm)
                aneg = sb(es, [128, 16])
                A(lambda e: e.activation(out=aneg[:], in_=sm[:, 0:16], func=AF.Exp), [t_sm], [t_sm])
                V(lambda e: e.tensor_scalar(out=aneg[:], in0=aneg[:], scalar1=-1.0, scalar2=None, op0=ALU.mult), [t_sm], [t_sm])
                xs = sb(es, [128, NT, 512], BF16); t_xs = Tok()
                Bt = sb(es, [128, NT, 256], BF16); t_Bt = Tok()
                BT = sb(es, [128, 2, NTOK], BF16); CT = sb(es, [128, 2, NTOK], BF16); t_BC = Tok()
                dtt = sb(es, [128, NT, 16]); dta = sb(es, [128, NT, 16]); t_dt = Tok()
                PADW = 2 + 256 + 2 + 2 + 2048 + 2
                ec = es.enter_context(ExitStack())
                pre1 = sb(ec, [128, PADW]); pre = [pre1, pre1]; t_p1 = Tok(); t_pre = [t_p1, t_p1]
                acc1 = sb(ec, [128, NTOK]); acc = [acc1, acc1]; t_a1 = Tok(); t_acc = [t_a1, t_a1]
                post1 = sb(ec, [128, NTOK], BF16); post = [post1, post1]; t_po1 = Tok(); t_post = [t_po1, t_po1]
                wc = [sb(ec, [128, 8, 128], BF16) for _ in range(2)]; t_wc = [Tok(), Tok()]
                G(lambda e: e.memset(pre1[:], 0.0), [], [t_p1])
                segs = [(2, 0, 256), (262, 256, 2048)]
                blocks = [(i * 512, min(512, NTOK - i * 512)) for i in range(5)]
                for ch in range(8):
                    col0 = (C_SX + ch * 128) if ch < 6 else (C_SC + (ch - 6) * 128)
                    i2 = ch % 2
                    load_w(wc[i2][:], w_in[l, :, col0:col0 + 128], t_wc[i2])
                    for bi, (t0, n) in enumerate(blocks):
                        bk = 5 + bi % 2
                        for kc in range(8):
                            MM(ps[bk][:, 0:n], wc[i2][:, kc, :], hT[:, kc, t0:t0 + n], [t_wc[i2]] + t_hT[t0 // 128:(t0 + n) // 128], [tp[bk]],
                               start=(kc == 0), stop=(kc == 7), sig=(kc == 7))
                        if t0 == 0:
                            A(lambda e, bk=bk, i2=i2: e.activation(out=pre[i2][:, 2:258], in_=ps[bk][:, 0:256], func=AF.Copy), [tp[bk]], [t_pre[i2]])
                            A(lambda e, bk=bk, i2=i2: e.activation(out=pre[i2][:, 262:518], in_=ps[bk][:, 256:512], func=AF.Copy), [tp[bk]], [t_pre[i2]])
                        else:
                            o = 262 + (t0 - 256)
                            A(lambda e, bk=bk, i2=i2, o=o, n=n: e.activation(out=pre[i2][:, o:o + n], in_=ps[bk][:, 0:n], func=AF.Copy), [tp[bk]], [t_pre[i2]])
                    for (po, ao, L) in segs:
                        V(lambda e, i2=i2, po=po, ao=ao, L=L, ch=ch: e.tensor_scalar(out=acc[i2][:, ao:ao + L], in0=pre[i2][:, po - 2:po - 2 + L], scalar1=cw[:, ch, 0:1], scalar2=None, op0=ALU.mult),
                          [t_pre[i2], t_cw], [t_acc[i2]])
                        for j in range(1, 5):
                            V(lambda e, i2=i2, po=po, ao=ao, L=L, ch=ch, j=j: e.scalar_tensor_tensor(out=acc[i2][:, ao:ao + L], in0=pre[i2][:, po - 2 + j:po - 2 + j + L], scalar=cw[:, ch, j:j + 1],
                                                                                                  in1=acc[i2][:, ao:ao + L], op0=ALU.mult, op1=ALU.add), [t_pre[i2], t_cw, t_acc[i2]], [t_acc[i2]])
                    if ch < 4:
                        dst = post[i2][:]
                    elif ch < 6:
                        dst = BT[:, ch - 4, :]
                    else:
                        dst = CT[:, ch - 6, :]
                    tdst = t_post[i2] if ch < 4 else t_BC
                    A(lambda e, i2=i2, dst=dst, ch=ch: e.activation(out=dst, in_=acc[i2][:], func=AF.Silu, bias=cb[:, ch:ch + 1]), [t_acc[i2], t_cw], [tdst])
                    if ch < 6:
                        srcT = post[i2] if ch < 4 else None
                        for ti in range(NT):
                            bk = 7
                            if ch < 4:
                                TR(psb[bk][:, 0:128], post[i2][:, ti * 128:(ti + 1) * 128], ident[:], [t_post[i2], t_c], [tp[bk]])
                                V(lambda e, bk=bk, ti=ti, ch=ch: e.tensor_copy(out=xs[:, ti, ch * 128:(ch + 1) * 128], in_=psb[bk][:, 0:128]), [tp[bk]], [t_xs])
                            else:
                                TR(psb[bk][:, 0:128], BT[:, ch - 4, ti * 128:(ti + 1) * 128], ident[:], [t_BC, t_c], [tp[bk]])
                                V(lambda e, bk=bk, ti=ti, ch=ch: e.tensor_copy(out=Bt[:, ti, (ch - 4) * 128:(ch - 3) * 128], in_=psb[bk][:, 0:128]), [tp[bk]], [t_Bt])
                for ti in range(NT):
                    bk = 5 + ti % 2
                    proj_tok(ps[bk][:, 0:16], hT, t_hT, ti, wdt, t_w, [tp[bk]])
                    V(lambda e, bk=bk, ti=ti: e.tensor_tensor(out=dtt[:, ti, :], in0=ps[bk][:, 0:16], in1=sm[:, 16:32], op=ALU.add), [tp[bk], t_sm], [t_dt])
                A(lambda e: e.activation(out=dtt[:], in_=dtt[:], func=AF.Exp), [t_dt], [t_dt])
                A(lambda e: e.activation(out=dtt[:], in_=dtt[:], func=AF.Ln, bias=1.0), [t_dt], [t_dt])
                V(lambda e: e.tensor_tensor(out=dta[:], in0=dtt[:], in1=aneg[:].unsqueeze(1).broadcast_to([128, NT, 16]), op=ALU.mult), [t_dt, t_sm], [t_dt])
                fw.barrier()
                ec.close()
                ysum = sb(es, [128, NT, 512]); t_ys = [Tok() for _ in range(NT)]
                S = sb(es, [128, 8, 64]); Hb = sb(es, [128, 8, 64], BF16); t_S = Tok(); t_Hb = Tok()
                cbT = sb(es, [128, 2, 128]); t_cbT = Tok()
                sc2 = [sb(es, [128, 5, 8]) for _ in range(2)]; t_sc2 = [Tok(), Tok()]
                r1 = sb(es, [128, 8, 128]); r2 = sb(es, [128, 8, 128]); t_r = Tok()
                LT = sb(es, [128, 8, 128]); t_LT = Tok()
                MT2 = [sb(es, [128, 8, 128], BF16) for _ in range(2)]; t_MT2 = [Tok(), Tok()]
                xdt2 = [sb(es, [128, 8, 64], BF16) for _ in range(2)]; xw2 = [sb(es, [128, 8, 64], BF16) for _ in range(2)]; t_xd2 = [Tok(), Tok()]
                ytmp = sb(es, [128, 8, 64]); t_yt = Tok()
                steps = []
                for d_ in range(2):
                    order = list(range(NT)) if d_ == 0 else [1, 0] + list(range(NT - 1, 1, -1))
                    for j_, ti_ in enumerate(order):
                        steps.append((d_, ti_, j_ == 0))

                def phaseA(k):
                    d, ti, first = steps[k]
                    sc = sc2[k % 2]; t_sc = t_sc2[k % 2]
                    MT = MT2[k % 2]; t_MT = t_MT2[k % 2]
                    xdt = xdt2[k % 2]; xw = xw2[k % 2]; t_xd = t_xd2[k % 2]
                    dcol = dta[:, ti, d * 8:(d + 1) * 8]
                    for g in range(2):
                        MM(ps[5][:, g * 128:(g + 1) * 128], BT[:, g, ti * 128:(ti + 1) * 128], CT[:, g, ti * 128:(ti + 1) * 128], [t_BC], [tp[5]], sig=(g == 1))
                    A(lambda e: e.activation(out=cbT[:], in_=ps[5][:, 0:256].rearrange("p (g q) -> p g q", g=2), func=AF.Copy), [tp[5]], [t_cbT])
                    MM(ps[6][:, 0:8], tri[d], dcol, [t_c, t_dt], [tp[6]], sig=False)
                    MM(ps[6][:, 8:16], onesf, dcol, [t_c, t_dt], [tp[6]])
                    V(lambda e: e.tensor_copy(out=sc[:, 0:2, :], in_=ps[6][:, 0:16].rearrange("p (a h) -> p a h", a=2)), [tp[6]], [t_sc])
                    V(lambda e: e.tensor_tensor(out=sc[:, 2, :], in0=sc[:, 1, :], in1=sc[:, 0, :], op=ALU.subtract), [t_sc], [t_sc])
                    A(lambda e: e.activation(out=sc[:, 2, :], in_=sc[:, 2, :], func=AF.Exp), [t_sc], [t_sc])
                    A(lambda e: e.activation(out=sc[:, 3, :], in_=sc[:, 0, :], func=AF.Exp), [t_sc], [t_sc])
                    A(lambda e: e.activation(out=sc[:, 4, :], in_=sc[:, 1, :], func=AF.Exp), [t_sc], [t_sc])
                    V(lambda e: e.tensor_tensor(out=r1[:], in0=tri[d].unsqueeze(1).broadcast_to([128, 8, 128]), in1=dcol.unsqueeze(2).broadcast_to([128, 8, 128]), op=ALU.mult),
                      [t_c, t_dt], [t_r])
                    G(lambda e: e.tensor_copy(out=r2[:], in_=dcol.unsqueeze(2).broadcast_to([128, 8, 128])), [t_dt], [t_r])
                    for hb in range(2):
                        o = ps[hb][:, 0:512]
                        MM(o, onesf, r1[:, hb * 4:(hb + 1) * 4, :].rearrange("p h q -> p (h q)"), [t_c, t_r], [tp[hb]], start=True, stop=False, sig=False)
                        MM(o, ntri[d], r2[:, hb * 4:(hb + 1) * 4, :].rearrange("p h q -> p (h q)"), [t_c, t_r], [tp[hb]], start=False, stop=False, sig=False)
                        MM(o, identf, mneg4[d], [t_c], [tp[hb]], start=False, stop=True)
                        A(lambda e, hb=hb: e.activation(out=LT[:, hb * 4:(hb + 1) * 4, :], in_=ps[hb][:, 0:512].rearrange("p (h q) -> p h q", h=4), func=AF.Exp), [tp[hb]], [t_LT])
                        V(lambda e, hb=hb: e.tensor_tensor(out=MT[:, hb * 4:(hb + 1) * 4, :], in0=LT[:, hb * 4:(hb + 1) * 4, :],
                                                           in1=cbT[:, hb:hb + 1, :].broadcast_to([128, 4, 128]), op=ALU.mult), [t_LT, t_cbT], [t_MT])
                    x3 = xs[:, ti, :].rearrange("p (h d) -> p h d", h=8)
                    V(lambda e: e.tensor_tensor(out=xdt[:], in0=x3, in1=dtt[:, ti, d * 8:(d + 1) * 8].unsqueeze(2).broadcast_to([128, 8, 64]), op=ALU.mult), [t_xs, t_dt], [t_xd])
                    G(lambda e: e.tensor_tensor(out=xw[:], in0=xdt[:], in1=sc[:, 2, :].unsqueeze(2).broadcast_to([128, 8, 64]), op=ALU.mult), [t_xd, t_sc], [t_xd])

                def phaseB(k):
                    d, ti, first = steps[k]
                    sc = sc2[k % 2]; t_sc = t_sc2[k % 2]
                    MT = MT2[k % 2]; t_MT = t_MT2[k % 2]
                    xdt = xdt2[k % 2]; xw = xw2[k % 2]; t_xd = t_xd2[k % 2]
                    if first:
                        G(lambda e: e.memset(S[:], 0.0), [], [t_S])
                        G(lambda e: e.memset(Hb[:], 0.0), [], [t_Hb])
                    for h in range(8):
                        MM(ps[2][:, h * 64:(h + 1) * 64], MT[:, h, :], xdt[:, h, :], [t_MT, t_xd], [tp[2]], sig=(h == 7))
                    for g in range(2):
                        MM(ps[3][:, g * 256:(g + 1) * 256], CT[:, g, ti * 128:(ti + 1) * 128], Hb[:, g * 4:(g + 1) * 4, :].rearrange("p h d -> p (h d)"), [t_BC, t_Hb], [tp[3]], sig=(g == 1))
                    for g in range(2):
                        MM(ps[4][:, g * 256:(g + 1) * 256], Bt[:, ti, g * 128:(g + 1) * 128], xw[:, g * 4:(g + 1) * 4, :].rearrange("p h d -> p (h d)"), [t_Bt, t_xd], [tp[4]], sig=(g == 1))
                    V(lambda e: e.tensor_tensor(out=ytmp[:], in0=ps[3][:, 0:512].rearrange("p (h d) -> p h d", h=8), in1=sc[:, 3, :].unsqueeze(2).broadcast_to([128, 8, 64]), op=ALU.mult),
                      [tp[3], t_sc], [t_yt])
                    V(lambda e: e.tensor_tensor(out=S[:], in0=S[:], in1=sc[:, 4, :].unsqueeze(2).broadcast_to([128, 8, 64]), op=ALU.mult), [t_S, t_sc], [t_S])
                    V(lambda e: e.tensor_tensor(out=S[:], in0=S[:], in1=ps[4][:, 0:512].rearrange("p (h d) -> p h d", h=8), op=ALU.add), [t_S, tp[4]], [t_S])
                    A(lambda e: e.activation(out=Hb[:], in_=S[:], func=AF.Copy), [t_S], [t_Hb])
                    yv = ysum[:, ti, :]
                    if d == 0:
                        V(lambda e: e.tensor_tensor(out=yv, in0=ps[2][:, 0:512], in1=ytmp[:].rearrange("p h d -> p (h d)"), op=ALU.add), [tp[2], t_yt], [t_ys[ti]])
                    else:
                        V(lambda e: e.tensor_tensor(out=ytmp[:].rearrange("p h d -> p (h d)"), in0=ps[2][:, 0:512], in1=ytmp[:].rearrange("p h d -> p (h d)"), op=ALU.add), [tp[2], t_yt], [t_yt])
                        G(lambda e: e.tensor_tensor(out=yv, in0=yv, in1=ytmp[:].rearrange("p h d -> p (h d)"), op=ALU.add), [t_yt, t_ys[ti]], [t_ys[ti]])

                warm()
                phaseA(0)
                for k in range(len(steps)):
                    if k + 1 < len(steps):
                        phaseA(k + 1)
                    phaseB(k)
                dsk = sm[:, 32:40]
                zf = sb(es, [128, 512]); tf = sb(es, [128, 512]); junk = sb(es, [128, 512], BF16); ss = sb(es, [128, 1]); t_f = Tok()
                yb = sb(es, [128, 4, 512], BF16); t_yb = Tok(); yTs = sb(es, [128, 4, 512], BF16); t_yTs = Tok()
                for ti in range(NT):
                    bk = 5 + ti % 2
                    proj_tok(ps[bk][:, 0:512], hT, t_hT, ti, wz, t_w, [tp[bk]])
                    A(lambda e, bk=bk: e.activation(out=zf[:], in_=ps[bk][:, 0:512], func=AF.Silu), [tp[bk]], [t_f])
                    x3 = xs[:, ti, :].rearrange("p (h d) -> p h d", h=8)
                    V(lambda e, x3=x3: e.tensor_tensor(out=tf[:].rearrange("p (h d) -> p h d", h=8), in0=x3, in1=dsk.unsqueeze(2).broadcast_to([128, 8, 64]), op=ALU.mult), [t_xs, t_sm], [t_f])
                    V(lambda e, ti=ti: e.tensor_tensor(out=tf[:], in0=tf[:], in1=ysum[:, ti, :], op=ALU.add), [t_f, t_ys[ti]], [t_f])
                    V(lambda e: e.tensor_tensor(out=tf[:], in0=tf[:], in1=zf[:], op=ALU.mult), [t_f], [t_f])
                    A(lambda e: e.activation(out=junk[:], in_=tf[:], func=AF.Square, accum_out=ss[:]), [t_f], [t_f])
                    rsqrt_mean(ss[:], 512, [t_f])
                    jslot = (ti % 4) if ti >= 2 else ti
                    jslot = ((ti - 2) % 4) if ti >= 2 else ti
                    V(lambda e, jslot=jslot: e.scalar_tensor_tensor(out=yb[:, jslot, :], in0=tf[:], scalar=ss[:, 0:1], in1=gn[:], op0=ALU.mult, op1=ALU.mult), [t_f, t_sm], [t_yb])
                    if ti == 1:
                        emit_yT(es, 3, yb, t_yb, 0, 256, yTs, t_yTs)
                    elif ti >= 2 and (ti - 2) % 4 == 3:
                        emit_yT(es, 3, yb, t_yb, (ti - 3) * 128, 512, yTs, t_yTs)
                fw.barrier()

        def stage_merge(l, b, hT, t_hT):
            with ExitStack() as es:
                gT = sb(es, [128, 8, NTOK], BF16); t_gT = [Tok() for _ in range(5)]
                blocks = [(i * 512, min(512, NTOK - i * 512)) for i in range(5)]
                with ExitStack() as e1:
                    wg = [sb(e1, [128, 8, 4, 128], BF16) for _ in range(2)]; t_wg = [Tok(), Tok()]
                    wb = [sb(e1, [128, 4, 4, 128], BF16) for _ in range(2)]; t_wb = [Tok(), Tok()]
                    yt = [sb(e1, [128, 4, 4, 512], BF16) for _ in range(2)]; t_yt = [Tok(), Tok()]
                    sg = [sb(e1, [128, 512]) for _ in range(2)]; t_sg = [Tok(), Tok()]
                    accg = sb(e1, [128, 512]); t_ag = Tok()
                    it = 0
                    for dc in range(8):
                        i2 = dc % 2
                        for k in range(4):
                            c0 = C_GATE + k * D + dc * 128
                            fw.dma("pool", wg[i2][:, :, k, :], w_in[l, :, c0:c0 + 128].rearrange("(k p) n -> p k n", p=128), writes=[t_wg[i2]])
                            fw.dma("pool", wb[i2][:, k, :, :], w_branch[l, k, :, dc * 128:(dc + 1) * 128].rearrange("(k p) n -> p k n", p=128), writes=[t_wb[i2]])
                        for bi, (t0, n) in enumerate(blocks):
                            y2 = it % 2
                            it += 1
                            rd = [tk(t_yT, (k, q0)) for k in range(4) for (q0, nq, _) in QBLOCKS if q0 < t0 + n and q0 + nq > t0]
                            for k in range(4):
                                fw.dma("sp", yt[y2][:, k, :, 0:n], yTd[k].rearrange("(c p) t -> p c t", p=128)[:, :, t0:t0 + n], reads=rd, writes=[t_yt[y2]])
                            for k in range(4):
                                pg = k % 2; pp = 2 + k % 2
                                for kc in range(8):
                                    MM(ps[pg][:, 0:n], wg[i2][:, kc, k, :], hT[:, kc, t0:t0 + n], [t_wg[i2]] + t_hT[t0 // 128:(t0 + n) // 128], [tp[pg]],
                                       start=(kc == 0), stop=(kc == 7), sig=(kc == 7))
                                for ec_ in range(4):
                                    MM(ps[pp][:, 0:n], wb[i2][:, k, ec_, :], yt[y2][:, k, ec_, 0:n], [t_wb[i2], t_yt[y2]], [tp[pp]],
                                       start=(ec_ == 0), stop=(ec_ == 3), sig=(ec_ == 3))
                                A(lambda e, pg=pg, k=k, n=n: e.activation(out=sg[k % 2][:, 0:n], in_=ps[pg][:, 0:n], func=AF.Sigmoid), [tp[pg]], [t_sg[k % 2]])
                                if k == 0:
                                    V(lambda e, pp=pp, n=n: e.tensor_tensor(out=accg[:, 0:n], in0=sg[0][:, 0:n], in1=ps[pp][:, 0:n], op=ALU.mult), [t_sg[0], tp[pp]], [t_ag])
                                else:
                                    V(lambda e, pp=pp, k=k, n=n: e.tensor_tensor(out=sg[k % 2][:, 0:n], in0=sg[k % 2][:, 0:n], in1=ps[pp][:, 0:n], op=ALU.mult), [t_sg[k % 2], tp[pp]], [t_sg[k % 2]])
                                    if k < 3:
                                        G(lambda e, k=k, n=n: e.tensor_tensor(out=accg[:, 0:n], in0=accg[:, 0:n], in1=sg[k % 2][:, 0:n], op=ALU.add), [t_ag, t_sg[k % 2]], [t_ag])
                                    else:
                                        G(lambda e, k=k, n=n, dc=dc, t0=t0: e.tensor_tensor(out=gT[:, dc, t0:t0 + n], in0=accg[:, 0:n], in1=sg[k % 2][:, 0:n], op=ALU.add),
                                          [t_ag, t_sg[k % 2]], [t_gT[bi]])
                    fw.barrier()
                with ExitStack() as e2:
                    wo = sb(e2, [128, 8, D], BF16); t_wo = Tok()
                    load_w(wo[:], w_out[l], t_wo)
                    tl, t_ab = {}, Tok()
                    for rname, row in (("b", b), ("c", 2)):
                        t = sb(e2, [128, D])
                        fw.dma("sp", t[:], combd[l, row, 2, :].partition_broadcast(128), reads=[t_comb[l]], writes=[t_ab])
                        tl[rname] = t
                    xb = [sb(e2, [128, D]) for _ in range(2)]; t_xb = [Tok(), Tok()]
                    junk = sb(e2, [128, D], BF16); ss = [sb(e2, [128, 1]) for _ in range(2)]; t_ss = [Tok(), Tok()]
                    t1 = [sb(e2, [128, D]) for _ in range(2)]; t_t1 = [Tok(), Tok()]
                    for ti in range(NT):
                        if l == 1 and ti < 2:
                            continue
                        i2 = ti % 2
                        src, rt = xsrc(l, b, ti)
                        fw.dma("sp", xb[i2][:], src, reads=rt, writes=[t_xb[i2]])
                        for half in range(2):
                            bk = 4 + half
                            for kc in range(8):
                                MM(ps[bk][:, 0:512], gT[:, kc, ti * 128:(ti + 1) * 128], wo[:, kc, half * 512:(half + 1) * 512], [t_gT[ti // 4], t_wo], [tp[bk]],
                                   start=(kc == 0), stop=(kc == 7), sig=(kc == 7))
                        A(lambda e, i2=i2: e.activation(out=junk[:, 0:512], in_=ps[4][:, 0:512], func=AF.Square, accum_out=ss[i2][:]), [tp[4]], [t_ss[i2]])
                        A(lambda e, i2=i2: e.activation(out=junk[:, 512:1024], in_=ps[5][:, 0:512], func=AF.Square, accum_out=t1[i2][:, 0:1]), [tp[5]], [t_t1[i2]])
                        V(lambda e, i2=i2: e.tensor_tensor(out=ss[i2][:], in0=ss[i2][:], in1=t1[i2][:, 0:1], op=ALU.add), [t_ss[i2], t_t1[i2]], [t_ss[i2]])
                        rsqrt_mean(ss[i2][:], D, [t_ss[i2]])
                        Gt = tl["c" if ti < 2 else "b"]
                        for half in range(2):
                            V(lambda e, i2=i2, half=half, Gt=Gt: e.scalar_tensor_tensor(out=t1[i2][:, half * 512:(half + 1) * 512], in0=ps[4 + half][:, 0:512], scalar=ss[i2][:, 0:1],
                                                                                       in1=Gt[:, half * 512:(half + 1) * 512], op0=ALU.mult, op1=ALU.mult),
                              [tp[4 + half], t_ss[i2], t_ab], [t_t1[i2]])
                        G(lambda e, i2=i2: e.tensor_tensor(out=t1[i2][:], in0=t1[i2][:], in1=xb[i2][:], op=ALU.add), [t_t1[i2], t_xb[i2]], [t_t1[i2]])
                        fw.dma("sp", xmid[b, ti * 128:(ti + 1) * 128, :], t1[i2][:], reads=[t_t1[i2]], writes=[tk(t_xmid, (b, ti))])
                    fw.barrier()

        def stage_ffn(l):
            with ExitStack() as es:
                w1 = sb(es, [128, 8, FFH], BF16); w3 = sb(es, [128, 8, FFH], BF16); w2 = sb(es, [128, 22, D], BF16); t_w = Tok()
                for c in range(0, FFH, 704):
                    load_w(w1[:, :, c:c + 704], ffn_w1[l, :, c:c + 704], t_w)
                    load_w(w3[:, :, c:c + 704], ffn_w3[l, :, c:c + 704], t_w)
                for c in range(0, 22, 11):
                    fw.dma("pool", w2[:, c:c + 11, :], ffn_w2[l, c * 128:(c + 11) * 128, :].rearrange("(k p) n -> p k n", p=128), writes=[t_w])
                xb = [sb(es, [128, D]) for _ in range(2)]; t_xb = [Tok(), Tok()]
                wk1 = (sb(es, [128, D], BF16), sb(es, [128, 1]), sb(es, [128, D]), sb(es, [128, D], BF16), Tok()); wks = [wk1, wk1]
                h2T1 = sb(es, [128, 8, 128], BF16); h2T = [h2T1, h2T1]; t_h2T1 = Tok(); t_h2T = [t_h2T1, t_h2T1]
                sl = [sb(es, [128, 512]) for _ in range(2)]; t_sl = [Tok(), Tok()]
                u = sb(es, [128, FFH], BF16); t_u = Tok()
                uT = sb(es, [128, 22, 128], BF16); t_uT = Tok()
                junk = wk1[0]; ss = sb(es, [128, 2]); t_ss = Tok()
                o11 = sb(es, [128, D]); o1 = [o11, o11]; t_o11 = Tok(); t_o1 = [t_o11, t_o11]
                nchunks = [(c, min(512, FFH - c)) for c in range(0, FFH, 512)]
                for b in range(2):
                    with ExitStack() as eb:
                        tl, t_ab = load_comb(eb, l, b, 3)
                        for ti in range(NT):
                            if l == 1 and ti < 2:
                                continue
                            i2 = ti % 2
                            rn = "c" if ti < 2 else "b"
                            fw.dma("sp", xb[i2][:], xmid[b, ti * 128:(ti + 1) * 128, :], reads=[tk(t_xmid, (b, ti))], writes=[t_xb[i2]])
                            norm_mod_T(eb, xb[i2][:], t_xb[i2], tl[(rn, 3)][:], tl[(rn, 4)][:], t_ab, h2T[i2][:], t_h2T[i2], wks[i2])
                            for ci, (c0, n) in enumerate(nchunks):
                                pa = 0 + 2 * (ci % 2); pb_ = 1 + 2 * (ci % 2)
                                for kc in range(8):
                                    MM(ps[pa][:, 0:n], h2T[i2][:, kc, :], w1[:, kc, c0:c0 + n], [t_h2T[i2], t_w], [tp[pa]], start=(kc == 0), stop=(kc == 7), sig=(kc == 7))
                                for kc in range(8):
                                    MM(ps[pb_][:, 0:n], h2T[i2][:, kc, :], w3[:, kc, c0:c0 + n], [t_h2T[i2], t_w], [tp[pb_]], start=(kc == 0), stop=(kc == 7), sig=(kc == 7))
                                A(lambda e, ci=ci, pa=pa, n=n: e.activation(out=sl[ci % 2][:, 0:n], in_=ps[pa][:, 0:n], func=AF.Silu), [tp[pa]], [t_sl[ci % 2]])
                                V(lambda e, ci=ci, pb_=pb_, n=n, c0=c0: e.tensor_tensor(out=u[:, c0:c0 + n], in0=sl[ci % 2][:, 0:n], in1=ps[pb_][:, 0:n], op=ALU.mult), [t_sl[ci % 2], tp[pb_]], [t_u])
                            for grp in range(3):
                                c_lo = grp * 8; c_hi = min(22, c_lo + 8)
                                for c in range(c_lo, c_hi):
                                    TR(psb[7][:, (c - c_lo) * 128:(c - c_lo + 1) * 128], u[:, c * 128:(c + 1) * 128], ident[:], [t_u, t_c], [tp[7]], sig=(c == c_hi - 1))
                                V(lambda e, c_lo=c_lo, c_hi=c_hi: e.tensor_copy(out=uT[:, c_lo:c_hi, :], in_=psb[7][:, 0:(c_hi - c_lo) * 128].rearrange("p (c t) -> p c t", t=128)), [tp[7]], [t_uT])
                            for half in range(2):
                                bk = 4 + half
                                for c in range(22):
                                    MM(ps[bk][:, 0:512], uT[:, c, :], w2[:, c, half * 512:(half + 1) * 512], [t_uT, t_w], [tp[bk]], start=(c == 0), stop=(c == 21), sig=(c == 21))
                            A(lambda e: e.activation(out=junk[:, 0:512], in_=ps[4][:, 0:512], func=AF.Square, accum_out=ss[:, 0:1]), [tp[4]], [t_ss])
                            A(lambda e: e.activation(out=junk[:, 512:1024], in_=ps[5][:, 0:512], func=AF.Square, accum_out=ss[:, 1:2]), [tp[5]], [t_ss])
                            V(lambda e: e.tensor_tensor(out=ss[:, 0:1], in0=ss[:, 0:1], in1=ss[:, 1:2], op=ALU.add), [t_ss], [t_ss])
                            rsqrt_mean(ss[:, 0:1], D, [t_ss])
                            Gt = tl[(rn, 5)]
                            for half in range(2):
                                V(lambda e, i2=i2, half=half, Gt=Gt: e.scalar_tensor_tensor(out=o1[i2][:, half * 512:(half + 1) * 512], in0=ps[4 + half][:, 0:512], scalar=ss[:, 0:1],
                                                                                           in1=Gt[:, half * 512:(half + 1) * 512], op0=ALU.mult, op1=ALU.mult),
                                  [tp[4 + half], t_ss, t_ab], [t_o1[i2]])
                            G(lambda e, i2=i2: e.tensor_tensor(out=o1[i2][:], in0=o1[i2][:], in1=xb[i2][:], op=ALU.add), [t_o1[i2], t_xb[i2]], [t_o1[i2]])
                            if l == 0:
                                fw.dma("sp", xres[b, ti * 128:(ti + 1) * 128, :], o1[i2][:], reads=[t_o1[i2]], writes=[tk(t_xres, (b, ti))])
                            else:
                                fw.dma("sp", out[b, (ti - 2) * 128:(ti - 1) * 128, :], o1[i2][:], reads=[t_o1[i2]])
                        fw.barrier()
                fw.barrier()

        for l in layers:
            if "mod" in ST:
                stage_mod(l)
            for b in range(2):
                with ExitStack() as eh:
                    hT = sb(eh, [128, 8, NTOK], BF16, "hT")
                    t_hT = [Tok(f"hT{ti}") for ti in range(NT)]
                    CUR["hT"] = hT; CUR["t_hT"] = t_hT
                    if "norm1" in ST:
                        stage_norm1(l, b, hT, t_hT)
                    if "gqa" in ST:
                        stage_gqa(l, b, hT, t_hT)
                    if "mla" in ST:
                        stage_mla(l, b, hT, t_hT)
                    if "na" in ST:
                        stage_na(l, b, hT, t_hT)
                    if "ssm" in ST:
                        stage_ssm(l, b, hT, t_hT)
                    if "merge" in ST:
                        stage_merge(l, b, hT, t_hT)
                    fw.barrier()
            if "ffn" in ST:
                stage_ffn(l)
        fw.barrier()
        fw.replay()
        print("instructions:", fw.n_instr)
    return nc


def _host_consts():
    i = np.arange(128)
    c = np.zeros((128, 14, 128), np.float32)
    c[:, 0, :] = np.eye(128)
    trif = (i[:, None] <= i[None, :]).astype(np.float32)
    trib = (i[:, None] >= i[None, :]).astype(np.float32)
    c[:, 1, :] = trif
    c[:, 2, :] = trib
    for a in range(4):
        c[:, 3 + a, :] = np.where(i[:, None] <= i[None, :], 0.0, NEG)
        c[:, 7 + a, :] = np.where(i[:, None] >= i[None, :], 0.0, NEG)
    c[:, 11, :] = 1.0
    c[:, 12, :] = -trif
    c[:, 13, :] = -trib
    return c


def _rope_table(n, dim):
    t = np.arange(n)
    row = (t // 64).astype(np.float32)
    col = (t % 64).astype(np.float32)
    quarter = dim // 4
    inv = (np.float32(10000.0) ** (-np.arange(quarter, dtype=np.float32) / np.float32(quarter))).astype(np.float32)
    ang = np.concatenate([row[:, None] * inv, col[:, None] * inv], axis=-1).astype(np.float32)
    cs = np.stack([np.cos(ang), np.sin(ang)], axis=0).astype(np.float32)
    return np.ascontiguousarray(cs.reshape(2, 16, 128, dim // 2).transpose(2, 0, 1, 3))


def _na_bias_table(rpb):
    L = rpb.shape[0]
    tab = np.full((L, 5, 128, 8, 5, 128), NEG, np.float32)
    k = np.arange(128)
    q = np.arange(128)
    for cls, t in enumerate((5, 0, 1, 14, 15)):
        _, chunks = na_chunks(t)
        r = 2 * t + q // 64
        qc = q % 64
        rs = np.clip(r - 4, 0, 24)
        cs = np.clip(qc - 8, 0, 48)
        for slot, c in enumerate(chunks):
            kr = 2 * c + k // 64
            kc = k % 64
            ok = ((kr[:, None] >= rs[None, :]) & (kr[:, None] <= rs[None, :] + 7) &
                  (kc[:, None] >= cs[None, :]) & (kc[:, None] <= cs[None, :] + 15))
            ri = np.clip(kr[:, None] - r[None, :] + 7, 0, 14)
            ci = np.clip(kc[:, None] - qc[None, :] + 15, 0, 30)
            g = rpb[:, :, ri, ci]
            g = np.where(ok[None, None], g, np.float32(NEG))
            tab[:, cls, :, :, slot, :] = g.transpose(0, 2, 1, 3)
    return tab


_PROG = {}


def _get_prog(debug=False, layers=(0, 1)):
    key = (debug, tuple(layers))
    if key not in _PROG:
        _PROG[key] = build_program(debug=debug, layers=layers)
    return _PROG[key]


def make_in_maps(inputs):
    f = lambda a: np.ascontiguousarray(np.asarray(a, dtype=np.float32))
    shared = {
        "w_ada": f(inputs["w_ada"]), "b_ada": f(inputs["b_ada"]),
        "g4": f(np.stack([inputs["g_pre1"], inputs["g_post1"], inputs["g_pre2"], inputs["g_post2"]], axis=1)),
        "w_in": f(inputs["w_in"]), "nab": _na_bias_table(f(inputs["na_rpb"])),
        "mla_g_q": f(inputs["mla_g_q"]), "mla_g_kv": f(inputs["mla_g_kv"]),
        "mla_w_uq": f(inputs["mla_w_uq"]), "mla_w_ukv": f(inputs["mla_w_ukv"]),
        "gqa_g": f(np.stack([inputs["gqa_g_q"], inputs["gqa_g_k"]], axis=1)),
        "conv_wT": f(np.transpose(np.asarray(inputs["ssm_conv_w"]), (0, 2, 1))),
        "conv_b": f(inputs["ssm_conv_b"]),
        "ssm_small": f(np.concatenate([np.asarray(inputs["ssm_a_log"]).reshape(2, 16), np.asarray(inputs["ssm_dt_bias"]).reshape(2, 16),
                                       np.asarray(inputs["ssm_d"]).reshape(2, 8)], axis=1)),
        "ssm_g_norm": f(inputs["ssm_g_norm"]),
        "w_branch": f(inputs["w_branch"]), "w_out": f(inputs["w_out"]),
        "ffn_w1": f(inputs["ffn_w1"]), "ffn_w3": f(inputs["ffn_w3"]), "ffn_w2": f(inputs["ffn_w2"]),
        "consts": _host_consts(), "ropeg": _rope_table(2048, 64), "ropem": _rope_table(2048, 32),
    }
    x = f(inputs["x"]); c = f(inputs["c"]); ctx = f(inputs["ctx"]); c_ctx = f(inputs["c_ctx"])
    maps = []
    for i in range(8):
        m = dict(shared)
        m["x"] = x[2 * i:2 * i + 2]
        m["ctx"] = ctx[2 * i:2 * i + 2]
        m["cvec"] = np.ascontiguousarray(np.concatenate([c[2 * i:2 * i + 2], c_ctx[None, :]], axis=0))
        maps.append(m)
    return maps


def kernel(**inputs):
    nc = _get_prog()
    maps = make_in_maps(inputs)
    res = run_bass_kernel_spmd(nc, maps, core_ids=list(range(8)))
    return np.concatenate([np.asarray(r["out"], dtype=np.float32) for r in res.results], axis=0)
```

```python
from contextlib import ExitStack
import numpy as np
from concourse.bass_utils import run_bass_kernel_spmd
import numpy as np
import concourse.bass as bass
import concourse.mybir as mybir
F32 = mybir.dt.float32
BF16 = mybir.dt.bfloat16
AF = mybir.ActivationFunctionType
ALU = mybir.AluOpType
AX = mybir.AxisListType

class Tok:
    __slots__ = ("w", "r", "name", "excl")
    def __init__(self, name="", excl=False):
        self.w = {}
        self.r = {}
        self.name = name
        self.excl = excl

class Q:
    def __init__(self, fw, name, attr, sem):
        self.fw = fw; self.name = name; self.attr = attr; self.sem = sem
        self.key = name
        self.count = 0
        self.seen = {}
        self.ops = []
        self.pending = False
        self.dsems = []
        self.dtarget = {}
        self.dnext = 0

class FW:
    def __init__(self, nc, stack, n_dma_sems=6):
        self.nc = nc
        self.q = {}
        self.semh = {}
        for name, attr in (("pe", "tensor"), ("act", "scalar"), ("dve", "vector"), ("pool", "gpsimd"), ("sp", "sync")):
            s = stack.enter_context(nc.semaphore("s_" + name))
            self.q[name] = Q(self, name, attr, s)
            self.semh[name] = s
        for qn in ("sp", "pool"):
            q = self.q[qn]
            for i in range(n_dma_sems):
                key = f"d_{qn}{i}"
                s = stack.enter_context(nc.semaphore(key))
                self.semh[key] = s
                q.dsems.append(key)
                q.dtarget[key] = 0
        self.n_instr = 0

    def _wait(self, q, key, val):
        if q.seen.get(key, 0) < val:
            q.ops.append(("w", key, val))
            q.seen[key] = val

    def _deps(self, q, reads, writes):
        for t in reads:
            for k, v in t.w.items():
                self._dep1(q, k, v)
            if t.excl:
                for k, v in t.r.items():
                    if k != q.key:
                        self._dep1(q, k, v)
        for t in writes:
            for k, v in t.w.items():
                self._dep1(q, k, v)
            for k, v in t.r.items():
                self._dep1(q, k, v)

    def _dep1(self, q, k, v):
        if k == q.key and v > q.count:
            return
        self._wait(q, k, v)

    def op(self, qn, fn, reads=(), writes=(), signal=True):
        q = self.q[qn]
        self._deps(q, reads, writes)
        if signal:
            q.count += 1
            q.ops.append(("i", fn, True))
            q.pending = False
        else:
            q.ops.append(("i", fn, False))
            q.pending = True
        ev = (q.key, q.count if signal else q.count + 1)
        self._mark(ev, reads, writes)
        self.n_instr += 1

    def _mark(self, ev, reads, writes):
        k, v = ev
        for t in writes:
            t.w = {k: v}
            t.r = {}
        for t in reads:
            if t.r.get(k, 0) < v:
                t.r[k] = v

    def dma(self, qn, out, in_, reads=(), writes=(), **kw):
        q = self.q[qn]
        self._deps(q, reads, writes)
        key = q.dsems[q.dnext]
        q.dnext = (q.dnext + 1) % len(q.dsems)
        self._wait(q, key, q.dtarget[key])
        q.dtarget[key] += 16
        q.ops.append(("d", out, in_, key, kw))
        self._mark((key, q.dtarget[key]), reads, writes)
        self.n_instr += 1

    def barrier(self):
        cur = {}
        for q in self.q.values():
            assert not q.pending, q.name
            cur[q.key] = q.count
            for k in q.dsems:
                cur[k] = q.dtarget[k]
        for q in self.q.values():
            for k, v in cur.items():
                if k == q.key:
                    continue
                if v > 0:
                    self._wait(q, k, v)

    def replay(self):
        nc = self.nc
        fwself = self
        with nc.Block() as block:
            def mk(q):
                def body(eng):
                    for o in q.ops:
                        if o[0] == "w":
                            eng.wait_ge(fwself.semh[o[1]], o[2])
                        elif o[0] == "i":
                            ins = o[1](eng)
                            if o[2]:
                                ins.then_inc(q.sem, 1)
                        else:
                            eng.dma_start(out=o[1], in_=o[2], **o[4]).then_inc(fwself.semh[o[3]], 16)
                return body
            block.tensor(mk(self.q["pe"]))
            block.scalar(mk(self.q["act"]))
            block.vector(mk(self.q["dve"]))
            block.gpsimd(mk(self.q["pool"]))
            block.sync(mk(self.q["sp"]))

D = 1024
NTOK = 2304
NT = 18
EPS = 1e-6
C_NAK, C_NAV, C_CKV, C_KR, C_GK, C_GV, C_SX, C_SB, C_SDT = 0, 512, 1024, 1280, 1312, 1440, 1568, 2080, 2336
C_NAQ, C_CQ, C_GQ, C_SC, C_SZ, C_GATE = 2352, 2864, 3248, 3760, 4016, 4528
FFH = 2816
NEG = -30000.0


def na_chunks(t):
    if t <= 1:
        return 1 + t, [0, 1, 2, 3]
    if t >= 14:
        return 3 + (t - 14), [12, 13, 14, 15]
    return 0, [t - 2, t - 1, t, t + 1, t + 2]


def build_program(debug=False, layers=(0, 1), stages=("mod", "norm1", "gqa", "mla", "na", "ssm", "merge", "ffn")):
    nc = bass.Bass("TRN2", target_bir_lowering=False)
    dt_in = lambda name, shape, dt=F32: nc.dram_tensor(name, list(shape), dt, kind="ExternalInput").ap()
    kind_scr = "ExternalOutput" if debug else "Internal"
    dt_scr = lambda name, shape, dt=F32: nc.dram_tensor(name, list(shape), dt, kind=kind_scr).ap()
    x_in = dt_in("x", [2, 2048, D])
    ctx_in = dt_in("ctx", [2, 256, D])
    cvec = dt_in("cvec", [3, D])
    w_ada = dt_in("w_ada", [2, D, 6 * D])
    b_ada = dt_in("b_ada", [2, 6 * D])
    g4 = dt_in("g4", [2, 4, D])
    w_in = dt_in("w_in", [2, D, 8624])
    nab = dt_in("nab", [2, 5, 128, 8, 5, 128])
    mla_g_q = dt_in("mla_g_q", [2, 384])
    mla_g_kv = dt_in("mla_g_kv", [2, 256])
    mla_w_uq = dt_in("mla_w_uq", [2, 384, 768])
    mla_w_ukv = dt_in("mla_w_ukv", [2, 256, 1024])
    gqa_g = dt_in("gqa_g", [2, 2, 64])
    conv_wT = dt_in("conv_wT", [2, D, 5])
    conv_b = dt_in("conv_b", [2, D])
    ssm_small = dt_in("ssm_small", [2, 40])
    ssm_g_norm = dt_in("ssm_g_norm", [2, 512])
    w_branch = dt_in("w_branch", [2, 4, 512, D])
    w_out = dt_in("w_out", [2, D, D])
    ffn_w1 = dt_in("ffn_w1", [2, D, FFH])
    ffn_w3 = dt_in("ffn_w3", [2, D, FFH])
    ffn_w2 = dt_in("ffn_w2", [2, FFH, D])
    consts = dt_in("consts", [128, 14, 128])
    ropeg = dt_in("ropeg", [128, 2, 16, 32])
    ropem = dt_in("ropem", [128, 2, 16, 16])
    out = nc.dram_tensor("out", [2, 2048, D], F32, kind="ExternalOutput").ap()
    combd = dt_scr("combd", [2, 3, 6, D])
    yTd = dt_scr("yTd", [4, 512, NTOK], BF16)
    xmid = dt_scr("xmid", [2, NTOK, D])
    xres = dt_scr("xres", [2, NTOK, D])

    ST = stages
    with ExitStack() as top:
        fw = FW(nc, top)
        uid = [0]

        def sb(es, shape, dt=F32, name="t"):
            uid[0] += 1
            t = es.enter_context(nc.sbuf_tensor(f"{name}{uid[0]}", list(shape), dt))
            return t

        ps = [top.enter_context(nc.psum_tensor(f"ps{i}", [128, 512], F32)) for i in range(8)]
        tp = [Tok(f"ps{i}", excl=True) for i in range(8)]
        psb = [p[:].bitcast(BF16) for p in ps]
        cst = sb(top, [128, 14, 128], F32, "cst")
        ident = sb(top, [128, 128], BF16, "ident")
        rg = sb(top, [128, 2, 16, 32], F32, "rg")
        rm = sb(top, [128, 2, 16, 16], F32, "rm")
        t_c = Tok("consts")
        fw.dma("sp", cst[:], consts, writes=[t_c])
        fw.dma("sp", rg[:], ropeg, writes=[t_c])
        fw.dma("sp", rm[:], ropem, writes=[t_c])
        fw.op("dve", lambda e: e.tensor_copy(out=ident[:], in_=cst[:, 0, :]), [t_c], [t_c])
        identf = cst[:, 0, :]
        tri = [cst[:, 1, :], cst[:, 2, :]]
        mneg4 = [cst[:, 3:7, :].rearrange("p a q -> p (a q)"), cst[:, 7:11, :].rearrange("p a q -> p (a q)")]
        onesf = cst[:, 11, :]
        ntri = [cst[:, 12, :], cst[:, 13, :]]
        t_comb = [Tok(f"comb{l}") for l in range(2)]
        t_yT = {}
        t_xmid = {}
        t_xres = {}

        def tk(dct, key):
            if key not in dct:
                dct[key] = Tok(str(key))
            return dct[key]

        def V(fn, r, w):
            fw.op("dve", fn, r, w)

        def A(fn, r, w):
            fw.op("act", fn, r, w)

        def G(fn, r, w):
            fw.op("pool", fn, r, w)

        def MM(o, lhsT, rhs, r, w, start=True, stop=True, sig=True):
            fw.op("pe", lambda e: e.matmul(o, lhsT=lhsT, rhs=rhs, start=start, stop=stop), r, w, signal=sig)

        def TR(o, in_, idn, r, w, sig=True):
            fw.op("pe", lambda e: e.transpose(out=o, in_=in_, identity=idn), r, w, signal=sig)

        def load_w(dst, src, tok, q="pool"):
            fw.dma(q, dst, src.rearrange("(k p) n -> p k n", p=128), writes=[tok])

        def bcast_load(dst, src_row, tok, parts=128):
            fw.dma("sp", dst, src_row.partition_broadcast(parts), writes=[tok])

        def rsqrt_mean(ap, n, r_w):
            A(lambda e: e.activation(out=ap, in_=ap, func=AF.Ln, scale=1.0 / n, bias=EPS), r_w, r_w)
            A(lambda e: e.activation(out=ap, in_=ap, func=AF.Exp, scale=-0.5), r_w, r_w)

        def xsrc(l, b, ti):
            if l == 0:
                if ti < 2:
                    return ctx_in[b, ti * 128:(ti + 1) * 128, :], []
                return x_in[b, (ti - 2) * 128:(ti - 1) * 128, :], []
            return xres[b, ti * 128:(ti + 1) * 128, :], [tk(t_xres, (b, ti))]

        def rope(dst3, src3, cos, sin, H, half, tmp, toks_r, tok_tmp, tok_dst):
            cb = cos.unsqueeze(1).broadcast_to([128, H, half])
            sn = sin.unsqueeze(1).broadcast_to([128, H, half])
            x1 = src3[:, :, 0:half]
            x2 = src3[:, :, half:2 * half]
            V(lambda e: e.tensor_tensor(out=tmp[:, 0], in0=x1, in1=cb, op=ALU.mult), toks_r, [tok_tmp])
            V(lambda e: e.tensor_tensor(out=tmp[:, 1], in0=x2, in1=sn, op=ALU.mult), toks_r, [tok_tmp])
            G(lambda e: e.tensor_tensor(out=tmp[:, 2], in0=x1, in1=sn, op=ALU.mult), toks_r, [tok_tmp])
            G(lambda e: e.tensor_tensor(out=tmp[:, 3], in0=x2, in1=cb, op=ALU.mult), toks_r, [tok_tmp])
            V(lambda e: e.tensor_tensor(out=dst3[:, :, 0:half], in0=tmp[:, 0], in1=tmp[:, 1], op=ALU.subtract), [tok_tmp], [tok_dst])
            V(lambda e: e.tensor_tensor(out=dst3[:, :, half:2 * half], in0=tmp[:, 2], in1=tmp[:, 3], op=ALU.add), [tok_tmp], [tok_dst])

        def stage_mod(l):
            with ExitStack() as es:
                cin = sb(es, [3, D]); t_cin = Tok()
                cs = sb(es, [3, D], BF16)
                csT = sb(es, [128, 8, 4], BF16); t_csT = Tok()
                modr = sb(es, [3, 6 * D]); t_mod = Tok()
                bad = sb(es, [3, 6 * D]); t_bad = Tok()
                g4t = sb(es, [3, 4, D]); t_g4 = Tok()
                comb = sb(es, [3, 6, D]); t_cb = Tok()
                wts = [sb(es, [128, 8, 512], BF16) for _ in range(2)]
                t_w = [Tok(), Tok()]
                fw.dma("sp", cin[:], cvec, writes=[t_cin])
                bcast_load(bad[:], b_ada[l, :], t_bad, 3)
                for j in range(4):
                    bcast_load(g4t[:, j, :], g4[l, j, :], t_g4, 3)
                A(lambda e: e.activation(out=cs[:], in_=cin[:], func=AF.Silu), [t_cin], [t_cin])
                for kc in range(8):
                    TR(psb[7][:, kc * 4:kc * 4 + 3], cs[:, kc * 128:(kc + 1) * 128], ident[0:3, 0:3], [t_cin, t_c], [tp[7]], sig=(kc == 7))
                V(lambda e: e.tensor_copy(out=csT[:, :, 0:3], in_=psb[7][:, 0:32].rearrange("p (k f) -> p k f", f=4)[:, :, 0:3]), [tp[7]], [t_csT])
                for n in range(12):
                    wt = wts[n % 2]
                    load_w(wt[:], w_ada[l, :, n * 512:(n + 1) * 512], t_w[n % 2])
                    pt = ps[n % 2]
                    for kc in range(8):
                        MM(pt[0:3, :], csT[:, kc, 0:3], wt[:, kc, :], [t_csT, t_w[n % 2]], [tp[n % 2]], start=(kc == 0), stop=(kc == 7), sig=(kc == 7))
                    V(lambda e, n=n, pt=pt: e.tensor_tensor(out=modr[:, n * 512:(n + 1) * 512], in0=pt[0:3, :], in1=bad[:, n * 512:(n + 1) * 512], op=ALU.add),
                      [tp[n % 2], t_bad], [t_mod])
                m = lambda j: modr[:, j * D:(j + 1) * D]
                V(lambda e: e.scalar_tensor_tensor(out=comb[:, 0, :], in0=m(1), scalar=1.0, in1=g4t[:, 0, :], op0=ALU.add, op1=ALU.mult), [t_mod, t_g4], [t_cb])
                V(lambda e: e.tensor_copy(out=comb[:, 1, :], in_=m(0)), [t_mod], [t_cb])
                V(lambda e: e.tensor_tensor(out=comb[:, 2, :], in0=m(2), in1=g4t[:, 1, :], op=ALU.mult), [t_mod, t_g4], [t_cb])
                V(lambda e: e.scalar_tensor_tensor(out=comb[:, 3, :], in0=m(4), scalar=1.0, in1=g4t[:, 2, :], op0=ALU.add, op1=ALU.mult), [t_mod, t_g4], [t_cb])
                V(lambda e: e.tensor_copy(out=comb[:, 4, :], in_=m(3)), [t_mod], [t_cb])
                V(lambda e: e.tensor_tensor(out=comb[:, 5, :], in0=m(5), in1=g4t[:, 3, :], op=ALU.mult), [t_mod, t_g4], [t_cb])
                fw.dma("sp", combd[l], comb[:], reads=[t_cb], writes=[t_comb[l]])
                fw.barrier()

        def norm_mod_T(es, xt, t_x, Ab, Bb, t_ab, hdst, t_hdst, wk):
            junk, ss, t1, hb, t_wk = wk
            A(lambda e: e.activation(out=junk[:], in_=xt, func=AF.Square, accum_out=ss[:]), [t_x], [t_wk])
            rsqrt_mean(ss[:], D, [t_wk])
            V(lambda e: e.scalar_tensor_tensor(out=t1[:], in0=xt, scalar=ss[:, 0:1], in1=Ab, op0=ALU.mult, op1=ALU.mult), [t_x, t_wk, t_ab], [t_wk])
            V(lambda e: e.tensor_tensor(out=hb[:], in0=t1[:], in1=Bb, op=ALU.add), [t_wk, t_ab], [t_wk])
            for kc in range(8):
                TR(psb[7][:, kc * 128:(kc + 1) * 128], hb[:, kc * 128:(kc + 1) * 128], ident[:], [t_wk, t_c], [tp[7]], sig=(kc == 7))
            A(lambda e: e.activation(out=hdst, in_=psb[7].rearrange("p (k t) -> p k t", k=8), func=AF.Copy), [tp[7]], [t_hdst])

        def load_comb(es, l, b, j0):
            tl = {}
            t_ab = Tok()
            for rname, row in (("b", b), ("c", 2)):
                for j in range(j0, j0 + 3):
                    t = sb(es, [128, D])
                    fw.dma("sp", t[:], combd[l, row, j, :].partition_broadcast(128), reads=[t_comb[l]], writes=[t_ab])
                    tl[(rname, j)] = t
            return tl, t_ab

        def stage_norm1(l, b, hT, t_hT):
            with ExitStack() as es:
                tl, t_ab = load_comb(es, l, b, 0)
                xb = [sb(es, [128, D]) for _ in range(2)]
                t_xb = [Tok(), Tok()]
                wks = []
                for i in range(2):
                    wks.append((sb(es, [128, D], BF16), sb(es, [128, 1]), sb(es, [128, D]), sb(es, [128, D], BF16), Tok()))
                for ti in range(NT):
                    src, rt = xsrc(l, b, ti)
                    fw.dma("sp", xb[ti % 2][:], src, reads=rt, writes=[t_xb[ti % 2]])
                    rn = "c" if ti < 2 else "b"
                    norm_mod_T(es, xb[ti % 2][:], t_xb[ti % 2], tl[(rn, 0)][:], tl[(rn, 1)][:], t_ab,
                               hT[:, :, ti * 128:(ti + 1) * 128], t_hT[ti], wks[ti % 2])
                fw.barrier()

        def proj_tok(pt, hT, t_hT, ti, wt, t_w, tps):
            for kc in range(8):
                MM(pt, hT[:, kc, ti * 128:(ti + 1) * 128], wt[:, kc, :], [t_hT[ti], t_w], tps, start=(kc == 0), stop=(kc == 7), sig=(kc == 7))

        def emit_yT(es, k, yb, t_yb, tok0, nq, yTs, t_yTs):
            for j in range(nq // 128):
                bk = 7
                for ec in range(4):
                    TR(psb[bk][:, ec * 128:(ec + 1) * 128], yb[:, j, ec * 128:(ec + 1) * 128], ident[:], [t_yb, t_c], [tp[bk]], sig=(ec == 3))
                V(lambda e, j=j, bk=bk: e.tensor_copy(out=yTs[:, :, j * 128:(j + 1) * 128], in_=psb[bk][:, 0:512].rearrange("p (c t) -> p c t", c=4)),
                  [tp[bk]], [t_yTs])
            fw.dma("sp", yTd[k].rearrange("(c p) t -> p c t", p=128)[:, :, tok0:tok0 + nq], yTs[:, :, 0:nq], reads=[t_yTs],
                   writes=[tk(t_yT, (k, tok0))])

        def attend(QT_fn, KT_fn, V_fn, t_q, t_kv, scale, nq, kchunks, yb, t_yb, pb, t_pb, rc, t_rc):
            nj = nq // 128
            last = len(kchunks) - 1
            items = [(h, ci, kc) for h in range(8) for ci, kc in enumerate(kchunks)]

            SBK = (0, 1, 6)

            def emitS(idx):
                h, ci, kc = items[idx]
                sbk = SBK[idx % 3]
                MM(ps[sbk][:, 0:nq], KT_fn(h, kc), QT_fn(h), [t_q, t_kv], [tp[sbk]])

            def emitRest(idx):
                h, ci, kc = items[idx]
                sbk = SBK[idx % 3]
                P = pb[idx % 3]
                tP = t_pb[idx % 3]
                A(lambda e: e.activation(out=P[:, 0:nq], in_=ps[sbk][:, 0:nq], func=AF.Exp, scale=scale), [tp[sbk]], [tP])
                for j in range(nj):
                    MM(ps[2 + j][:, 0:65], P[:, j * 128:(j + 1) * 128], V_fn(h, kc), [tP, t_kv], [tp[2 + j]],
                       start=(ci == 0), stop=(ci == last), sig=(ci == last))
                if ci == last:
                    for j in range(nj):
                        r = rc[j % 4]
                        tr_ = t_rc[j % 4]
                        V(lambda e, j=j, r=r: e.reciprocal(out=r[:], in_=ps[2 + j][:, 64:65]), [tp[2 + j]], [tr_])
                        V(lambda e, j=j, r=r: e.tensor_scalar(out=yb[:, j, h * 64:(h + 1) * 64], in0=ps[2 + j][:, 0:64], scalar1=r[:, 0:1], scalar2=None, op0=ALU.mult),
                          [tp[2 + j], tr_], [t_yb])

            emitS(0)
            if len(items) > 1:
                emitS(1)
            for idx in range(len(items)):
                if idx + 2 < len(items):
                    emitS(idx + 2)
                emitRest(idx)

        QBLOCKS = [(0, 256, [0, 1])] + [(256 + 512 * j, 512, list(range(NT))) for j in range(4)]

        def attn_work(es):
            yb = sb(es, [128, 4, 512], BF16); t_yb = Tok()
            yTs = sb(es, [128, 4, 512], BF16); t_yTs = Tok()
            pb = [sb(es, [128, 512], BF16) for _ in range(3)]
            t_pb = [Tok() for _ in range(3)]
            rc = [sb(es, [128, 1]) for _ in range(4)]
            t_rc = [Tok() for _ in range(4)]
            return yb, t_yb, yTs, t_yTs, pb, t_pb, rc, t_rc

        def stage_gqa(l, b, hT, t_hT):
            with ExitStack() as es:
                wq = sb(es, [128, 8, 512], BF16); wkv = sb(es, [128, 8, 256], BF16); t_w = Tok()
                load_w(wq[:], w_in[l, :, C_GQ:C_GQ + 512], t_w)
                load_w(wkv[:], w_in[l, :, C_GK:C_GK + 256], t_w)
                gq = sb(es, [128, 64]); gk = sb(es, [128, 64]); t_g = Tok()
                bcast_load(gq[:], gqa_g[l, 0, :], t_g)
                bcast_load(gk[:], gqa_g[l, 1, :], t_g)
                KT = sb(es, [128, 2, NTOK], BF16); Vg = sb(es, [128, NT, 2, 80], BF16); t_kv = Tok()
                G(lambda e: e.memset(Vg[:, :, :, 64:65], 1.0), [], [t_kv])
                qf = sb(es, [128, 512]); sq = sb(es, [128, 512]); ssq = sb(es, [128, 8]); qn = sb(es, [128, 512]); t_qw = Tok()
                tmp = sb(es, [128, 4, 8, 32]); t_tmp = Tok()
                qb = sb(es, [128, 512], BF16); t_qb = Tok()
                kd = sb(es, [128, 2, 2, 64], BF16); t_kd = Tok()
                QTb = sb(es, [128, 4, 512], BF16); t_q = Tok()
                work = attn_work(es)

                def normrope(src_ps, tps, H, gt, ti, dst3, t_dst):
                    n = H * 64
                    A(lambda e: e.activation(out=qf[:, 0:n], in_=src_ps, func=AF.Copy), tps, [t_qw])
                    V(lambda e: e.tensor_tensor(out=sq[:, 0:n], in0=qf[:, 0:n], in1=qf[:, 0:n], op=ALU.mult), [t_qw], [t_qw])
                    V(lambda e: e.tensor_reduce(out=ssq[:, 0:H], in_=sq[:, 0:n].rearrange("p (h d) -> p h d", h=H), axis=AX.X, op=ALU.add), [t_qw], [t_qw])
                    rsqrt_mean(ssq[:, 0:H], 64, [t_qw])
                    q3 = qf[:, 0:n].rearrange("p (h d) -> p h d", h=H)
                    n3 = qn[:, 0:n].rearrange("p (h d) -> p h d", h=H)
                    V(lambda e: e.tensor_tensor(out=n3, in0=q3, in1=ssq[:, 0:H].unsqueeze(2).broadcast_to([128, H, 64]), op=ALU.mult), [t_qw], [t_qw])
                    if ti >= 2 and 'noRope' not in ST:
                        V(lambda e: e.tensor_tensor(out=n3, in0=n3, in1=gt[:].unsqueeze(1).broadcast_to([128, H, 64]), op=ALU.mult), [t_qw, t_g], [t_qw])
                        rope(dst3, n3, rg[:, 0, ti - 2, :], rg[:, 1, ti - 2, :], H, 32, tmp[:, :, 0:H, :], [t_qw, t_c], t_tmp, t_dst)
                    else:
                        V(lambda e: e.tensor_tensor(out=dst3, in0=n3, in1=gt[:].unsqueeze(1).broadcast_to([128, H, 64]), op=ALU.mult), [t_qw, t_g], [t_dst])

                for ti in range(NT):
                    bk = 6
                    proj_tok(ps[bk][:, 0:256], hT, t_hT, ti, wkv, t_w, [tp[bk]])
                    V(lambda e, ti=ti, bk=bk: e.tensor_copy(out=Vg[:, ti, :, 0:64], in_=ps[bk][:, 128:256].rearrange("p (g d) -> p g d", g=2)), [tp[bk]], [t_kv])
                    if 'gqaK0' in ST:
                        continue
                    normrope(ps[bk][:, 0:128], [tp[bk]], 2, gk, ti, kd[:, :, 0, :], t_kd)
                    if 'gqaK1' in ST:
                        continue
                    G(lambda e: e.tensor_copy(out=kd[:, :, 1, :], in_=kd[:, :, 0, :]), [t_kd], [t_kd])
                    for g in range(2):
                        TR(psb[7][:, g * 128:(g + 1) * 128], kd[:, g, :, :].rearrange("p c d -> p (c d)"), ident[:], [t_kd, t_c], [tp[7]], sig=(g == 1))
                    A(lambda e, ti=ti, bk=bk: e.activation(out=KT[:, :, ti * 128:(ti + 1) * 128], in_=psb[7][:, 0:256].rearrange("p (g t) -> p g t", g=2), func=AF.Copy),
                      [tp[7]], [t_kv])
                for (tok0, nq, kch) in QBLOCKS:
                    if 'gqaK' in ST:
                        continue
                    for jt in range(nq // 128):
                        ti = tok0 // 128 + jt
                        bk = 6
                        proj_tok(ps[bk][:, 0:512], hT, t_hT, ti, wq, t_w, [tp[bk]])
                        normrope(ps[bk][:, 0:512], [tp[bk]], 8, gq, ti, qb[:].rearrange("p (h d) -> p h d", h=8), t_qb)
                        for pr in range(4):
                            TR(psb[7][:, pr * 128:(pr + 1) * 128], qb[:, pr * 128:(pr + 1) * 128], ident[:], [t_qb, t_c], [tp[7]], sig=(pr == 3))
                        A(lambda e, jt=jt, bk=bk: e.activation(out=QTb[:, :, jt * 128:(jt + 1) * 128], in_=psb[7][:, 0:512].rearrange("p (c t) -> p c t", c=4), func=AF.Copy),
                          [tp[7]], [t_q])
                    yb, t_yb, yTs, t_yTs, pb, t_pb, rc, t_rc = work
                    if 'noattn' in ST:
                        continue
                    attend(lambda h: QTb[(h % 2) * 64:(h % 2) * 64 + 64, h // 2, 0:nq],
                           lambda h, kc: KT[(h % 2) * 64:(h % 2) * 64 + 64, h // 4, kc * 128:(kc + 1) * 128],
                           lambda h, kc: Vg[:, kc, h // 4, 0:65],
                           t_q, t_kv, 0.125, nq, kch, yb, t_yb, pb, t_pb, rc, t_rc)
                    emit_yT(es, 2, yb, t_yb, tok0, nq, yTs, t_yTs)
                fw.barrier()

        def stage_mla(l, b, hT, t_hT):
            with ExitStack() as es:
                wkv = sb(es, [128, 8, 288], BF16); wq = sb(es, [128, 8, 384], BF16); t_w = Tok()
                load_w(wkv[:], w_in[l, :, C_CKV:C_CKV + 288], t_w)
                load_w(wq[:], w_in[l, :, C_CQ:C_CQ + 384], t_w)
                wuq = sb(es, [128, 3, 768], BF16); wukv = sb(es, [128, 2, 1024], BF16)
                load_w(wuq[:], mla_w_uq[l], t_w)
                load_w(wukv[:], mla_w_ukv[l], t_w)
                gq = sb(es, [128, 384]); gkv = sb(es, [128, 256]); t_g = Tok()
                bcast_load(gq[:], mla_g_q[l, :], t_g)
                bcast_load(gkv[:], mla_g_kv[l, :], t_g)
                KT = sb(es, [96, 8, NTOK], BF16); Vm = sb(es, [128, NT, 8, 80], BF16); t_kv = Tok()
                G(lambda e: e.memset(Vm[:, :, :, 64:65], 1.0), [], [t_kv])
                cf = sb(es, [128, 416]); junk = sb(es, [128, 384], BF16); ss = sb(es, [128, 1]); cn = sb(es, [128, 384], BF16); t_cw = Tok()
                cT = sb(es, [128, 3, 128], BF16); t_cT = Tok()
                kpe = sb(es, [128, 1, 32]); tmp = sb(es, [128, 4, 8, 16]); t_tmp = Tok(); t_kpe = Tok()
                kfull = sb(es, [128, 8, 96], BF16); t_kf = Tok()
                qfull = sb(es, [128, 8, 96], BF16); t_qf = Tok()
                qpe = sb(es, [128, 8, 32]); t_qpe = Tok()
                QTb = sb(es, [96, 8, 512], BF16); t_q = Tok()
                work = attn_work(es)

                def lowrank(src_ps, tps, n, gt):
                    A(lambda e: e.activation(out=cf[:, 0:n], in_=src_ps, func=AF.Copy), tps, [t_cw])
                    A(lambda e: e.activation(out=junk[:, 0:n], in_=cf[:, 0:n], func=AF.Square, accum_out=ss[:]), [t_cw], [t_cw])
                    rsqrt_mean(ss[:], n, [t_cw])
                    V(lambda e: e.scalar_tensor_tensor(out=cn[:, 0:n], in0=cf[:, 0:n], scalar=ss[:, 0:1], in1=gt[:, 0:n], op0=ALU.mult, op1=ALU.mult), [t_cw, t_g], [t_cw])
                    for c in range(n // 128):
                        TR(psb[7][:, c * 128:(c + 1) * 128], cn[:, c * 128:(c + 1) * 128], ident[:], [t_cw, t_c], [tp[7]], sig=(c == n // 128 - 1))
                    V(lambda e: e.tensor_copy(out=cT[:, 0:n // 128, :], in_=psb[7][:, 0:n].rearrange("p (c t) -> p c t", t=128)), [tp[7]], [t_cT])

                for ti in range(NT):
                    proj_tok(ps[6][:, 0:288], hT, t_hT, ti, wkv, t_w, [tp[6]])
                    if ti >= 2:
                        A(lambda e: e.activation(out=cf[:, 384:416], in_=ps[6][:, 256:288], func=AF.Copy), [tp[6]], [t_kpe])
                        rope(kpe[:], cf[:, 384:416].rearrange("p (h d) -> p h d", h=1), rm[:, 0, ti - 2, :], rm[:, 1, ti - 2, :], 1, 16, tmp[:, :, 0:1, :], [t_kpe, t_c], t_tmp, t_kpe)
                    else:
                        A(lambda e: e.activation(out=kpe[:, 0, :], in_=ps[6][:, 256:288], func=AF.Copy), [tp[6]], [t_kpe])
                    lowrank(ps[6][:, 0:256], [tp[6]], 256, gkv)
                    for half in range(2):
                        bk = 4 + half
                        for c in range(2):
                            MM(ps[bk][:, 0:512], cT[:, c, :], wukv[:, c, half * 512:(half + 1) * 512], [t_cT, t_w], [tp[bk]], start=(c == 0), stop=(c == 1), sig=(c == 1))
                        kv3 = ps[bk][:, 0:512].rearrange("p (h d) -> p h d", h=4)
                        V(lambda e, kv3=kv3, half=half: e.tensor_copy(out=kfull[:, half * 4:(half + 1) * 4, 0:64], in_=kv3[:, :, 0:64]), [tp[bk]], [t_kf])
                        A(lambda e, kv3=kv3, half=half, ti=ti: e.activation(out=Vm[:, ti, half * 4:(half + 1) * 4, 0:64], in_=kv3[:, :, 64:128], func=AF.Copy), [tp[bk]], [t_kv])
                    G(lambda e: e.tensor_copy(out=kfull[:, :, 64:96], in_=kpe[:].broadcast_to([128, 8, 32])), [t_kpe], [t_kf])
                    for h in range(8):
                        TR(psb[7][0:96, h * 128:(h + 1) * 128], kfull[:, h, :], ident[:], [t_kf, t_c], [tp[7]], sig=(h == 7))
                    A(lambda e, ti=ti: e.activation(out=KT[:, :, ti * 128:(ti + 1) * 128], in_=psb[7][0:96, :].rearrange("p (h t) -> p h t", h=8), func=AF.Copy), [tp[7]], [t_kv])
                for (tok0, nq, kch) in QBLOCKS:
                    for jt in range(nq // 128):
                        ti = tok0 // 128 + jt
                        proj_tok(ps[6][:, 0:384], hT, t_hT, ti, wq, t_w, [tp[6]])
                        lowrank(ps[6][:, 0:384], [tp[6]], 384, gq)
                        for c in range(3):
                            MM(ps[4][:, 0:512], cT[:, c, :], wuq[:, c, 0:512], [t_cT, t_w], [tp[4]], start=(c == 0), stop=(c == 2), sig=(c == 2))
                        for c in range(3):
                            MM(ps[5][:, 0:256], cT[:, c, :], wuq[:, c, 512:768], [t_cT, t_w], [tp[5]], start=(c == 0), stop=(c == 2), sig=(c == 2))
                        for h in range(8):
                            c0 = h * 96
                            for (a, bnd, dst_off) in ((c0, c0 + 64, 0),):
                                pass
                        def colcopy(dst_fn, c_lo, c_hi, eng):
                            segs = []
                            if c_lo < 512:
                                segs.append((4, c_lo, min(c_hi, 512), c_lo))
                            if c_hi > 512:
                                segs.append((5, max(c_lo, 512) - 512, c_hi - 512, max(c_lo, 512)))
                            for (bk, lo, hi, g0) in segs:
                                eng(lambda e, bk=bk, lo=lo, hi=hi, g0=g0: e.tensor_copy(out=dst_fn(g0 - c_lo, g0 - c_lo + hi - lo), in_=ps[bk][:, lo:hi]), [tp[bk]], None)
                        for h in range(8):
                            c0 = h * 96
                            segs = []
                            for (lo, hi, kind) in ((c0, c0 + 64, "n"), (c0 + 64, c0 + 96, "p")):
                                parts = []
                                if lo < 512:
                                    parts.append((4, lo, min(hi, 512), 0))
                                if hi > 512:
                                    parts.append((5, max(lo, 512) - 512, hi - 512, max(lo, 512) - lo))
                                for (bk, a, bb, off) in parts:
                                    if kind == "n":
                                        V(lambda e, bk=bk, a=a, bb=bb, off=off, h=h: e.tensor_copy(out=qfull[:, h, off:off + bb - a], in_=ps[bk][:, a:bb]), [tp[bk]], [t_qf])
                                    else:
                                        dst = qpe if ti >= 2 else None
                                        if ti >= 2:
                                            V(lambda e, bk=bk, a=a, bb=bb, off=off, h=h: e.tensor_copy(out=qpe[:, h, off:off + bb - a], in_=ps[bk][:, a:bb]), [tp[bk]], [t_qpe])
                                        else:
                                            V(lambda e, bk=bk, a=a, bb=bb, off=off, h=h: e.tensor_copy(out=qfull[:, h, 64 + off:64 + off + bb - a], in_=ps[bk][:, a:bb]), [tp[bk]], [t_qf])
                        if ti >= 2:
                            rope(qfull[:, :, 64:96], qpe[:], rm[:, 0, ti - 2, :], rm[:, 1, ti - 2, :], 8, 16, tmp[:], [t_qpe, t_c], t_tmp, t_qf)
                        for h in range(8):
                            TR(psb[7][0:96, h * 128:(h + 1) * 128], qfull[:, h, :], ident[:], [t_qf, t_c], [tp[7]], sig=(h == 7))
                        A(lambda e, jt=jt: e.activation(out=QTb[:, :, jt * 128:(jt + 1) * 128], in_=psb[7][0:96, :].rearrange("p (h t) -> p h t", h=8), func=AF.Copy), [tp[7]], [t_q])
                    yb, t_yb, yTs, t_yTs, pb, t_pb, rc, t_rc = work
                    attend(lambda h: QTb[:, h, 0:nq],
                           lambda h, kc: KT[:, h, kc * 128:(kc + 1) * 128],
                           lambda h, kc: Vm[:, kc, h, 0:65],
                           t_q, t_kv, 96.0 ** -0.5, nq, kch, yb, t_yb, pb, t_pb, rc, t_rc)
                    emit_yT(es, 1, yb, t_yb, tok0, nq, yTs, t_yTs)
                fw.barrier()

        def stage_na(l, b, hT, t_hT):
            with ExitStack() as es:
                wk = sb(es, [128, 8, 512], BF16); wv = sb(es, [128, 8, 512], BF16); wq = sb(es, [128, 8, 512], BF16); t_w = Tok()
                load_w(wk[:], w_in[l, :, C_NAK:C_NAK + 512], t_w)
                load_w(wv[:], w_in[l, :, C_NAV:C_NAV + 512], t_w)
                load_w(wq[:], w_in[l, :, C_NAQ:C_NAQ + 512], t_w)
                KT = sb(es, [128, 4, NTOK], BF16); Vn = sb(es, [128, NT, 8, 80], BF16); t_kv = Tok()
                G(lambda e: e.memset(Vn[:, :, :, 64:65], 1.0), [], [t_kv])
                QT = sb(es, [128, 4, NTOK], BF16); t_q = Tok()
                blocks = [(i * 512, min(512, NTOK - i * 512)) for i in range(5)]
                cnt = 0
                for (wt, dstT, tdst) in ((wk, KT, t_kv), (wq, QT, t_q)):
                    for pr in range(4):
                        for (t0, n) in blocks:
                            bk = 5 + cnt % 2
                            cnt += 1
                            for kc in range(8):
                                MM(ps[bk][:, 0:n], wt[:, kc, pr * 128:(pr + 1) * 128], hT[:, kc, t0:t0 + n], [t_w] + t_hT[t0 // 128:(t0 + n) // 128], [tp[bk]],
                                   start=(kc == 0), stop=(kc == 7), sig=(kc == 7))
                            A(lambda e, bk=bk, dstT=dstT, pr=pr, t0=t0, n=n: e.activation(out=dstT[:, pr, t0:t0 + n], in_=ps[bk][:, 0:n], func=AF.Copy), [tp[bk]], [tdst])
                for ti in range(NT):
                    bk = 5 + ti % 2
                    proj_tok(ps[bk][:, 0:512], hT, t_hT, ti, wv, t_w, [tp[bk]])
                    V(lambda e, ti=ti, bk=bk: e.tensor_copy(out=Vn[:, ti, :, 0:64], in_=ps[bk][:, 0:512].rearrange("p (h d) -> p h d", h=8)), [tp[bk]], [t_kv])
                yb, t_yb, yTs, t_yTs, pb, t_pb, rc, t_rc = attn_work(es)
                attend(lambda h: QT[(h % 2) * 64:(h % 2) * 64 + 64, h // 2, 0:256],
                       lambda h, kc: KT[(h % 2) * 64:(h % 2) * 64 + 64, h // 2, kc * 128:(kc + 1) * 128],
                       lambda h, kc: Vn[:, kc, h, 0:65],
                       t_q, t_kv, 0.125, 256, [0, 1], yb, t_yb, pb, t_pb, rc, t_rc)
                emit_yT(es, 0, yb, t_yb, 0, 256, yTs, t_yTs)
                bt = [sb(es, [128, 5, 128]) for _ in range(3)]; t_bt = [Tok() for _ in range(3)]
                bres = sb(es, [128, 8, 5, 128]); t_bres = Tok()
                fw.dma("sp", bres[:], nab[l, 0], writes=[t_bres])
                Sb = [sb(es, [128, 5, 128]) for _ in range(2)]; t_Sb = [Tok() for _ in range(2)]
                Pn = [sb(es, [128, 7, 128], BF16) for _ in range(2)]; t_Pn = [Tok() for _ in range(2)]
                iters = [(t, h) for t in range(16) for h in range(8)]

                def na_S(it):
                    t, h = iters[it]
                    cls, chunks = na_chunks(t)
                    nloc = len(chunks)
                    q0 = 256 + t * 128
                    par = (h % 2) * 64
                    if cls != 0:
                        fw.dma("sp", bt[it % 3][:], nab[l, cls, :, h, :, :], writes=[t_bt[it % 3]])
                    pa = 0 + 2 * (it % 2); pbk = 1 + 2 * (it % 2)
                    qv = QT[par:par + 64, h // 2, q0:q0 + 128]
                    for s_, c in enumerate(chunks[:4]):
                        kt0 = 256 + c * 128
                        MM(ps[pa][:, s_ * 128:(s_ + 1) * 128], KT[par:par + 64, h // 2, kt0:kt0 + 128], qv, [t_q, t_kv], [tp[pa]], sig=(s_ == 3))
                    for s_ in range(2):
                        MM(ps[pbk][:, s_ * 128:(s_ + 1) * 128], KT[par:par + 64, h // 2, s_ * 128:(s_ + 1) * 128], qv, [t_q, t_kv], [tp[pbk]], sig=(s_ == 1 and nloc == 4))
                    if nloc == 5:
                        kt0 = 256 + chunks[4] * 128
                        MM(ps[pbk][:, 256:384], KT[par:par + 64, h // 2, kt0:kt0 + 128], qv, [t_q, t_kv], [tp[pbk]])

                def na_rest(it):
                    t, h = iters[it]
                    cls, chunks = na_chunks(t)
                    nloc = len(chunks)
                    if cls == 0:
                        B_ = bres[:, h]; tB = t_bres
                    else:
                        B_ = bt[it % 3][:]; tB = t_bt[it % 3]
                    pa = 0 + 2 * (it % 2); pbk = 1 + 2 * (it % 2)
                    S_ = Sb[it % 2]; tS = t_Sb[it % 2]
                    P_ = Pn[it % 2]; tP = t_Pn[it % 2]
                    V(lambda e: e.scalar_tensor_tensor(out=S_[:, 0:4, :], in0=ps[pa][:, 0:512].rearrange("p (s q) -> p s q", s=4), scalar=0.125,
                                                       in1=B_[:, 0:4, :], op0=ALU.mult, op1=ALU.add), [tp[pa], tB], [tS])
                    if nloc == 5:
                        V(lambda e: e.scalar_tensor_tensor(out=S_[:, 4, :], in0=ps[pbk][:, 256:384], scalar=0.125,
                                                           in1=B_[:, 4, :], op0=ALU.mult, op1=ALU.add), [tp[pbk], tB], [tS])
                    A(lambda e: e.activation(out=P_[:, 5:7, :], in_=ps[pbk][:, 0:256].rearrange("p (s q) -> p s q", s=2), func=AF.Exp, scale=0.125), [tp[pbk]], [tP])
                    A(lambda e: e.activation(out=P_[:, 0:nloc, :], in_=S_[:, 0:nloc, :], func=AF.Exp), [tS], [tP])
                    ob = 4 + it % 2
                    seq = [(5, 0), (6, 1)] + [(s_, 2 + c) for s_, c in enumerate(chunks)]
                    for i, (s_, kti) in enumerate(seq):
                        MM(ps[ob][:, 0:65], P_[:, s_, :], Vn[:, kti, h, 0:65], [tP, t_kv], [tp[ob]], start=(i == 0), stop=(i == len(seq) - 1), sig=(i == len(seq) - 1))
                    r = rc[it % 4]; tr_ = t_rc[it % 4]
                    V(lambda e: e.reciprocal(out=r[:], in_=ps[ob][:, 64:65]), [tp[ob]], [tr_])
                    V(lambda e: e.tensor_scalar(out=yb[:, t % 4, h * 64:(h + 1) * 64], in0=ps[ob][:, 0:64], scalar1=r[:, 0:1], scalar2=None, op0=ALU.mult),
                      [tp[ob], tr_], [t_yb])
                    if h == 7 and t % 4 == 3:
                        emit_yT(es, 0, yb, t_yb, 256 + (t - 3) * 128, 512, yTs, t_yTs)

                na_S(0)
                for it in range(len(iters)):
                    if it + 1 < len(iters):
                        na_S(it + 1)
                    na_rest(it)
                fw.barrier()

        def stage_ssm(l, b, hT, t_hT):
            with ExitStack() as es:
                t_w = Tok()
                wdt = sb(es, [128, 8, 16], BF16); load_w(wdt[:], w_in[l, :, C_SDT:C_SDT + 16], t_w)
                wz = sb(es, [128, 8, 512], BF16); load_w(wz[:], w_in[l, :, C_SZ:C_SZ + 512], t_w)
                cw = sb(es, [128, 8, 5]); cb = sb(es, [128, 8]); t_cw = Tok()
                fw.dma("sp", cw[:], conv_wT[l].rearrange("(c p) j -> p c j", p=128), writes=[t_cw])
                fw.dma("sp", cb[:], conv_b[l].rearrange("(c p) -> p c", p=128), writes=[t_cw], allow_slow_non_contiguous=True)
                sm = sb(es, [128, 40]); t_sm = Tok()
                bcast_load(sm[:], ssm_small[l, :], t_sm)
                gn = sb(es, [128, 512]); bcast_load(gn[:], ssm_g_norm[l, :], t_sm)
                aneg = sb(es, [128, 16])
                A(lambda e: e.activation(out=aneg[:], in_=sm[:, 0:16], func=AF.Exp), [t_sm], [t_sm])
                V(lambda e: e.tensor_scalar(out=aneg[:], in0=aneg[:], scalar1=-1.0, scalar2=None, op0=ALU.mult), [t_sm], [t_sm])
                xs = sb(es, [128, NT, 512], BF16); t_xs = Tok()
                Bt = sb(es, [128, NT, 256], BF16); t_Bt = Tok()
                BT = sb(es, [128, 2, NTOK], BF16); CT = sb(es, [128, 2, NTOK], BF16); t_BC = Tok()
                dtt = sb(es, [128, NT, 16]); dta = sb(es, [128, NT, 16]); t_dt = Tok()
                PADW = 2 + 256 + 2 + 2 + 2048 + 2
                ec = es.enter_context(ExitStack())
                pre1 = sb(ec, [128, PADW]); pre = [pre1, pre1]; t_p1 = Tok(); t_pre = [t_p1, t_p1]
                acc1 = sb(ec, [128, NTOK]); acc = [acc1, acc1]; t_a1 = Tok(); t_acc = [t_a1, t_a1]
                post1 = sb(ec, [128, NTOK], BF16); post = [post1, post1]; t_po1 = Tok(); t_post = [t_po1, t_po1]
                wc = [sb(ec, [128, 8, 128], BF16) for _ in range(2)]; t_wc = [Tok(), Tok()]
                G(lambda e: e.memset(pre1[:], 0.0), [], [t_p1])
                segs = [(2, 0, 256), (262, 256, 2048)]
                blocks = [(i * 512, min(512, NTOK - i * 512)) for i in range(5)]
                for ch in range(8):
                    col0 = (C_SX + ch * 128) if ch < 6 else (C_SC + (ch - 6) * 128)
                    i2 = ch % 2
                    load_w(wc[i2][:], w_in[l, :, col0:col0 + 128], t_wc[i2])
                    for bi, (t0, n) in enumerate(blocks):
                        bk = 5 + bi % 2
                        for kc in range(8):
                            MM(ps[bk][:, 0:n], wc[i2][:, kc, :], hT[:, kc, t0:t0 + n], [t_wc[i2]] + t_hT[t0 // 128:(t0 + n) // 128], [tp[bk]],
                               start=(kc == 0), stop=(kc == 7), sig=(kc == 7))
                        if t0 == 0:
                            A(lambda e, bk=bk, i2=i2: e.activation(out=pre[i2][:, 2:258], in_=ps[bk][:, 0:256], func=AF.Copy), [tp[bk]], [t_pre[i2]])
                            A(lambda e, bk=bk, i2=i2: e.activation(out=pre[i2][:, 262:518], in_=ps[bk][:, 256:512], func=AF.Copy), [tp[bk]], [t_pre[i2]])
                        else:
                            o = 262 + (t0 - 256)
                            A(lambda e, bk=bk, i2=i2, o=o, n=n: e.activation(out=pre[i2][:, o:o + n], in_=ps[bk][:, 0:n], func=AF.Copy), [tp[bk]], [t_pre[i2]])
                    for (po, ao, L) in segs:
                        V(lambda e, i2=i2, po=po, ao=ao, L=L, ch=ch: e.tensor_scalar(out=acc[i2][:, ao:ao + L], in0=pre[i2][:, po - 2:po - 2 + L], scalar1=cw[:, ch, 0:1], scalar2=None, op0=ALU.mult),
                          [t_pre[i2], t_cw], [t_acc[i2]])
                        for j in range(1, 5):
                            V(lambda e, i2=i2, po=po, ao=ao, L=L, ch=ch, j=j: e.scalar_tensor_tensor(out=acc[i2][:, ao:ao + L], in0=pre[i2][:, po - 2 + j:po - 2 + j + L], scalar=cw[:, ch, j:j + 1],
                                                                                                  in1=acc[i2][:, ao:ao + L], op0=ALU.mult, op1=ALU.add), [t_pre[i2], t_cw, t_acc[i2]], [t_acc[i2]])
                    if ch < 4:
                        dst = post[i2][:]
                    elif ch < 6:
                        dst = BT[:, ch - 4, :]
                    else:
                        dst = CT[:, ch - 6, :]
                    tdst = t_post[i2] if ch < 4 else t_BC
                    A(lambda e, i2=i2, dst=dst, ch=ch: e.activation(out=dst, in_=acc[i2][:], func=AF.Silu, bias=cb[:, ch:ch + 1]), [t_acc[i2], t_cw], [tdst])
                    if ch < 6:
                        srcT = post[i2] if ch < 4 else None
                        for ti in range(NT):
                            bk = 7
                            if ch < 4:
                                TR(psb[bk][:, 0:128], post[i2][:, ti * 128:(ti + 1) * 128], ident[:], [t_post[i2], t_c], [tp[bk]])
                                V(lambda e, bk=bk, ti=ti, ch=ch: e.tensor_copy(out=xs[:, ti, ch * 128:(ch + 1) * 128], in_=psb[bk][:, 0:128]), [tp[bk]], [t_xs])
                            else:
                                TR(psb[bk][:, 0:128], BT[:, ch - 4, ti * 128:(ti + 1) * 128], ident[:], [t_BC, t_c], [tp[bk]])
                                V(lambda e, bk=bk, ti=ti, ch=ch: e.tensor_copy(out=Bt[:, ti, (ch - 4) * 128:(ch - 3) * 128], in_=psb[bk][:, 0:128]), [tp[bk]], [t_Bt])
                for ti in range(NT):
                    bk = 5 + ti % 2
                    proj_tok(ps[bk][:, 0:16], hT, t_hT, ti, wdt, t_w, [tp[bk]])
                    V(lambda e, bk=bk, ti=ti: e.tensor_tensor(out=dtt[:, ti, :], in0=ps[bk][:, 0:16], in1=sm[:, 16:32], op=ALU.add), [tp[bk], t_sm], [t_dt])
                A(lambda e: e.activation(out=dtt[:], in_=dtt[:], func=AF.Exp), [t_dt], [t_dt])
                A(lambda e: e.activation(out=dtt[:], in_=dtt[:], func=AF.Ln, bias=1.0), [t_dt], [t_dt])
                V(lambda e: e.tensor_tensor(out=dta[:], in0=dtt[:], in1=aneg[:].unsqueeze(1).broadcast_to([128, NT, 16]), op=ALU.mult), [t_dt, t_sm], [t_dt])
                fw.barrier()
                ec.close()
                ysum = sb(es, [128, NT, 512]); t_ys = [Tok() for _ in range(NT)]
                S = sb(es, [128, 8, 64]); Hb = sb(es, [128, 8, 64], BF16); t_S = Tok(); t_Hb = Tok()
                cbT = sb(es, [128, 2, 128]); t_cbT = Tok()
                sc2 = [sb(es, [128, 5, 8]) for _ in range(2)]; t_sc2 = [Tok(), Tok()]
                r1 = sb(es, [128, 8, 128]); r2 = sb(es, [128, 8, 128]); t_r = Tok()
                LT = sb(es, [128, 8, 128]); t_LT = Tok()
                MT2 = [sb(es, [128, 8, 128], BF16) for _ in range(2)]; t_MT2 = [Tok(), Tok()]
                xdt2 = [sb(es, [128, 8, 64], BF16) for _ in range(2)]; xw2 = [sb(es, [128, 8, 64], BF16) for _ in range(2)]; t_xd2 = [Tok(), Tok()]
                ytmp = sb(es, [128, 8, 64]); t_yt = Tok()
                steps = []
                for d_ in range(2):
                    order = list(range(NT)) if d_ == 0 else [1, 0] + list(range(NT - 1, 1, -1))
                    for j_, ti_ in enumerate(order):
                        steps.append((d_, ti_, j_ == 0))

                def phaseA(k):
                    d, ti, first = steps[k]
                    sc = sc2[k % 2]; t_sc = t_sc2[k % 2]
                    MT = MT2[k % 2]; t_MT = t_MT2[k % 2]
                    xdt = xdt2[k % 2]; xw = xw2[k % 2]; t_xd = t_xd2[k % 2]
                    dcol = dta[:, ti, d * 8:(d + 1) * 8]
                    for g in range(2):
                        MM(ps[5][:, g * 128:(g + 1) * 128], BT[:, g, ti * 128:(ti + 1) * 128], CT[:, g, ti * 128:(ti + 1) * 128], [t_BC], [tp[5]], sig=(g == 1))
                    A(lambda e: e.activation(out=cbT[:], in_=ps[5][:, 0:256].rearrange("p (g q) -> p g q", g=2), func=AF.Copy), [tp[5]], [t_cbT])
                    MM(ps[6][:, 0:8], tri[d], dcol, [t_c, t_dt], [tp[6]], sig=False)
                    MM(ps[6][:, 8:16], onesf, dcol, [t_c, t_dt], [tp[6]])
                    V(lambda e: e.tensor_copy(out=sc[:, 0:2, :], in_=ps[6][:, 0:16].rearrange("p (a h) -> p a h", a=2)), [tp[6]], [t_sc])
                    V(lambda e: e.tensor_tensor(out=sc[:, 2, :], in0=sc[:, 1, :], in1=sc[:, 0, :], op=ALU.subtract), [t_sc], [t_sc])
                    A(lambda e: e.activation(out=sc[:, 2, :], in_=sc[:, 2, :], func=AF.Exp), [t_sc], [t_sc])
                    A(lambda e: e.activation(out=sc[:, 3, :], in_=sc[:, 0, :], func=AF.Exp), [t_sc], [t_sc])
                    A(lambda e: e.activation(out=sc[:, 4, :], in_=sc[:, 1, :], func=AF.Exp), [t_sc], [t_sc])
                    V(lambda e: e.tensor_tensor(out=r1[:], in0=tri[d].unsqueeze(1).broadcast_to([128, 8, 128]), in1=dcol.unsqueeze(2).broadcast_to([128, 8, 128]), op=ALU.mult),
                      [t_c, t_dt], [t_r])
                    G(lambda e: e.tensor_copy(out=r2[:], in_=dcol.unsqueeze(2).broadcast_to([128, 8, 128])), [t_dt], [t_r])
                    for hb in range(2):
                        o = ps[hb][:, 0:512]
                        MM(o, onesf, r1[:, hb * 4:(hb + 1) * 4, :].rearrange("p h q -> p (h q)"), [t_c, t_r], [tp[hb]], start=True, stop=False, sig=False)
                        MM(o, ntri[d], r2[:, hb * 4:(hb + 1) * 4, :].rearrange("p h q -> p (h q)"), [t_c, t_r], [tp[hb]], start=False, stop=False, sig=False)
                        MM(o, identf, mneg4[d], [t_c], [tp[hb]], start=False, stop=True)
                        A(lambda e, hb=hb: e.activation(out=LT[:, hb * 4:(hb + 1) * 4, :], in_=ps[hb][:, 0:512].rearrange("p (h q) -> p h q", h=4), func=AF.Exp), [tp[hb]], [t_LT])
                        V(lambda e, hb=hb: e.tensor_tensor(out=MT[:, hb * 4:(hb + 1) * 4, :], in0=LT[:, hb * 4:(hb + 1) * 4, :],
                                                           in1=cbT[:, hb:hb + 1, :].broadcast_to([128, 4, 128]), op=ALU.mult), [t_LT, t_cbT], [t_MT])
                    x3 = xs[:, ti, :].rearrange("p (h d) -> p h d", h=8)
                    V(lambda e: e.tensor_tensor(out=xdt[:], in0=x3, in1=dtt[:, ti, d * 8:(d + 1) * 8].unsqueeze(2).broadcast_to([128, 8, 64]), op=ALU.mult), [t_xs, t_dt], [t_xd])
                    G(lambda e: e.tensor_tensor(out=xw[:], in0=xdt[:], in1=sc[:, 2, :].unsqueeze(2).broadcast_to([128, 8, 64]), op=ALU.mult), [t_xd, t_sc], [t_xd])

                def phaseB(k):
                    d, ti, first = steps[k]
                    sc = sc2[k % 2]; t_sc = t_sc2[k % 2]
                    MT = MT2[k % 2]; t_MT = t_MT2[k % 2]
                    xdt = xdt2[k % 2]; xw = xw2[k % 2]; t_xd = t_xd2[k % 2]
                    if first:
                        G(lambda e: e.memset(S[:], 0.0), [], [t_S])
                        G(lambda e: e.memset(Hb[:], 0.0), [], [t_Hb])
                    for h in range(8):
                        MM(ps[2][:, h * 64:(h + 1) * 64], MT[:, h, :], xdt[:, h, :], [t_MT, t_xd], [tp[2]], sig=(h == 7))
                    for g in range(2):
                        MM(ps[3][:, g * 256:(g + 1) * 256], CT[:, g, ti * 128:(ti + 1) * 128], Hb[:, g * 4:(g + 1) * 4, :].rearrange("p h d -> p (h d)"), [t_BC, t_Hb], [tp[3]], sig=(g == 1))
                    for g in range(2):
                        MM(ps[4][:, g * 256:(g + 1) * 256], Bt[:, ti, g * 128:(g + 1) * 128], xw[:, g * 4:(g + 1) * 4, :].rearrange("p h d -> p (h d)"), [t_Bt, t_xd], [tp[4]], sig=(g == 1))
                    V(lambda e: e.tensor_tensor(out=ytmp[:], in0=ps[3][:, 0:512].rearrange("p (h d) -> p h d", h=8), in1=sc[:, 3, :].unsqueeze(2).broadcast_to([128, 8, 64]), op=ALU.mult),
                      [tp[3], t_sc], [t_yt])
                    V(lambda e: e.tensor_tensor(out=S[:], in0=S[:], in1=sc[:, 4, :].unsqueeze(2).broadcast_to([128, 8, 64]), op=ALU.mult), [t_S, t_sc], [t_S])
                    V(lambda e: e.tensor_tensor(out=S[:], in0=S[:], in1=ps[4][:, 0:512].rearrange("p (h d) -> p h d", h=8), op=ALU.add), [t_S, tp[4]], [t_S])
                    A(lambda e: e.activation(out=Hb[:], in_=S[:], func=AF.Copy), [t_S], [t_Hb])
                    yv = ysum[:, ti, :]
                    if d == 0:
                        V(lambda e: e.tensor_tensor(out=yv, in0=ps[2][:, 0:512], in1=ytmp[:].rearrange("p h d -> p (h d)"), op=ALU.add), [tp[2], t_yt], [t_ys[ti]])
                    else:
                        V(lambda e: e.tensor_tensor(out=ytmp[:].rearrange("p h d -> p (h d)"), in0=ps[2][:, 0:512], in1=ytmp[:].rearrange("p h d -> p (h d)"), op=ALU.add), [tp[2], t_yt], [t_yt])
                        G(lambda e: e.tensor_tensor(out=yv, in0=yv, in1=ytmp[:].rearrange("p h d -> p (h d)"), op=ALU.add), [t_yt, t_ys[ti]], [t_ys[ti]])

                phaseA(0)
                for k in range(len(steps)):
                    if k + 1 < len(steps):
                        phaseA(k + 1)
                    phaseB(k)
                dsk = sm[:, 32:40]
                zf = sb(es, [128, 512]); tf = sb(es, [128, 512]); junk = sb(es, [128, 512], BF16); ss = sb(es, [128, 1]); t_f = Tok()
                yb = sb(es, [128, 4, 512], BF16); t_yb = Tok(); yTs = sb(es, [128, 4, 512], BF16); t_yTs = Tok()
                for ti in range(NT):
                    bk = 5 + ti % 2
                    proj_tok(ps[bk][:, 0:512], hT, t_hT, ti, wz, t_w, [tp[bk]])
                    A(lambda e, bk=bk: e.activation(out=zf[:], in_=ps[bk][:, 0:512], func=AF.Silu), [tp[bk]], [t_f])
                    x3 = xs[:, ti, :].rearrange("p (h d) -> p h d", h=8)
                    V(lambda e, x3=x3: e.tensor_tensor(out=tf[:].rearrange("p (h d) -> p h d", h=8), in0=x3, in1=dsk.unsqueeze(2).broadcast_to([128, 8, 64]), op=ALU.mult), [t_xs, t_sm], [t_f])
                    V(lambda e, ti=ti: e.tensor_tensor(out=tf[:], in0=tf[:], in1=ysum[:, ti, :], op=ALU.add), [t_f, t_ys[ti]], [t_f])
                    V(lambda e: e.tensor_tensor(out=tf[:], in0=tf[:], in1=zf[:], op=ALU.mult), [t_f], [t_f])
                    A(lambda e: e.activation(out=junk[:], in_=tf[:], func=AF.Square, accum_out=ss[:]), [t_f], [t_f])
                    rsqrt_mean(ss[:], 512, [t_f])
                    jslot = (ti % 4) if ti >= 2 else ti
                    jslot = ((ti - 2) % 4) if ti >= 2 else ti
                    V(lambda e, jslot=jslot: e.scalar_tensor_tensor(out=yb[:, jslot, :], in0=tf[:], scalar=ss[:, 0:1], in1=gn[:], op0=ALU.mult, op1=ALU.mult), [t_f, t_sm], [t_yb])
                    if ti == 1:
                        emit_yT(es, 3, yb, t_yb, 0, 256, yTs, t_yTs)
                    elif ti >= 2 and (ti - 2) % 4 == 3:
                        emit_yT(es, 3, yb, t_yb, (ti - 3) * 128, 512, yTs, t_yTs)
                fw.barrier()

        def stage_merge(l, b, hT, t_hT):
            with ExitStack() as es:
                gT = sb(es, [128, 8, NTOK], BF16); t_gT = [Tok() for _ in range(5)]
                blocks = [(i * 512, min(512, NTOK - i * 512)) for i in range(5)]
                with ExitStack() as e1:
                    wg = [sb(e1, [128, 8, 4, 128], BF16) for _ in range(2)]; t_wg = [Tok(), Tok()]
                    wb = [sb(e1, [128, 4, 4, 128], BF16) for _ in range(2)]; t_wb = [Tok(), Tok()]
                    yt = [sb(e1, [128, 4, 4, 512], BF16) for _ in range(2)]; t_yt = [Tok(), Tok()]
                    sg = [sb(e1, [128, 512]) for _ in range(2)]; t_sg = [Tok(), Tok()]
                    accg = sb(e1, [128, 512]); t_ag = Tok()
                    it = 0
                    for dc in range(8):
                        i2 = dc % 2
                        for k in range(4):
                            c0 = C_GATE + k * D + dc * 128
                            fw.dma("pool", wg[i2][:, :, k, :], w_in[l, :, c0:c0 + 128].rearrange("(k p) n -> p k n", p=128), writes=[t_wg[i2]])
                            fw.dma("pool", wb[i2][:, k, :, :], w_branch[l, k, :, dc * 128:(dc + 1) * 128].rearrange("(k p) n -> p k n", p=128), writes=[t_wb[i2]])
                        for bi, (t0, n) in enumerate(blocks):
                            y2 = it % 2
                            it += 1
                            rd = [tk(t_yT, (k, q0)) for k in range(4) for (q0, nq, _) in QBLOCKS if q0 < t0 + n and q0 + nq > t0]
                            for k in range(4):
                                fw.dma("sp", yt[y2][:, k, :, 0:n], yTd[k].rearrange("(c p) t -> p c t", p=128)[:, :, t0:t0 + n], reads=rd, writes=[t_yt[y2]])
                            for k in range(4):
                                pg = k % 2; pp = 2 + k % 2
                                for kc in range(8):
                                    MM(ps[pg][:, 0:n], wg[i2][:, kc, k, :], hT[:, kc, t0:t0 + n], [t_wg[i2]] + t_hT[t0 // 128:(t0 + n) // 128], [tp[pg]],
                                       start=(kc == 0), stop=(kc == 7), sig=(kc == 7))
                                for ec_ in range(4):
                                    MM(ps[pp][:, 0:n], wb[i2][:, k, ec_, :], yt[y2][:, k, ec_, 0:n], [t_wb[i2], t_yt[y2]], [tp[pp]],
                                       start=(ec_ == 0), stop=(ec_ == 3), sig=(ec_ == 3))
                                A(lambda e, pg=pg, k=k, n=n: e.activation(out=sg[k % 2][:, 0:n], in_=ps[pg][:, 0:n], func=AF.Sigmoid), [tp[pg]], [t_sg[k % 2]])
                                if k == 0:
                                    V(lambda e, pp=pp, n=n: e.tensor_tensor(out=accg[:, 0:n], in0=sg[0][:, 0:n], in1=ps[pp][:, 0:n], op=ALU.mult), [t_sg[0], tp[pp]], [t_ag])
                                else:
                                    V(lambda e, pp=pp, k=k, n=n: e.tensor_tensor(out=sg[k % 2][:, 0:n], in0=sg[k % 2][:, 0:n], in1=ps[pp][:, 0:n], op=ALU.mult), [t_sg[k % 2], tp[pp]], [t_sg[k % 2]])
                                    if k < 3:
                                        G(lambda e, k=k, n=n: e.tensor_tensor(out=accg[:, 0:n], in0=accg[:, 0:n], in1=sg[k % 2][:, 0:n], op=ALU.add), [t_ag, t_sg[k % 2]], [t_ag])
                                    else:
                                        G(lambda e, k=k, n=n, dc=dc, t0=t0: e.tensor_tensor(out=gT[:, dc, t0:t0 + n], in0=accg[:, 0:n], in1=sg[k % 2][:, 0:n], op=ALU.add),
                                          [t_ag, t_sg[k % 2]], [t_gT[bi]])
                    fw.barrier()
                with ExitStack() as e2:
                    wo = sb(e2, [128, 8, D], BF16); t_wo = Tok()
                    load_w(wo[:], w_out[l], t_wo)
                    tl, t_ab = {}, Tok()
                    for rname, row in (("b", b), ("c", 2)):
                        t = sb(e2, [128, D])
                        fw.dma("sp", t[:], combd[l, row, 2, :].partition_broadcast(128), reads=[t_comb[l]], writes=[t_ab])
                        tl[rname] = t
                    xb = [sb(e2, [128, D]) for _ in range(2)]; t_xb = [Tok(), Tok()]
                    junk = sb(e2, [128, D], BF16); ss = [sb(e2, [128, 1]) for _ in range(2)]; t_ss = [Tok(), Tok()]
                    t1 = [sb(e2, [128, D]) for _ in range(2)]; t_t1 = [Tok(), Tok()]
                    for ti in range(NT):
                        if l == 1 and ti < 2:
                            continue
                        i2 = ti % 2
                        src, rt = xsrc(l, b, ti)
                        fw.dma("sp", xb[i2][:], src, reads=rt, writes=[t_xb[i2]])
                        for half in range(2):
                            bk = 4 + half
                            for kc in range(8):
                                MM(ps[bk][:, 0:512], gT[:, kc, ti * 128:(ti + 1) * 128], wo[:, kc, half * 512:(half + 1) * 512], [t_gT[ti // 4], t_wo], [tp[bk]],
                                   start=(kc == 0), stop=(kc == 7), sig=(kc == 7))
                        A(lambda e, i2=i2: e.activation(out=junk[:, 0:512], in_=ps[4][:, 0:512], func=AF.Square, accum_out=ss[i2][:]), [tp[4]], [t_ss[i2]])
                        A(lambda e, i2=i2: e.activation(out=junk[:, 512:1024], in_=ps[5][:, 0:512], func=AF.Square, accum_out=t1[i2][:, 0:1]), [tp[5]], [t_t1[i2]])
                        V(lambda e, i2=i2: e.tensor_tensor(out=ss[i2][:], in0=ss[i2][:], in1=t1[i2][:, 0:1], op=ALU.add), [t_ss[i2], t_t1[i2]], [t_ss[i2]])
                        rsqrt_mean(ss[i2][:], D, [t_ss[i2]])
                        Gt = tl["c" if ti < 2 else "b"]
                        for half in range(2):
                            V(lambda e, i2=i2, half=half, Gt=Gt: e.scalar_tensor_tensor(out=t1[i2][:, half * 512:(half + 1) * 512], in0=ps[4 + half][:, 0:512], scalar=ss[i2][:, 0:1],
                                                                                       in1=Gt[:, half * 512:(half + 1) * 512], op0=ALU.mult, op1=ALU.mult),
                              [tp[4 + half], t_ss[i2], t_ab], [t_t1[i2]])
                        G(lambda e, i2=i2: e.tensor_tensor(out=t1[i2][:], in0=t1[i2][:], in1=xb[i2][:], op=ALU.add), [t_t1[i2], t_xb[i2]], [t_t1[i2]])
                        fw.dma("sp", xmid[b, ti * 128:(ti + 1) * 128, :], t1[i2][:], reads=[t_t1[i2]], writes=[tk(t_xmid, (b, ti))])
                    fw.barrier()

        def stage_ffn(l):
            with ExitStack() as es:
                w1 = sb(es, [128, 8, FFH], BF16); w3 = sb(es, [128, 8, FFH], BF16); w2 = sb(es, [128, 22, D], BF16); t_w = Tok()
                for c in range(0, FFH, 704):
                    load_w(w1[:, :, c:c + 704], ffn_w1[l, :, c:c + 704], t_w)
                    load_w(w3[:, :, c:c + 704], ffn_w3[l, :, c:c + 704], t_w)
                for c in range(0, 22, 11):
                    fw.dma("pool", w2[:, c:c + 11, :], ffn_w2[l, c * 128:(c + 11) * 128, :].rearrange("(k p) n -> p k n", p=128), writes=[t_w])
                xb = [sb(es, [128, D]) for _ in range(2)]; t_xb = [Tok(), Tok()]
                wk1 = (sb(es, [128, D], BF16), sb(es, [128, 1]), sb(es, [128, D]), sb(es, [128, D], BF16), Tok()); wks = [wk1, wk1]
                h2T = [sb(es, [128, 8, 128], BF16) for _ in range(2)]; t_h2T = [Tok(), Tok()]
                sl = [sb(es, [128, 512]) for _ in range(2)]; t_sl = [Tok(), Tok()]
                u = sb(es, [128, FFH], BF16); t_u = Tok()
                uT = sb(es, [128, 22, 128], BF16); t_uT = Tok()
                junk = wk1[0]; ss = sb(es, [128, 2]); t_ss = Tok()
                o11 = sb(es, [128, D]); t_o11 = Tok()
                tlc = [sb(es, [128, D]) for _ in range(3)]
                nchunks = [(c, min(512, FFH - c)) for c in range(0, FFH, 512)]
                for b in range(2):
                    tiles = [ti for ti in range(NT) if not (l == 1 and ti < 2)]
                    rowof = lambda ti: 2 if ti < 2 else b
                    state = {"row": None, "t_ab": None}

                    def E1(ti):
                        i2 = ti % 2
                        if state["row"] != rowof(ti):
                            state["row"] = rowof(ti)
                            state["t_ab"] = state["t_ab"] or Tok()
                            for j in range(3):
                                fw.dma("sp", tlc[j][:], combd[l, rowof(ti), 3 + j, :].partition_broadcast(128), reads=[t_comb[l]], writes=[state["t_ab"]])
                        fw.dma("sp", xb[i2][:], xmid[b, ti * 128:(ti + 1) * 128, :], reads=[tk(t_xmid, (b, ti))], writes=[t_xb[i2]])
                        norm_mod_T(es, xb[i2][:], t_xb[i2], tlc[0][:], tlc[1][:], state["t_ab"], h2T[i2][:], t_h2T[i2], wk1)

                    def E2(ti):
                        i2 = ti % 2
                        for ci, (c0, n) in enumerate(nchunks):
                            pa = 0 + 2 * (ci % 2); pb_ = 1 + 2 * (ci % 2)
                            for kc in range(8):
                                MM(ps[pa][:, 0:n], h2T[i2][:, kc, :], w1[:, kc, c0:c0 + n], [t_h2T[i2], t_w], [tp[pa]], start=(kc == 0), stop=(kc == 7), sig=(kc == 7))
                            for kc in range(8):
                                MM(ps[pb_][:, 0:n], h2T[i2][:, kc, :], w3[:, kc, c0:c0 + n], [t_h2T[i2], t_w], [tp[pb_]], start=(kc == 0), stop=(kc == 7), sig=(kc == 7))
                            A(lambda e, ci=ci, pa=pa, n=n: e.activation(out=sl[ci % 2][:, 0:n], in_=ps[pa][:, 0:n], func=AF.Silu), [tp[pa]], [t_sl[ci % 2]])
                            V(lambda e, ci=ci, pb_=pb_, n=n, c0=c0: e.tensor_tensor(out=u[:, c0:c0 + n], in0=sl[ci % 2][:, 0:n], in1=ps[pb_][:, 0:n], op=ALU.mult), [t_sl[ci % 2], tp[pb_]], [t_u])
                        for grp in range(3):
                            c_lo = grp * 8; c_hi = min(22, c_lo + 8)
                            for c in range(c_lo, c_hi):
                                TR(psb[7][:, (c - c_lo) * 128:(c - c_lo + 1) * 128], u[:, c * 128:(c + 1) * 128], ident[:], [t_u, t_c], [tp[7]], sig=(c == c_hi - 1))
                            V(lambda e, c_lo=c_lo, c_hi=c_hi: e.tensor_copy(out=uT[:, c_lo:c_hi, :], in_=psb[7][:, 0:(c_hi - c_lo) * 128].rearrange("p (c t) -> p c t", t=128)), [tp[7]], [t_uT])

                    def E3(ti):
                        i2 = ti % 2
                        t_ab = state["t_ab"]
                        for half in range(2):
                            bk = 4 + half
                            for c in range(22):
                                MM(ps[bk][:, 0:512], uT[:, c, :], w2[:, c, half * 512:(half + 1) * 512], [t_uT, t_w], [tp[bk]], start=(c == 0), stop=(c == 21), sig=(c == 21))
                        A(lambda e: e.activation(out=junk[:, 0:512], in_=ps[4][:, 0:512], func=AF.Square, accum_out=ss[:, 0:1]), [tp[4]], [t_ss])
                        A(lambda e: e.activation(out=junk[:, 512:1024], in_=ps[5][:, 0:512], func=AF.Square, accum_out=ss[:, 1:2]), [tp[5]], [t_ss])
                        V(lambda e: e.tensor_tensor(out=ss[:, 0:1], in0=ss[:, 0:1], in1=ss[:, 1:2], op=ALU.add), [t_ss], [t_ss])
                        rsqrt_mean(ss[:, 0:1], D, [t_ss])
                        for half in range(2):
                            V(lambda e, half=half: e.scalar_tensor_tensor(out=o11[:, half * 512:(half + 1) * 512], in0=ps[4 + half][:, 0:512], scalar=ss[:, 0:1],
                                                                          in1=tlc[2][:, half * 512:(half + 1) * 512], op0=ALU.mult, op1=ALU.mult),
                              [tp[4 + half], t_ss, t_ab], [t_o11])
                        G(lambda e: e.tensor_tensor(out=o11[:], in0=o11[:], in1=xb[i2][:], op=ALU.add), [t_o11, t_xb[i2]], [t_o11])
                        if l == 0:
                            fw.dma("sp", xres[b, ti * 128:(ti + 1) * 128, :], o11[:], reads=[t_o11], writes=[tk(t_xres, (b, ti))])
                        else:
                            fw.dma("sp", out[b, (ti - 2) * 128:(ti - 1) * 128, :], o11[:], reads=[t_o11])

                    E1(tiles[0])
                    for idx, ti in enumerate(tiles):
                        nxt = tiles[idx + 1] if idx + 1 < len(tiles) else None
                        pre = nxt is not None and rowof(nxt) == rowof(ti)
                        if pre:
                            E1(nxt)
                        E2(ti)
                        E3(ti)
                        if nxt is not None and not pre:
                            E1(nxt)
                    fw.barrier()
                fw.barrier()

        for l in layers:
            if "mod" in ST:
                stage_mod(l)
            for b in range(2):
                with ExitStack() as eh:
                    hT = sb(eh, [128, 8, NTOK], BF16, "hT")
                    t_hT = [Tok(f"hT{ti}") for ti in range(NT)]
                    if "norm1" in ST:
                        stage_norm1(l, b, hT, t_hT)
                    if "gqa" in ST:
                        stage_gqa(l, b, hT, t_hT)
                    if "mla" in ST:
                        stage_mla(l, b, hT, t_hT)
                    if "na" in ST:
                        stage_na(l, b, hT, t_hT)
                    if "ssm" in ST:
                        stage_ssm(l, b, hT, t_hT)
                    if "merge" in ST:
                        stage_merge(l, b, hT, t_hT)
                    fw.barrier()
            if "ffn" in ST:
                stage_ffn(l)
        fw.barrier()
        fw.replay()
        print("instructions:", fw.n_instr)
    return nc


def _host_consts():
    i = np.arange(128)
    c = np.zeros((128, 14, 128), np.float32)
    c[:, 0, :] = np.eye(128)
    trif = (i[:, None] <= i[None, :]).astype(np.float32)
    trib = (i[:, None] >= i[None, :]).astype(np.float32)
    c[:, 1, :] = trif
    c[:, 2, :] = trib
    for a in range(4):
        c[:, 3 + a, :] = np.where(i[:, None] <= i[None, :], 0.0, NEG)
        c[:, 7 + a, :] = np.where(i[:, None] >= i[None, :], 0.0, NEG)
    c[:, 11, :] = 1.0
    c[:, 12, :] = -trif
    c[:, 13, :] = -trib
    return c


def _rope_table(n, dim):
    t = np.arange(n)
    row = (t // 64).astype(np.float32)
    col = (t % 64).astype(np.float32)
    quarter = dim // 4
    inv = (np.float32(10000.0) ** (-np.arange(quarter, dtype=np.float32) / np.float32(quarter))).astype(np.float32)
    ang = np.concatenate([row[:, None] * inv, col[:, None] * inv], axis=-1).astype(np.float32)
    cs = np.stack([np.cos(ang), np.sin(ang)], axis=0).astype(np.float32)
    return np.ascontiguousarray(cs.reshape(2, 16, 128, dim // 2).transpose(2, 0, 1, 3))


def _na_bias_table(rpb):
    L = rpb.shape[0]
    tab = np.full((L, 5, 128, 8, 5, 128), NEG, np.float32)
    k = np.arange(128)
    q = np.arange(128)
    for cls, t in enumerate((5, 0, 1, 14, 15)):
        _, chunks = na_chunks(t)
        r = 2 * t + q // 64
        qc = q % 64
        rs = np.clip(r - 4, 0, 24)
        cs = np.clip(qc - 8, 0, 48)
        for slot, c in enumerate(chunks):
            kr = 2 * c + k // 64
            kc = k % 64
            ok = ((kr[:, None] >= rs[None, :]) & (kr[:, None] <= rs[None, :] + 7) &
                  (kc[:, None] >= cs[None, :]) & (kc[:, None] <= cs[None, :] + 15))
            ri = np.clip(kr[:, None] - r[None, :] + 7, 0, 14)
            ci = np.clip(kc[:, None] - qc[None, :] + 15, 0, 30)
            g = rpb[:, :, ri, ci]
            g = np.where(ok[None, None], g, np.float32(NEG))
            tab[:, cls, :, :, slot, :] = g.transpose(0, 2, 1, 3)
    return tab


_PROG = {}


def _get_prog(debug=False, layers=(0, 1)):
    key = (debug, tuple(layers))
    if key not in _PROG:
        _PROG[key] = build_program(debug=debug, layers=layers)
    return _PROG[key]


def make_in_maps(inputs):
    f = lambda a: np.ascontiguousarray(np.asarray(a, dtype=np.float32))
    shared = {
        "w_ada": f(inputs["w_ada"]), "b_ada": f(inputs["b_ada"]),
        "g4": f(np.stack([inputs["g_pre1"], inputs["g_post1"], inputs["g_pre2"], inputs["g_post2"]], axis=1)),
        "w_in": f(inputs["w_in"]), "nab": _na_bias_table(f(inputs["na_rpb"])),
        "mla_g_q": f(inputs["mla_g_q"]), "mla_g_kv": f(inputs["mla_g_kv"]),
        "mla_w_uq": f(inputs["mla_w_uq"]), "mla_w_ukv": f(inputs["mla_w_ukv"]),
        "gqa_g": f(np.stack([inputs["gqa_g_q"], inputs["gqa_g_k"]], axis=1)),
        "conv_wT": f(np.transpose(np.asarray(inputs["ssm_conv_w"]), (0, 2, 1))),
        "conv_b": f(inputs["ssm_conv_b"]),
        "ssm_small": f(np.concatenate([np.asarray(inputs["ssm_a_log"]).reshape(2, 16), np.asarray(inputs["ssm_dt_bias"]).reshape(2, 16),
                                       np.asarray(inputs["ssm_d"]).reshape(2, 8)], axis=1)),
        "ssm_g_norm": f(inputs["ssm_g_norm"]),
        "w_branch": f(inputs["w_branch"]), "w_out": f(inputs["w_out"]),
        "ffn_w1": f(inputs["ffn_w1"]), "ffn_w3": f(inputs["ffn_w3"]), "ffn_w2": f(inputs["ffn_w2"]),
        "consts": _host_consts(), "ropeg": _rope_table(2048, 64), "ropem": _rope_table(2048, 32),
    }
    x = f(inputs["x"]); c = f(inputs["c"]); ctx = f(inputs["ctx"]); c_ctx = f(inputs["c_ctx"])
    maps = []
    for i in range(8):
        m = dict(shared)
        m["x"] = x[2 * i:2 * i + 2]
        m["ctx"] = ctx[2 * i:2 * i + 2]
        m["cvec"] = np.ascontiguousarray(np.concatenate([c[2 * i:2 * i + 2], c_ctx[None, :]], axis=0))
        maps.append(m)
    return maps


def kernel(**inputs):
    nc = _get_prog()
    maps = make_in_maps(inputs)
    res = run_bass_kernel_spmd(nc, maps, core_ids=list(range(8)))
    return np.concatenate([np.asarray(r["out"], dtype=np.float32) for r in res.results], axis=0)
```

```python
from contextlib import ExitStack
import numpy as np
from concourse.bass_utils import run_bass_kernel_spmd
import numpy as np
import concourse.bass as bass
import concourse.mybir as mybir
F32 = mybir.dt.float32
BF16 = mybir.dt.bfloat16
AF = mybir.ActivationFunctionType
ALU = mybir.AluOpType
AX = mybir.AxisListType

class Tok:
    __slots__ = ("w", "r", "name", "excl")
    def __init__(self, name="", excl=False):
        self.w = {}
        self.r = {}
        self.name = name
        self.excl = excl

class Q:
    def __init__(self, fw, name, attr, sem):
        self.fw = fw; self.name = name; self.attr = attr; self.sem = sem
        self.key = name
        self.count = 0
        self.seen = {}
        self.ops = []
        self.pending = False
        self.dsems = []
        self.dtarget = {}
        self.dnext = 0

class FW:
    def __init__(self, nc, stack, n_dma_sems=6):
        self.nc = nc
        self.q = {}
        self.semh = {}
        for name, attr in (("pe", "tensor"), ("act", "scalar"), ("dve", "vector"), ("pool", "gpsimd"), ("sp", "sync")):
            s = stack.enter_context(nc.semaphore("s_" + name))
            self.q[name] = Q(self, name, attr, s)
            self.semh[name] = s
        for qn in ("sp", "pool"):
            q = self.q[qn]
            for i in range(n_dma_sems):
                key = f"d_{qn}{i}"
                s = stack.enter_context(nc.semaphore(key))
                self.semh[key] = s
                q.dsems.append(key)
                q.dtarget[key] = 0
        self.n_instr = 0

    def _wait(self, q, key, val):
        if q.seen.get(key, 0) < val:
            q.ops.append(("w", key, val))
            q.seen[key] = val

    def _deps(self, q, reads, writes):
        for t in reads:
            for k, v in t.w.items():
                self._dep1(q, k, v)
            if t.excl:
                for k, v in t.r.items():
                    if k != q.key:
                        self._dep1(q, k, v)
        for t in writes:
            for k, v in t.w.items():
                self._dep1(q, k, v)
            for k, v in t.r.items():
                self._dep1(q, k, v)

    def _dep1(self, q, k, v):
        if k == q.key and v > q.count:
            return
        self._wait(q, k, v)

    def op(self, qn, fn, reads=(), writes=(), signal=True):
        q = self.q[qn]
        self._deps(q, reads, writes)
        if signal:
            q.count += 1
            q.ops.append(("i", fn, True))
            q.pending = False
        else:
            q.ops.append(("i", fn, False))
            q.pending = True
        ev = (q.key, q.count if signal else q.count + 1)
        self._mark(ev, reads, writes)
        self.n_instr += 1

    def _mark(self, ev, reads, writes):
        k, v = ev
        for t in writes:
            t.w = {k: v}
            t.r = {}
        for t in reads:
            if t.r.get(k, 0) < v:
                t.r[k] = v

    def dma(self, qn, out, in_, reads=(), writes=(), **kw):
        q = self.q[qn]
        self._deps(q, reads, writes)
        key = q.dsems[q.dnext]
        q.dnext = (q.dnext + 1) % len(q.dsems)
        self._wait(q, key, q.dtarget[key])
        q.dtarget[key] += 16
        q.ops.append(("d", out, in_, key, kw))
        self._mark((key, q.dtarget[key]), reads, writes)
        self.n_instr += 1

    def barrier(self):
        cur = {}
        for q in self.q.values():
            assert not q.pending, q.name
            cur[q.key] = q.count
            for k in q.dsems:
                cur[k] = q.dtarget[k]
        for q in self.q.values():
            for k, v in cur.items():
                if k == q.key:
                    continue
                if v > 0:
                    self._wait(q, k, v)

    def replay(self):
        nc = self.nc
        fwself = self
        with nc.Block() as block:
            def mk(q):
                def body(eng):
                    for o in q.ops:
                        if o[0] == "w":
                            eng.wait_ge(fwself.semh[o[1]], o[2])
                        elif o[0] == "i":
                            ins = o[1](eng)
                            if o[2]:
                                ins.then_inc(q.sem, 1)
                        else:
                            eng.dma_start(out=o[1], in_=o[2], **o[4]).then_inc(fwself.semh[o[3]], 16)
                return body
            block.tensor(mk(self.q["pe"]))
            block.scalar(mk(self.q["act"]))
            block.vector(mk(self.q["dve"]))
            block.gpsimd(mk(self.q["pool"]))
            block.sync(mk(self.q["sp"]))

D = 1024
NTOK = 2304
NT = 18
EPS = 1e-6
C_NAK, C_NAV, C_CKV, C_KR, C_GK, C_GV, C_SX, C_SB, C_SDT = 0, 512, 1024, 1280, 1312, 1440, 1568, 2080, 2336
C_NAQ, C_CQ, C_GQ, C_SC, C_SZ, C_GATE = 2352, 2864, 3248, 3760, 4016, 4528
FFH = 2816
NEG = -30000.0


def na_chunks(t):
    if t <= 1:
        return 1 + t, [0, 1, 2, 3]
    if t >= 14:
        return 3 + (t - 14), [12, 13, 14, 15]
    return 0, [t - 2, t - 1, t, t + 1, t + 2]


def build_program(debug=False, layers=(0, 1), stages=("mod", "norm1", "gqa", "mla", "na", "ssm", "merge", "ffn")):
    nc = bass.Bass("TRN2", target_bir_lowering=False)
    dt_in = lambda name, shape, dt=F32: nc.dram_tensor(name, list(shape), dt, kind="ExternalInput").ap()
    kind_scr = "ExternalOutput" if debug else "Internal"
    dt_scr = lambda name, shape, dt=F32: nc.dram_tensor(name, list(shape), dt, kind=kind_scr).ap()
    x_in = dt_in("x", [2, 2048, D])
    ctx_in = dt_in("ctx", [2, 256, D])
    cvec = dt_in("cvec", [3, D])
    w_ada = dt_in("w_ada", [2, D, 6 * D])
    b_ada = dt_in("b_ada", [2, 6 * D])
    g4 = dt_in("g4", [2, 4, D])
    w_in = dt_in("w_in", [2, D, 8624])
    nab = dt_in("nab", [2, 5, 128, 8, 5, 128])
    mla_g_q = dt_in("mla_g_q", [2, 384])
    mla_g_kv = dt_in("mla_g_kv", [2, 256])
    mla_w_uq = dt_in("mla_w_uq", [2, 384, 768])
    mla_w_ukv = dt_in("mla_w_ukv", [2, 256, 1024])
    gqa_g = dt_in("gqa_g", [2, 2, 64])
    conv_wT = dt_in("conv_wT", [2, D, 5])
    conv_b = dt_in("conv_b", [2, D])
    ssm_small = dt_in("ssm_small", [2, 40])
    ssm_g_norm = dt_in("ssm_g_norm", [2, 512])
    w_branch = dt_in("w_branch", [2, 4, 512, D])
    w_out = dt_in("w_out", [2, D, D])
    ffn_w1 = dt_in("ffn_w1", [2, D, FFH])
    ffn_w3 = dt_in("ffn_w3", [2, D, FFH])
    ffn_w2 = dt_in("ffn_w2", [2, FFH, D])
    consts = dt_in("consts", [128, 14, 128])
    ropeg = dt_in("ropeg", [128, 2, 16, 32])
    ropem = dt_in("ropem", [128, 2, 16, 16])
    out = nc.dram_tensor("out", [2, 2048, D], F32, kind="ExternalOutput").ap()
    combd = dt_scr("combd", [2, 3, 6, D])
    yTd = dt_scr("yTd", [4, 512, NTOK], BF16)
    xmid = dt_scr("xmid", [2, NTOK, D])
    xres = dt_scr("xres", [2, NTOK, D])

    ST = stages
    with ExitStack() as top:
        fw = FW(nc, top)
        uid = [0]

        def sb(es, shape, dt=F32, name="t"):
            uid[0] += 1
            t = es.enter_context(nc.sbuf_tensor(f"{name}{uid[0]}", list(shape), dt))
            return t

        ps = [top.enter_context(nc.psum_tensor(f"ps{i}", [128, 512], F32)) for i in range(8)]
        tp = [Tok(f"ps{i}", excl=True) for i in range(8)]
        psb = [p[:].bitcast(BF16) for p in ps]
        cst = sb(top, [128, 14, 128], F32, "cst")
        ident = sb(top, [128, 128], BF16, "ident")
        rg = sb(top, [128, 2, 16, 32], F32, "rg")
        rm = sb(top, [128, 2, 16, 16], F32, "rm")
        t_c = Tok("consts")
        fw.dma("sp", cst[:], consts, writes=[t_c])
        fw.dma("sp", rg[:], ropeg, writes=[t_c])
        fw.dma("sp", rm[:], ropem, writes=[t_c])
        fw.op("dve", lambda e: e.tensor_copy(out=ident[:], in_=cst[:, 0, :]), [t_c], [t_c])
        identf = cst[:, 0, :]
        tri = [cst[:, 1, :], cst[:, 2, :]]
        mneg4 = [cst[:, 3:7, :].rearrange("p a q -> p (a q)"), cst[:, 7:11, :].rearrange("p a q -> p (a q)")]
        onesf = cst[:, 11, :]
        ntri = [cst[:, 12, :], cst[:, 13, :]]
        t_comb = [Tok(f"comb{l}") for l in range(2)]
        t_yT = {}
        t_xmid = {}
        t_xres = {}

        def tk(dct, key):
            if key not in dct:
                dct[key] = Tok(str(key))
            return dct[key]

        def V(fn, r, w):
            fw.op("dve", fn, r, w)

        def A(fn, r, w):
            fw.op("act", fn, r, w)

        def G(fn, r, w):
            fw.op("pool", fn, r, w)

        def MM(o, lhsT, rhs, r, w, start=True, stop=True, sig=True):
            fw.op("pe", lambda e: e.matmul(o, lhsT=lhsT, rhs=rhs, start=start, stop=stop), r, w, signal=sig)

        def TR(o, in_, idn, r, w, sig=True):
            fw.op("pe", lambda e: e.transpose(out=o, in_=in_, identity=idn), r, w, signal=sig)

        def load_w(dst, src, tok, q="pool"):
            fw.dma(q, dst, src.rearrange("(k p) n -> p k n", p=128), writes=[tok])

        def bcast_load(dst, src_row, tok, parts=128):
            fw.dma("sp", dst, src_row.partition_broadcast(parts), writes=[tok])

        def rsqrt_mean(ap, n, r_w):
            A(lambda e: e.activation(out=ap, in_=ap, func=AF.Ln, scale=1.0 / n, bias=EPS), r_w, r_w)
            A(lambda e: e.activation(out=ap, in_=ap, func=AF.Exp, scale=-0.5), r_w, r_w)

        def xsrc(l, b, ti):
            if l == 0:
                if ti < 2:
                    return ctx_in[b, ti * 128:(ti + 1) * 128, :], []
                return x_in[b, (ti - 2) * 128:(ti - 1) * 128, :], []
            return xres[b, ti * 128:(ti + 1) * 128, :], [tk(t_xres, (b, ti))]

        def rope(dst3, src3, cos, sin, H, half, tmp, toks_r, tok_tmp, tok_dst):
            cb = cos.unsqueeze(1).broadcast_to([128, H, half])
            sn = sin.unsqueeze(1).broadcast_to([128, H, half])
            x1 = src3[:, :, 0:half]
            x2 = src3[:, :, half:2 * half]
            V(lambda e: e.tensor_tensor(out=tmp[:, 0], in0=x1, in1=cb, op=ALU.mult), toks_r, [tok_tmp])
            V(lambda e: e.tensor_tensor(out=tmp[:, 1], in0=x2, in1=sn, op=ALU.mult), toks_r, [tok_tmp])
            G(lambda e: e.tensor_tensor(out=tmp[:, 2], in0=x1, in1=sn, op=ALU.mult), toks_r, [tok_tmp])
            G(lambda e: e.tensor_tensor(out=tmp[:, 3], in0=x2, in1=cb, op=ALU.mult), toks_r, [tok_tmp])
            V(lambda e: e.tensor_tensor(out=dst3[:, :, 0:half], in0=tmp[:, 0], in1=tmp[:, 1], op=ALU.subtract), [tok_tmp], [tok_dst])
            V(lambda e: e.tensor_tensor(out=dst3[:, :, half:2 * half], in0=tmp[:, 2], in1=tmp[:, 3], op=ALU.add), [tok_tmp], [tok_dst])

        def stage_mod(l):
            with ExitStack() as es:
                cin = sb(es, [3, D]); t_cin = Tok()
                cs = sb(es, [3, D], BF16)
                csT = sb(es, [128, 8, 4], BF16); t_csT = Tok()
                modr = sb(es, [3, 6 * D]); t_mod = Tok()
                bad = sb(es, [3, 6 * D]); t_bad = Tok()
                g4t = sb(es, [3, 4, D]); t_g4 = Tok()
                comb = sb(es, [3, 6, D]); t_cb = Tok()
                wts = [sb(es, [128, 8, 512], BF16) for _ in range(2)]
                t_w = [Tok(), Tok()]
                fw.dma("sp", cin[:], cvec, writes=[t_cin])
                bcast_load(bad[:], b_ada[l, :], t_bad, 3)
                for j in range(4):
                    bcast_load(g4t[:, j, :], g4[l, j, :], t_g4, 3)
                A(lambda e: e.activation(out=cs[:], in_=cin[:], func=AF.Silu), [t_cin], [t_cin])
                for kc in range(8):
                    TR(psb[7][:, kc * 4:kc * 4 + 3], cs[:, kc * 128:(kc + 1) * 128], ident[0:3, 0:3], [t_cin, t_c], [tp[7]], sig=(kc == 7))
                V(lambda e: e.tensor_copy(out=csT[:, :, 0:3], in_=psb[7][:, 0:32].rearrange("p (k f) -> p k f", f=4)[:, :, 0:3]), [tp[7]], [t_csT])
                for n in range(12):
                    wt = wts[n % 2]
                    load_w(wt[:], w_ada[l, :, n * 512:(n + 1) * 512], t_w[n % 2])
                    pt = ps[n % 2]
                    for kc in range(8):
                        MM(pt[0:3, :], csT[:, kc, 0:3], wt[:, kc, :], [t_csT, t_w[n % 2]], [tp[n % 2]], start=(kc == 0), stop=(kc == 7), sig=(kc == 7))
                    V(lambda e, n=n, pt=pt: e.tensor_tensor(out=modr[:, n * 512:(n + 1) * 512], in0=pt[0:3, :], in1=bad[:, n * 512:(n + 1) * 512], op=ALU.add),
                      [tp[n % 2], t_bad], [t_mod])
                m = lambda j: modr[:, j * D:(j + 1) * D]
                V(lambda e: e.scalar_tensor_tensor(out=comb[:, 0, :], in0=m(1), scalar=1.0, in1=g4t[:, 0, :], op0=ALU.add, op1=ALU.mult), [t_mod, t_g4], [t_cb])
                V(lambda e: e.tensor_copy(out=comb[:, 1, :], in_=m(0)), [t_mod], [t_cb])
                V(lambda e: e.tensor_tensor(out=comb[:, 2, :], in0=m(2), in1=g4t[:, 1, :], op=ALU.mult), [t_mod, t_g4], [t_cb])
                V(lambda e: e.scalar_tensor_tensor(out=comb[:, 3, :], in0=m(4), scalar=1.0, in1=g4t[:, 2, :], op0=ALU.add, op1=ALU.mult), [t_mod, t_g4], [t_cb])
                V(lambda e: e.tensor_copy(out=comb[:, 4, :], in_=m(3)), [t_mod], [t_cb])
                V(lambda e: e.tensor_tensor(out=comb[:, 5, :], in0=m(5), in1=g4t[:, 3, :], op=ALU.mult), [t_mod, t_g4], [t_cb])
                fw.dma("sp", combd[l], comb[:], reads=[t_cb], writes=[t_comb[l]])
                fw.barrier()

        def norm_mod_T(es, xt, t_x, Ab, Bb, t_ab, hdst, t_hdst, wk):
            junk, ss, t1, hb, t_wk = wk
            A(lambda e: e.activation(out=junk[:], in_=xt, func=AF.Square, accum_out=ss[:]), [t_x], [t_wk])
            rsqrt_mean(ss[:], D, [t_wk])
            V(lambda e: e.scalar_tensor_tensor(out=t1[:], in0=xt, scalar=ss[:, 0:1], in1=Ab, op0=ALU.mult, op1=ALU.mult), [t_x, t_wk, t_ab], [t_wk])
            V(lambda e: e.tensor_tensor(out=hb[:], in0=t1[:], in1=Bb, op=ALU.add), [t_wk, t_ab], [t_wk])
            for kc in range(8):
                TR(psb[7][:, kc * 128:(kc + 1) * 128], hb[:, kc * 128:(kc + 1) * 128], ident[:], [t_wk, t_c], [tp[7]], sig=(kc == 7))
            A(lambda e: e.activation(out=hdst, in_=psb[7].rearrange("p (k t) -> p k t", k=8), func=AF.Copy), [tp[7]], [t_hdst])

        def load_comb(es, l, b, j0):
            tl = {}
            t_ab = Tok()
            for rname, row in (("b", b), ("c", 2)):
                for j in range(j0, j0 + 3):
                    t = sb(es, [128, D])
                    fw.dma("sp", t[:], combd[l, row, j, :].partition_broadcast(128), reads=[t_comb[l]], writes=[t_ab])
                    tl[(rname, j)] = t
            return tl, t_ab

        def stage_norm1(l, b, hT, t_hT):
            with ExitStack() as es:
                tl, t_ab = load_comb(es, l, b, 0)
                xb = [sb(es, [128, D]) for _ in range(2)]
                t_xb = [Tok(), Tok()]
                wks = []
                for i in range(2):
                    wks.append((sb(es, [128, D], BF16), sb(es, [128, 1]), sb(es, [128, D]), sb(es, [128, D], BF16), Tok()))
                for ti in range(NT):
                    src, rt = xsrc(l, b, ti)
                    fw.dma("sp", xb[ti % 2][:], src, reads=rt, writes=[t_xb[ti % 2]])
                    rn = "c" if ti < 2 else "b"
                    norm_mod_T(es, xb[ti % 2][:], t_xb[ti % 2], tl[(rn, 0)][:], tl[(rn, 1)][:], t_ab,
                               hT[:, :, ti * 128:(ti + 1) * 128], t_hT[ti], wks[ti % 2])
                fw.barrier()

        def proj_tok(pt, hT, t_hT, ti, wt, t_w, tps):
            for kc in range(8):
                MM(pt, hT[:, kc, ti * 128:(ti + 1) * 128], wt[:, kc, :], [t_hT[ti], t_w], tps, start=(kc == 0), stop=(kc == 7), sig=(kc == 7))

        def emit_yT(es, k, yb, t_yb, tok0, nq, yTs, t_yTs):
            for j in range(nq // 128):
                bk = 7
                for ec in range(4):
                    TR(psb[bk][:, ec * 128:(ec + 1) * 128], yb[:, j, ec * 128:(ec + 1) * 128], ident[:], [t_yb, t_c], [tp[bk]], sig=(ec == 3))
                V(lambda e, j=j, bk=bk: e.tensor_copy(out=yTs[:, :, j * 128:(j + 1) * 128], in_=psb[bk][:, 0:512].rearrange("p (c t) -> p c t", c=4)),
                  [tp[bk]], [t_yTs])
            fw.dma("sp", yTd[k].rearrange("(c p) t -> p c t", p=128)[:, :, tok0:tok0 + nq], yTs[:, :, 0:nq], reads=[t_yTs],
                   writes=[tk(t_yT, (k, tok0))])

        def attend(QT_fn, KT_fn, V_fn, t_q, t_kv, scale, nq, kchunks, yb, t_yb, pb, t_pb, rc, t_rc):
            nj = nq // 128
            last = len(kchunks) - 1
            items = [(h, ci, kc) for h in range(8) for ci, kc in enumerate(kchunks)]

            SBK = (0, 1, 6)

            def emitS(idx):
                h, ci, kc = items[idx]
                sbk = SBK[idx % 3]
                MM(ps[sbk][:, 0:nq], KT_fn(h, kc), QT_fn(h), [t_q, t_kv], [tp[sbk]])

            def emitRest(idx):
                h, ci, kc = items[idx]
                sbk = SBK[idx % 3]
                P = pb[idx % 3]
                tP = t_pb[idx % 3]
                A(lambda e: e.activation(out=P[:, 0:nq], in_=ps[sbk][:, 0:nq], func=AF.Exp, scale=scale), [tp[sbk]], [tP])
                for j in range(nj):
                    MM(ps[2 + j][:, 0:65], P[:, j * 128:(j + 1) * 128], V_fn(h, kc), [tP, t_kv], [tp[2 + j]],
                       start=(ci == 0), stop=(ci == last), sig=(ci == last))
                if ci == last:
                    for j in range(nj):
                        r = rc[j % 4]
                        tr_ = t_rc[j % 4]
                        V(lambda e, j=j, r=r: e.reciprocal(out=r[:], in_=ps[2 + j][:, 64:65]), [tp[2 + j]], [tr_])
                        V(lambda e, j=j, r=r: e.tensor_scalar(out=yb[:, j, h * 64:(h + 1) * 64], in0=ps[2 + j][:, 0:64], scalar1=r[:, 0:1], scalar2=None, op0=ALU.mult),
                          [tp[2 + j], tr_], [t_yb])

            emitS(0)
            if len(items) > 1:
                emitS(1)
            for idx in range(len(items)):
                if idx + 2 < len(items):
                    emitS(idx + 2)
                emitRest(idx)

        QBLOCKS = [(0, 256, [0, 1])] + [(256 + 512 * j, 512, list(range(NT))) for j in range(4)]

        def attn_work(es):
            yb = sb(es, [128, 4, 512], BF16); t_yb = Tok()
            yTs = sb(es, [128, 4, 512], BF16); t_yTs = Tok()
            pb = [sb(es, [128, 512], BF16) for _ in range(3)]
            t_pb = [Tok() for _ in range(3)]
            rc = [sb(es, [128, 1]) for _ in range(4)]
            t_rc = [Tok() for _ in range(4)]
            return yb, t_yb, yTs, t_yTs, pb, t_pb, rc, t_rc

        def stage_gqa(l, b, hT, t_hT):
            with ExitStack() as es:
                wq = sb(es, [128, 8, 512], BF16); wkv = sb(es, [128, 8, 256], BF16); t_w = Tok()
                load_w(wq[:], w_in[l, :, C_GQ:C_GQ + 512], t_w)
                load_w(wkv[:], w_in[l, :, C_GK:C_GK + 256], t_w)
                gq = sb(es, [128, 64]); gk = sb(es, [128, 64]); t_g = Tok()
                bcast_load(gq[:], gqa_g[l, 0, :], t_g)
                bcast_load(gk[:], gqa_g[l, 1, :], t_g)
                KT = sb(es, [128, 2, NTOK], BF16); Vg = sb(es, [128, NT, 2, 80], BF16); t_kv = Tok()
                G(lambda e: e.memset(Vg[:, :, :, 64:65], 1.0), [], [t_kv])
                qf = sb(es, [128, 512]); sq = sb(es, [128, 512]); ssq = sb(es, [128, 8]); qn = sb(es, [128, 512]); t_qw = Tok()
                tmp = sb(es, [128, 4, 8, 32]); t_tmp = Tok()
                qb = sb(es, [128, 512], BF16); t_qb = Tok()
                kd = sb(es, [128, 2, 2, 64], BF16); t_kd = Tok()
                QTb = sb(es, [128, 4, 512], BF16); t_q = Tok()
                work = attn_work(es)

                def normrope(src_ps, tps, H, gt, ti, dst3, t_dst):
                    n = H * 64
                    A(lambda e: e.activation(out=qf[:, 0:n], in_=src_ps, func=AF.Copy), tps, [t_qw])
                    V(lambda e: e.tensor_tensor(out=sq[:, 0:n], in0=qf[:, 0:n], in1=qf[:, 0:n], op=ALU.mult), [t_qw], [t_qw])
                    V(lambda e: e.tensor_reduce(out=ssq[:, 0:H], in_=sq[:, 0:n].rearrange("p (h d) -> p h d", h=H), axis=AX.X, op=ALU.add), [t_qw], [t_qw])
                    rsqrt_mean(ssq[:, 0:H], 64, [t_qw])
                    q3 = qf[:, 0:n].rearrange("p (h d) -> p h d", h=H)
                    n3 = qn[:, 0:n].rearrange("p (h d) -> p h d", h=H)
                    V(lambda e: e.tensor_tensor(out=n3, in0=q3, in1=ssq[:, 0:H].unsqueeze(2).broadcast_to([128, H, 64]), op=ALU.mult), [t_qw], [t_qw])
                    if ti >= 2 and 'noRope' not in ST:
                        V(lambda e: e.tensor_tensor(out=n3, in0=n3, in1=gt[:].unsqueeze(1).broadcast_to([128, H, 64]), op=ALU.mult), [t_qw, t_g], [t_qw])
                        rope(dst3, n3, rg[:, 0, ti - 2, :], rg[:, 1, ti - 2, :], H, 32, tmp[:, :, 0:H, :], [t_qw, t_c], t_tmp, t_dst)
                    else:
                        V(lambda e: e.tensor_tensor(out=dst3, in0=n3, in1=gt[:].unsqueeze(1).broadcast_to([128, H, 64]), op=ALU.mult), [t_qw, t_g], [t_dst])

                for ti in range(NT):
                    bk = 6
                    proj_tok(ps[bk][:, 0:256], hT, t_hT, ti, wkv, t_w, [tp[bk]])
                    V(lambda e, ti=ti, bk=bk: e.tensor_copy(out=Vg[:, ti, :, 0:64], in_=ps[bk][:, 128:256].rearrange("p (g d) -> p g d", g=2)), [tp[bk]], [t_kv])
                    if 'gqaK0' in ST:
                        continue
                    normrope(ps[bk][:, 0:128], [tp[bk]], 2, gk, ti, kd[:, :, 0, :], t_kd)
                    if 'gqaK1' in ST:
                        continue
                    G(lambda e: e.tensor_copy(out=kd[:, :, 1, :], in_=kd[:, :, 0, :]), [t_kd], [t_kd])
                    for g in range(2):
                        TR(psb[7][:, g * 128:(g + 1) * 128], kd[:, g, :, :].rearrange("p c d -> p (c d)"), ident[:], [t_kd, t_c], [tp[7]], sig=(g == 1))
                    A(lambda e, ti=ti, bk=bk: e.activation(out=KT[:, :, ti * 128:(ti + 1) * 128], in_=psb[7][:, 0:256].rearrange("p (g t) -> p g t", g=2), func=AF.Copy),
                      [tp[7]], [t_kv])
                for (tok0, nq, kch) in QBLOCKS:
                    if 'gqaK' in ST:
                        continue
                    for jt in range(nq // 128):
                        ti = tok0 // 128 + jt
                        bk = 6
                        proj_tok(ps[bk][:, 0:512], hT, t_hT, ti, wq, t_w, [tp[bk]])
                        normrope(ps[bk][:, 0:512], [tp[bk]], 8, gq, ti, qb[:].rearrange("p (h d) -> p h d", h=8), t_qb)
                        for pr in range(4):
                            TR(psb[7][:, pr * 128:(pr + 1) * 128], qb[:, pr * 128:(pr + 1) * 128], ident[:], [t_qb, t_c], [tp[7]], sig=(pr == 3))
                        A(lambda e, jt=jt, bk=bk: e.activation(out=QTb[:, :, jt * 128:(jt + 1) * 128], in_=psb[7][:, 0:512].rearrange("p (c t) -> p c t", c=4), func=AF.Copy),
                          [tp[7]], [t_q])
                    yb, t_yb, yTs, t_yTs, pb, t_pb, rc, t_rc = work
                    if 'noattn' in ST:
                        continue
                    attend(lambda h: QTb[(h % 2) * 64:(h % 2) * 64 + 64, h // 2, 0:nq],
                           lambda h, kc: KT[(h % 2) * 64:(h % 2) * 64 + 64, h // 4, kc * 128:(kc + 1) * 128],
                           lambda h, kc: Vg[:, kc, h // 4, 0:65],
                           t_q, t_kv, 0.125, nq, kch, yb, t_yb, pb, t_pb, rc, t_rc)
                    emit_yT(es, 2, yb, t_yb, tok0, nq, yTs, t_yTs)
                fw.barrier()

        def stage_mla(l, b, hT, t_hT):
            with ExitStack() as es:
                wkv = sb(es, [128, 8, 288], BF16); wq = sb(es, [128, 8, 384], BF16); t_w = Tok()
                load_w(wkv[:], w_in[l, :, C_CKV:C_CKV + 288], t_w)
                load_w(wq[:], w_in[l, :, C_CQ:C_CQ + 384], t_w)
                wuq = sb(es, [128, 3, 768], BF16); wukv = sb(es, [128, 2, 1024], BF16)
                load_w(wuq[:], mla_w_uq[l], t_w)
                load_w(wukv[:], mla_w_ukv[l], t_w)
                gq = sb(es, [128, 384]); gkv = sb(es, [128, 256]); t_g = Tok()
                bcast_load(gq[:], mla_g_q[l, :], t_g)
                bcast_load(gkv[:], mla_g_kv[l, :], t_g)
                KT = sb(es, [96, 8, NTOK], BF16); Vm = sb(es, [128, NT, 8, 80], BF16); t_kv = Tok()
                G(lambda e: e.memset(Vm[:, :, :, 64:65], 1.0), [], [t_kv])
                cf = sb(es, [128, 416]); junk = sb(es, [128, 384], BF16); ss = sb(es, [128, 1]); cn = sb(es, [128, 384], BF16); t_cw = Tok()
                cT = sb(es, [128, 3, 128], BF16); t_cT = Tok()
                kpe = sb(es, [128, 1, 32]); tmp = sb(es, [128, 4, 8, 16]); t_tmp = Tok(); t_kpe = Tok()
                kfull = sb(es, [128, 8, 96], BF16); t_kf = Tok()
                qfull = sb(es, [128, 8, 96], BF16); t_qf = Tok()
                qpe = sb(es, [128, 8, 32]); t_qpe = Tok()
                QTb = sb(es, [96, 8, 512], BF16); t_q = Tok()
                work = attn_work(es)

                def lowrank(src_ps, tps, n, gt):
                    A(lambda e: e.activation(out=cf[:, 0:n], in_=src_ps, func=AF.Copy), tps, [t_cw])
                    A(lambda e: e.activation(out=junk[:, 0:n], in_=cf[:, 0:n], func=AF.Square, accum_out=ss[:]), [t_cw], [t_cw])
                    rsqrt_mean(ss[:], n, [t_cw])
                    V(lambda e: e.scalar_tensor_tensor(out=cn[:, 0:n], in0=cf[:, 0:n], scalar=ss[:, 0:1], in1=gt[:, 0:n], op0=ALU.mult, op1=ALU.mult), [t_cw, t_g], [t_cw])
                    for c in range(n // 128):
                        TR(psb[7][:, c * 128:(c + 1) * 128], cn[:, c * 128:(c + 1) * 128], ident[:], [t_cw, t_c], [tp[7]], sig=(c == n // 128 - 1))
                    V(lambda e: e.tensor_copy(out=cT[:, 0:n // 128, :], in_=psb[7][:, 0:n].rearrange("p (c t) -> p c t", t=128)), [tp[7]], [t_cT])

                for ti in range(NT):
                    proj_tok(ps[6][:, 0:288], hT, t_hT, ti, wkv, t_w, [tp[6]])
                    if ti >= 2:
                        A(lambda e: e.activation(out=cf[:, 384:416], in_=ps[6][:, 256:288], func=AF.Copy), [tp[6]], [t_kpe])
                        rope(kpe[:], cf[:, 384:416].rearrange("p (h d) -> p h d", h=1), rm[:, 0, ti - 2, :], rm[:, 1, ti - 2, :], 1, 16, tmp[:, :, 0:1, :], [t_kpe, t_c], t_tmp, t_kpe)
                    else:
                        A(lambda e: e.activation(out=kpe[:, 0, :], in_=ps[6][:, 256:288], func=AF.Copy), [tp[6]], [t_kpe])
                    lowrank(ps[6][:, 0:256], [tp[6]], 256, gkv)
                    for half in range(2):
                        bk = 4 + half
                        for c in range(2):
                            MM(ps[bk][:, 0:512], cT[:, c, :], wukv[:, c, half * 512:(half + 1) * 512], [t_cT, t_w], [tp[bk]], start=(c == 0), stop=(c == 1), sig=(c == 1))
                        kv3 = ps[bk][:, 0:512].rearrange("p (h d) -> p h d", h=4)
                        V(lambda e, kv3=kv3, half=half: e.tensor_copy(out=kfull[:, half * 4:(half + 1) * 4, 0:64], in_=kv3[:, :, 0:64]), [tp[bk]], [t_kf])
                        A(lambda e, kv3=kv3, half=half, ti=ti: e.activation(out=Vm[:, ti, half * 4:(half + 1) * 4, 0:64], in_=kv3[:, :, 64:128], func=AF.Copy), [tp[bk]], [t_kv])
                    G(lambda e: e.tensor_copy(out=kfull[:, :, 64:96], in_=kpe[:].broadcast_to([128, 8, 32])), [t_kpe], [t_kf])
                    for h in range(8):
                        TR(psb[7][0:96, h * 128:(h + 1) * 128], kfull[:, h, :], ident[:], [t_kf, t_c], [tp[7]], sig=(h == 7))
                    A(lambda e, ti=ti: e.activation(out=KT[:, :, ti * 128:(ti + 1) * 128], in_=psb[7][0:96, :].rearrange("p (h t) -> p h t", h=8), func=AF.Copy), [tp[7]], [t_kv])
                for (tok0, nq, kch) in QBLOCKS:
                    for jt in range(nq // 128):
                        ti = tok0 // 128 + jt
                        proj_tok(ps[6][:, 0:384], hT, t_hT, ti, wq, t_w, [tp[6]])
                        lowrank(ps[6][:, 0:384], [tp[6]], 384, gq)
                        for c in range(3):
                            MM(ps[4][:, 0:512], cT[:, c, :], wuq[:, c, 0:512], [t_cT, t_w], [tp[4]], start=(c == 0), stop=(c == 2), sig=(c == 2))
                        for c in range(3):
                            MM(ps[5][:, 0:256], cT[:, c, :], wuq[:, c, 512:768], [t_cT, t_w], [tp[5]], start=(c == 0), stop=(c == 2), sig=(c == 2))
                        for h in range(8):
                            c0 = h * 96
                            for (a, bnd, dst_off) in ((c0, c0 + 64, 0),):
                                pass
                        def colcopy(dst_fn, c_lo, c_hi, eng):
                            segs = []
                            if c_lo < 512:
                                segs.append((4, c_lo, min(c_hi, 512), c_lo))
                            if c_hi > 512:
                                segs.append((5, max(c_lo, 512) - 512, c_hi - 512, max(c_lo, 512)))
                            for (bk, lo, hi, g0) in segs:
                                eng(lambda e, bk=bk, lo=lo, hi=hi, g0=g0: e.tensor_copy(out=dst_fn(g0 - c_lo, g0 - c_lo + hi - lo), in_=ps[bk][:, lo:hi]), [tp[bk]], None)
                        for h in range(8):
                            c0 = h * 96
                            segs = []
                            for (lo, hi, kind) in ((c0, c0 + 64, "n"), (c0 + 64, c0 + 96, "p")):
                                parts = []
                                if lo < 512:
                                    parts.append((4, lo, min(hi, 512), 0))
                                if hi > 512:
                                    parts.append((5, max(lo, 512) - 512, hi - 512, max(lo, 512) - lo))
                                for (bk, a, bb, off) in parts:
                                    if kind == "n":
                                        V(lambda e, bk=bk, a=a, bb=bb, off=off, h=h: e.tensor_copy(out=qfull[:, h, off:off + bb - a], in_=ps[bk][:, a:bb]), [tp[bk]], [t_qf])
                                    else:
                                        dst = qpe if ti >= 2 else None
                                        if ti >= 2:
                                            V(lambda e, bk=bk, a=a, bb=bb, off=off, h=h: e.tensor_copy(out=qpe[:, h, off:off + bb - a], in_=ps[bk][:, a:bb]), [tp[bk]], [t_qpe])
                                        else:
                                            V(lambda e, bk=bk, a=a, bb=bb, off=off, h=h: e.tensor_copy(out=qfull[:, h, 64 + off:64 + off + bb - a], in_=ps[bk][:, a:bb]), [tp[bk]], [t_qf])
                        if ti >= 2:
                            rope(qfull[:, :, 64:96], qpe[:], rm[:, 0, ti - 2, :], rm[:, 1, ti - 2, :], 8, 16, tmp[:], [t_qpe, t_c], t_tmp, t_qf)
                        for h in range(8):
                            TR(psb[7][0:96, h * 128:(h + 1) * 128], qfull[:, h, :], ident[:], [t_qf, t_c], [tp[7]], sig=(h == 7))
                        A(lambda e, jt=jt: e.activation(out=QTb[:, :, jt * 128:(jt + 1) * 128], in_=psb[7][0:96, :].rearrange("p (h t) -> p h t", h=8), func=AF.Copy), [tp[7]], [t_q])
                    yb, t_yb, yTs, t_yTs, pb, t_pb, rc, t_rc = work
                    attend(lambda h: QTb[:, h, 0:nq],
                           lambda h, kc: KT[:, h, kc * 128:(kc + 1) * 128],
                           lambda h, kc: Vm[:, kc, h, 0:65],
                           t_q, t_kv, 96.0 ** -0.5, nq, kch, yb, t_yb, pb, t_pb, rc, t_rc)
                    emit_yT(es, 1, yb, t_yb, tok0, nq, yTs, t_yTs)
                fw.barrier()

        def stage_na(l, b, hT, t_hT):
            with ExitStack() as es:
                wk = sb(es, [128, 8, 512], BF16); wv = sb(es, [128, 8, 512], BF16); wq = sb(es, [128, 8, 512], BF16); t_w = Tok()
                load_w(wk[:], w_in[l, :, C_NAK:C_NAK + 512], t_w)
                load_w(wv[:], w_in[l, :, C_NAV:C_NAV + 512], t_w)
                load_w(wq[:], w_in[l, :, C_NAQ:C_NAQ + 512], t_w)
                KT = sb(es, [128, 4, NTOK], BF16); Vn = sb(es, [128, NT, 8, 80], BF16); t_kv = Tok()
                G(lambda e: e.memset(Vn[:, :, :, 64:65], 1.0), [], [t_kv])
                QT = sb(es, [128, 4, NTOK], BF16); t_q = Tok()
                blocks = [(i * 512, min(512, NTOK - i * 512)) for i in range(5)]
                cnt = 0
                for (wt, dstT, tdst) in ((wk, KT, t_kv), (wq, QT, t_q)):
                    for pr in range(4):
                        for (t0, n) in blocks:
                            bk = 5 + cnt % 2
                            cnt += 1
                            for kc in range(8):
                                MM(ps[bk][:, 0:n], wt[:, kc, pr * 128:(pr + 1) * 128], hT[:, kc, t0:t0 + n], [t_w] + t_hT[t0 // 128:(t0 + n) // 128], [tp[bk]],
                                   start=(kc == 0), stop=(kc == 7), sig=(kc == 7))
                            A(lambda e, bk=bk, dstT=dstT, pr=pr, t0=t0, n=n: e.activation(out=dstT[:, pr, t0:t0 + n], in_=ps[bk][:, 0:n], func=AF.Copy), [tp[bk]], [tdst])
                for ti in range(NT):
                    bk = 5 + ti % 2
                    proj_tok(ps[bk][:, 0:512], hT, t_hT, ti, wv, t_w, [tp[bk]])
                    V(lambda e, ti=ti, bk=bk: e.tensor_copy(out=Vn[:, ti, :, 0:64], in_=ps[bk][:, 0:512].rearrange("p (h d) -> p h d", h=8)), [tp[bk]], [t_kv])
                yb, t_yb, yTs, t_yTs, pb, t_pb, rc, t_rc = attn_work(es)
                attend(lambda h: QT[(h % 2) * 64:(h % 2) * 64 + 64, h // 2, 0:256],
                       lambda h, kc: KT[(h % 2) * 64:(h % 2) * 64 + 64, h // 2, kc * 128:(kc + 1) * 128],
                       lambda h, kc: Vn[:, kc, h, 0:65],
                       t_q, t_kv, 0.125, 256, [0, 1], yb, t_yb, pb, t_pb, rc, t_rc)
                emit_yT(es, 0, yb, t_yb, 0, 256, yTs, t_yTs)
                bt = [sb(es, [128, 5, 128]) for _ in range(3)]; t_bt = [Tok() for _ in range(3)]
                bres = sb(es, [128, 8, 5, 128]); t_bres = Tok()
                fw.dma("sp", bres[:], nab[l, 0], writes=[t_bres])
                Sb = [sb(es, [128, 5, 128]) for _ in range(2)]; t_Sb = [Tok() for _ in range(2)]
                Pn = [sb(es, [128, 7, 128], BF16) for _ in range(2)]; t_Pn = [Tok() for _ in range(2)]
                iters = [(t, h) for t in range(16) for h in range(8)]

                def na_S(it):
                    t, h = iters[it]
                    cls, chunks = na_chunks(t)
                    nloc = len(chunks)
                    q0 = 256 + t * 128
                    par = (h % 2) * 64
                    if cls != 0:
                        fw.dma("sp", bt[it % 3][:], nab[l, cls, :, h, :, :], writes=[t_bt[it % 3]])
                    pa = 0 + 2 * (it % 2); pbk = 1 + 2 * (it % 2)
                    qv = QT[par:par + 64, h // 2, q0:q0 + 128]
                    for s_, c in enumerate(chunks[:4]):
                        kt0 = 256 + c * 128
                        MM(ps[pa][:, s_ * 128:(s_ + 1) * 128], KT[par:par + 64, h // 2, kt0:kt0 + 128], qv, [t_q, t_kv], [tp[pa]], sig=(s_ == 3))
                    for s_ in range(2):
                        MM(ps[pbk][:, s_ * 128:(s_ + 1) * 128], KT[par:par + 64, h // 2, s_ * 128:(s_ + 1) * 128], qv, [t_q, t_kv], [tp[pbk]], sig=(s_ == 1 and nloc == 4))
                    if nloc == 5:
                        kt0 = 256 + chunks[4] * 128
                        MM(ps[pbk][:, 256:384], KT[par:par + 64, h // 2, kt0:kt0 + 128], qv, [t_q, t_kv], [tp[pbk]])

                def na_rest(it):
                    t, h = iters[it]
                    cls, chunks = na_chunks(t)
                    nloc = len(chunks)
                    if cls == 0:
                        B_ = bres[:, h]; tB = t_bres
                    else:
                        B_ = bt[it % 3][:]; tB = t_bt[it % 3]
                    pa = 0 + 2 * (it % 2); pbk = 1 + 2 * (it % 2)
                    S_ = Sb[it % 2]; tS = t_Sb[it % 2]
                    P_ = Pn[it % 2]; tP = t_Pn[it % 2]
                    V(lambda e: e.scalar_tensor_tensor(out=S_[:, 0:4, :], in0=ps[pa][:, 0:512].rearrange("p (s q) -> p s q", s=4), scalar=0.125,
                                                       in1=B_[:, 0:4, :], op0=ALU.mult, op1=ALU.add), [tp[pa], tB], [tS])
                    if nloc == 5:
                        V(lambda e: e.scalar_tensor_tensor(out=S_[:, 4, :], in0=ps[pbk][:, 256:384], scalar=0.125,
                                                           in1=B_[:, 4, :], op0=ALU.mult, op1=ALU.add), [tp[pbk], tB], [tS])
                    A(lambda e: e.activation(out=P_[:, 5:7, :], in_=ps[pbk][:, 0:256].rearrange("p (s q) -> p s q", s=2), func=AF.Exp, scale=0.125), [tp[pbk]], [tP])
                    A(lambda e: e.activation(out=P_[:, 0:nloc, :], in_=S_[:, 0:nloc, :], func=AF.Exp), [tS], [tP])
                    ob = 4 + it % 2
                    seq = [(5, 0), (6, 1)] + [(s_, 2 + c) for s_, c in enumerate(chunks)]
                    for i, (s_, kti) in enumerate(seq):
                        MM(ps[ob][:, 0:65], P_[:, s_, :], Vn[:, kti, h, 0:65], [tP, t_kv], [tp[ob]], start=(i == 0), stop=(i == len(seq) - 1), sig=(i == len(seq) - 1))
                    r = rc[it % 4]; tr_ = t_rc[it % 4]
                    V(lambda e: e.reciprocal(out=r[:], in_=ps[ob][:, 64:65]), [tp[ob]], [tr_])
                    V(lambda e: e.tensor_scalar(out=yb[:, t % 4, h * 64:(h + 1) * 64], in0=ps[ob][:, 0:64], scalar1=r[:, 0:1], scalar2=None, op0=ALU.mult),
                      [tp[ob], tr_], [t_yb])
                    if h == 7 and t % 4 == 3:
                        emit_yT(es, 0, yb, t_yb, 256 + (t - 3) * 128, 512, yTs, t_yTs)

                na_S(0)
                for it in range(len(iters)):
                    if it + 1 < len(iters):
                        na_S(it + 1)
                    na_rest(it)
                fw.barrier()

        def stage_ssm(l, b, hT, t_hT):
            with ExitStack() as es:
                t_w = Tok()
                wdt = sb(es, [128, 8, 16], BF16); load_w(wdt[:], w_in[l, :, C_SDT:C_SDT + 16], t_w)
                wz = sb(es, [128, 8, 512], BF16); load_w(wz[:], w_in[l, :, C_SZ:C_SZ + 512], t_w)
                cw = sb(es, [128, 8, 5]); cb = sb(es, [128, 8]); t_cw = Tok()
                fw.dma("sp", cw[:], conv_wT[l].rearrange("(c p) j -> p c j", p=128), writes=[t_cw])
                fw.dma("sp", cb[:], conv_b[l].rearrange("(c p) -> p c", p=128), writes=[t_cw], allow_slow_non_contiguous=True)
                sm = sb(es, [128, 40]); t_sm = Tok()
                bcast_load(sm[:], ssm_small[l, :], t_sm)
                gn = sb(es, [128, 512]); bcast_load(gn[:], ssm_g_norm[l, :], t_sm)
                aneg = sb(es, [128, 16])
                A(lambda e: e.activation(out=aneg[:], in_=sm[:, 0:16], func=AF.Exp), [t_sm], [t_sm])
                V(lambda e: e.tensor_scalar(out=aneg[:], in0=aneg[:], scalar1=-1.0, scalar2=None, op0=ALU.mult), [t_sm], [t_sm])
                xs = sb(es, [128, NT, 512], BF16); t_xs = Tok()
                Bt = sb(es, [128, NT, 256], BF16); t_Bt = Tok()
                BT = sb(es, [128, 2, NTOK], BF16); CT = sb(es, [128, 2, NTOK], BF16); t_BC = Tok()
                dtt = sb(es, [128, NT, 16]); dta = sb(es, [128, NT, 16]); t_dt = Tok()
                PADW = 2 + 256 + 2 + 2 + 2048 + 2
                ec = es.enter_context(ExitStack())
                pre1 = sb(ec, [128, PADW]); pre = [pre1, pre1]; t_p1 = Tok(); t_pre = [t_p1, t_p1]
                acc1 = sb(ec, [128, NTOK]); acc = [acc1, acc1]; t_a1 = Tok(); t_acc = [t_a1, t_a1]
                post1 = sb(ec, [128, NTOK], BF16); post = [post1, post1]; t_po1 = Tok(); t_post = [t_po1, t_po1]
                wc = [sb(ec, [128, 8, 128], BF16) for _ in range(2)]; t_wc = [Tok(), Tok()]
                G(lambda e: e.memset(pre1[:], 0.0), [], [t_p1])
                segs = [(2, 0, 256), (262, 256, 2048)]
                blocks = [(i * 512, min(512, NTOK - i * 512)) for i in range(5)]
                for ch in range(8):
                    col0 = (C_SX + ch * 128) if ch < 6 else (C_SC + (ch - 6) * 128)
                    i2 = ch % 2
                    load_w(wc[i2][:], w_in[l, :, col0:col0 + 128], t_wc[i2])
                    for bi, (t0, n) in enumerate(blocks):
                        bk = 5 + bi % 2
                        for kc in range(8):
                            MM(ps[bk][:, 0:n], wc[i2][:, kc, :], hT[:, kc, t0:t0 + n], [t_wc[i2]] + t_hT[t0 // 128:(t0 + n) // 128], [tp[bk]],
                               start=(kc == 0), stop=(kc == 7), sig=(kc == 7))
                        if t0 == 0:
                            A(lambda e, bk=bk, i2=i2: e.activation(out=pre[i2][:, 2:258], in_=ps[bk][:, 0:256], func=AF.Copy), [tp[bk]], [t_pre[i2]])
                            A(lambda e, bk=bk, i2=i2: e.activation(out=pre[i2][:, 262:518], in_=ps[bk][:, 256:512], func=AF.Copy), [tp[bk]], [t_pre[i2]])
                        else:
                            o = 262 + (t0 - 256)
                            A(lambda e, bk=bk, i2=i2, o=o, n=n: e.activation(out=pre[i2][:, o:o + n], in_=ps[bk][:, 0:n], func=AF.Copy), [tp[bk]], [t_pre[i2]])
                    for (po, ao, L) in segs:
                        V(lambda e, i2=i2, po=po, ao=ao, L=L, ch=ch: e.tensor_scalar(out=acc[i2][:, ao:ao + L], in0=pre[i2][:, po - 2:po - 2 + L], scalar1=cw[:, ch, 0:1], scalar2=None, op0=ALU.mult),
                          [t_pre[i2], t_cw], [t_acc[i2]])
                        for j in range(1, 5):
                            V(lambda e, i2=i2, po=po, ao=ao, L=L, ch=ch, j=j: e.scalar_tensor_tensor(out=acc[i2][:, ao:ao + L], in0=pre[i2][:, po - 2 + j:po - 2 + j + L], scalar=cw[:, ch, j:j + 1],
                                                                                                  in1=acc[i2][:, ao:ao + L], op0=ALU.mult, op1=ALU.add), [t_pre[i2], t_cw, t_acc[i2]], [t_acc[i2]])
                    if ch < 4:
                        dst = post[i2][:]
                    elif ch < 6:
                        dst = BT[:, ch - 4, :]
                    else:
                        dst = CT[:, ch - 6, :]
                    tdst = t_post[i2] if ch < 4 else t_BC
                    A(lambda e, i2=i2, dst=dst, ch=ch: e.activation(out=dst, in_=acc[i2][:], func=AF.Silu, bias=cb[:, ch:ch + 1]), [t_acc[i2], t_cw], [tdst])
                    if ch < 6:
                        srcT = post[i2] if ch < 4 else None
                        for ti in range(NT):
                            bk = 7
                            if ch < 4:
                                TR(psb[bk][:, 0:128], post[i2][:, ti * 128:(ti + 1) * 128], ident[:], [t_post[i2], t_c], [tp[bk]])
                                V(lambda e, bk=bk, ti=ti, ch=ch: e.tensor_copy(out=xs[:, ti, ch * 128:(ch + 1) * 128], in_=psb[bk][:, 0:128]), [tp[bk]], [t_xs])
                            else:
                                TR(psb[bk][:, 0:128], BT[:, ch - 4, ti * 128:(ti + 1) * 128], ident[:], [t_BC, t_c], [tp[bk]])
                                V(lambda e, bk=bk, ti=ti, ch=ch: e.tensor_copy(out=Bt[:, ti, (ch - 4) * 128:(ch - 3) * 128], in_=psb[bk][:, 0:128]), [tp[bk]], [t_Bt])
                for ti in range(NT):
                    bk = 5 + ti % 2
                    proj_tok(ps[bk][:, 0:16], hT, t_hT, ti, wdt, t_w, [tp[bk]])
                    V(lambda e, bk=bk, ti=ti: e.tensor_tensor(out=dtt[:, ti, :], in0=ps[bk][:, 0:16], in1=sm[:, 16:32], op=ALU.add), [tp[bk], t_sm], [t_dt])
                A(lambda e: e.activation(out=dtt[:], in_=dtt[:], func=AF.Exp), [t_dt], [t_dt])
                A(lambda e: e.activation(out=dtt[:], in_=dtt[:], func=AF.Ln, bias=1.0), [t_dt], [t_dt])
                V(lambda e: e.tensor_tensor(out=dta[:], in0=dtt[:], in1=aneg[:].unsqueeze(1).broadcast_to([128, NT, 16]), op=ALU.mult), [t_dt, t_sm], [t_dt])
                fw.barrier()
                ec.close()
                ysum = sb(es, [128, NT, 512]); t_ys = [Tok() for _ in range(NT)]
                S = sb(es, [128, 8, 64]); Hb = sb(es, [128, 8, 64], BF16); t_S = Tok(); t_Hb = Tok()
                cbT = sb(es, [128, 2, 128]); t_cbT = Tok()
                sc2 = [sb(es, [128, 5, 8]) for _ in range(2)]; t_sc2 = [Tok(), Tok()]
                r1 = sb(es, [128, 8, 128]); r2 = sb(es, [128, 8, 128]); t_r = Tok()
                LT = sb(es, [128, 8, 128]); t_LT = Tok()
                MT2 = [sb(es, [128, 8, 128], BF16) for _ in range(2)]; t_MT2 = [Tok(), Tok()]
                xdt2 = [sb(es, [128, 8, 64], BF16) for _ in range(2)]; xw2 = [sb(es, [128, 8, 64], BF16) for _ in range(2)]; t_xd2 = [Tok(), Tok()]
                ytmp = sb(es, [128, 8, 64]); t_yt = Tok()
                steps = []
                for d_ in range(2):
                    order = list(range(NT)) if d_ == 0 else [1, 0] + list(range(NT - 1, 1, -1))
                    for j_, ti_ in enumerate(order):
                        steps.append((d_, ti_, j_ == 0))

                def phaseA(k):
                    d, ti, first = steps[k]
                    sc = sc2[k % 2]; t_sc = t_sc2[k % 2]
                    MT = MT2[k % 2]; t_MT = t_MT2[k % 2]
                    xdt = xdt2[k % 2]; xw = xw2[k % 2]; t_xd = t_xd2[k % 2]
                    dcol = dta[:, ti, d * 8:(d + 1) * 8]
                    for g in range(2):
                        MM(ps[5][:, g * 128:(g + 1) * 128], BT[:, g, ti * 128:(ti + 1) * 128], CT[:, g, ti * 128:(ti + 1) * 128], [t_BC], [tp[5]], sig=(g == 1))
                    A(lambda e: e.activation(out=cbT[:], in_=ps[5][:, 0:256].rearrange("p (g q) -> p g q", g=2), func=AF.Copy), [tp[5]], [t_cbT])
                    MM(ps[6][:, 0:8], tri[d], dcol, [t_c, t_dt], [tp[6]], sig=False)
                    MM(ps[6][:, 8:16], onesf, dcol, [t_c, t_dt], [tp[6]])
                    V(lambda e: e.tensor_copy(out=sc[:, 0:2, :], in_=ps[6][:, 0:16].rearrange("p (a h) -> p a h", a=2)), [tp[6]], [t_sc])
                    V(lambda e: e.tensor_tensor(out=sc[:, 2, :], in0=sc[:, 1, :], in1=sc[:, 0, :], op=ALU.subtract), [t_sc], [t_sc])
                    A(lambda e: e.activation(out=sc[:, 2, :], in_=sc[:, 2, :], func=AF.Exp), [t_sc], [t_sc])
                    A(lambda e: e.activation(out=sc[:, 3, :], in_=sc[:, 0, :], func=AF.Exp), [t_sc], [t_sc])
                    A(lambda e: e.activation(out=sc[:, 4, :], in_=sc[:, 1, :], func=AF.Exp), [t_sc], [t_sc])
                    V(lambda e: e.tensor_tensor(out=r1[:], in0=tri[d].unsqueeze(1).broadcast_to([128, 8, 128]), in1=dcol.unsqueeze(2).broadcast_to([128, 8, 128]), op=ALU.mult),
                      [t_c, t_dt], [t_r])
                    G(lambda e: e.tensor_copy(out=r2[:], in_=dcol.unsqueeze(2).broadcast_to([128, 8, 128])), [t_dt], [t_r])
                    for hb in range(2):
                        o = ps[hb][:, 0:512]
                        MM(o, onesf, r1[:, hb * 4:(hb + 1) * 4, :].rearrange("p h q -> p (h q)"), [t_c, t_r], [tp[hb]], start=True, stop=False, sig=False)
                        MM(o, ntri[d], r2[:, hb * 4:(hb + 1) * 4, :].rearrange("p h q -> p (h q)"), [t_c, t_r], [tp[hb]], start=False, stop=False, sig=False)
                        MM(o, identf, mneg4[d], [t_c], [tp[hb]], start=False, stop=True)
                        A(lambda e, hb=hb: e.activation(out=LT[:, hb * 4:(hb + 1) * 4, :], in_=ps[hb][:, 0:512].rearrange("p (h q) -> p h q", h=4), func=AF.Exp), [tp[hb]], [t_LT])
                        V(lambda e, hb=hb: e.tensor_tensor(out=MT[:, hb * 4:(hb + 1) * 4, :], in0=LT[:, hb * 4:(hb + 1) * 4, :],
                                                           in1=cbT[:, hb:hb + 1, :].broadcast_to([128, 4, 128]), op=ALU.mult), [t_LT, t_cbT], [t_MT])
                    x3 = xs[:, ti, :].rearrange("p (h d) -> p h d", h=8)
                    V(lambda e: e.tensor_tensor(out=xdt[:], in0=x3, in1=dtt[:, ti, d * 8:(d + 1) * 8].unsqueeze(2).broadcast_to([128, 8, 64]), op=ALU.mult), [t_xs, t_dt], [t_xd])
                    G(lambda e: e.tensor_tensor(out=xw[:], in0=xdt[:], in1=sc[:, 2, :].unsqueeze(2).broadcast_to([128, 8, 64]), op=ALU.mult), [t_xd, t_sc], [t_xd])

                def phaseB(k):
                    d, ti, first = steps[k]
                    sc = sc2[k % 2]; t_sc = t_sc2[k % 2]
                    MT = MT2[k % 2]; t_MT = t_MT2[k % 2]
                    xdt = xdt2[k % 2]; xw = xw2[k % 2]; t_xd = t_xd2[k % 2]
                    if first:
                        G(lambda e: e.memset(S[:], 0.0), [], [t_S])
                        G(lambda e: e.memset(Hb[:], 0.0), [], [t_Hb])
                    for h in range(8):
                        MM(ps[2][:, h * 64:(h + 1) * 64], MT[:, h, :], xdt[:, h, :], [t_MT, t_xd], [tp[2]], sig=(h == 7))
                    for g in range(2):
                        MM(ps[3][:, g * 256:(g + 1) * 256], CT[:, g, ti * 128:(ti + 1) * 128], Hb[:, g * 4:(g + 1) * 4, :].rearrange("p h d -> p (h d)"), [t_BC, t_Hb], [tp[3]], sig=(g == 1))
                    for g in range(2):
                        MM(ps[4][:, g * 256:(g + 1) * 256], Bt[:, ti, g * 128:(g + 1) * 128], xw[:, g * 4:(g + 1) * 4, :].rearrange("p h d -> p (h d)"), [t_Bt, t_xd], [tp[4]], sig=(g == 1))
                    V(lambda e: e.tensor_tensor(out=ytmp[:], in0=ps[3][:, 0:512].rearrange("p (h d) -> p h d", h=8), in1=sc[:, 3, :].unsqueeze(2).broadcast_to([128, 8, 64]), op=ALU.mult),
                      [tp[3], t_sc], [t_yt])
                    V(lambda e: e.tensor_tensor(out=S[:], in0=S[:], in1=sc[:, 4, :].unsqueeze(2).broadcast_to([128, 8, 64]), op=ALU.mult), [t_S, t_sc], [t_S])
                    V(lambda e: e.tensor_tensor(out=S[:], in0=S[:], in1=ps[4][:, 0:512].rearrange("p (h d) -> p h d", h=8), op=ALU.add), [t_S, tp[4]], [t_S])
                    A(lambda e: e.activation(out=Hb[:], in_=S[:], func=AF.Copy), [t_S], [t_Hb])
                    yv = ysum[:, ti, :]
                    if d == 0:
                        V(lambda e: e.tensor_tensor(out=yv, in0=ps[2][:, 0:512], in1=ytmp[:].rearrange("p h d -> p (h d)"), op=ALU.add), [tp[2], t_yt], [t_ys[ti]])
                    else:
                        V(lambda e: e.tensor_tensor(out=ytmp[:].rearrange("p h d -> p (h d)"), in0=ps[2][:, 0:512], in1=ytmp[:].rearrange("p h d -> p (h d)"), op=ALU.add), [tp[2], t_yt], [t_yt])
                        G(lambda e: e.tensor_tensor(out=yv, in0=yv, in1=ytmp[:].rearrange("p h d -> p (h d)"), op=ALU.add), [t_yt, t_ys[ti]], [t_ys[ti]])

                phaseA(0)
                for k in range(len(steps)):
                    if k + 1 < len(steps):
                        phaseA(k + 1)
                    phaseB(k)
                dsk = sm[:, 32:40]
                zf = sb(es, [128, 512]); tf = sb(es, [128, 512]); junk = sb(es, [128, 512], BF16); ss = sb(es, [128, 1]); t_f = Tok()
                yb = sb(es, [128, 4, 512], BF16); t_yb = Tok(); yTs = sb(es, [128, 4, 512], BF16); t_yTs = Tok()
                for ti in range(NT):
                    bk = 5 + ti % 2
                    proj_tok(ps[bk][:, 0:512], hT, t_hT, ti, wz, t_w, [tp[bk]])
                    A(lambda e, bk=bk: e.activation(out=zf[:], in_=ps[bk][:, 0:512], func=AF.Silu), [tp[bk]], [t_f])
                    x3 = xs[:, ti, :].rearrange("p (h d) -> p h d", h=8)
                    V(lambda e, x3=x3: e.tensor_tensor(out=tf[:].rearrange("p (h d) -> p h d", h=8), in0=x3, in1=dsk.unsqueeze(2).broadcast_to([128, 8, 64]), op=ALU.mult), [t_xs, t_sm], [t_f])
                    V(lambda e, ti=ti: e.tensor_tensor(out=tf[:], in0=tf[:], in1=ysum[:, ti, :], op=ALU.add), [t_f, t_ys[ti]], [t_f])
                    V(lambda e: e.tensor_tensor(out=tf[:], in0=tf[:], in1=zf[:], op=ALU.mult), [t_f], [t_f])
                    A(lambda e: e.activation(out=junk[:], in_=tf[:], func=AF.Square, accum_out=ss[:]), [t_f], [t_f])
                    rsqrt_mean(ss[:], 512, [t_f])
                    jslot = (ti % 4) if ti >= 2 else ti
                    jslot = ((ti - 2) % 4) if ti >= 2 else ti
                    V(lambda e, jslot=jslot: e.scalar_tensor_tensor(out=yb[:, jslot, :], in0=tf[:], scalar=ss[:, 0:1], in1=gn[:], op0=ALU.mult, op1=ALU.mult), [t_f, t_sm], [t_yb])
                    if ti == 1:
                        emit_yT(es, 3, yb, t_yb, 0, 256, yTs, t_yTs)
                    elif ti >= 2 and (ti - 2) % 4 == 3:
                        emit_yT(es, 3, yb, t_yb, (ti - 3) * 128, 512, yTs, t_yTs)
                fw.barrier()

        def stage_merge(l, b, hT, t_hT):
            with ExitStack() as es:
                gT = sb(es, [128, 8, NTOK], BF16); t_gT = [Tok() for _ in range(5)]
                blocks = [(i * 512, min(512, NTOK - i * 512)) for i in range(5)]
                with ExitStack() as e1:
                    wg = [sb(e1, [128, 8, 4, 128], BF16) for _ in range(2)]; t_wg = [Tok(), Tok()]
                    wb = [sb(e1, [128, 4, 4, 128], BF16) for _ in range(2)]; t_wb = [Tok(), Tok()]
                    ytr = sb(e1, [128, 4, 4, NTOK], BF16); t_ytb = [Tok() for _ in range(5)]
                    sg = [sb(e1, [128, 512]) for _ in range(2)]; t_sg = [Tok(), Tok()]
                    accg = sb(e1, [128, 512]); t_ag = Tok()
                    it = 0
                    for dc in range(8):
                        i2 = dc % 2
                        for k in range(4):
                            c0 = C_GATE + k * D + dc * 128
                            fw.dma("pool", wg[i2][:, :, k, :], w_in[l, :, c0:c0 + 128].rearrange("(k p) n -> p k n", p=128), writes=[t_wg[i2]])
                            fw.dma("pool", wb[i2][:, k, :, :], w_branch[l, k, :, dc * 128:(dc + 1) * 128].rearrange("(k p) n -> p k n", p=128), writes=[t_wb[i2]])
                        for bi, (t0, n) in enumerate(blocks):
                            it += 1
                            if dc == 0:
                                rd = [tk(t_yT, (k, q0)) for k in range(4) for (q0, nq, _) in QBLOCKS if q0 < t0 + n and q0 + nq > t0]
                                for k in range(4):
                                    fw.dma("sp", ytr[:, k, :, t0:t0 + n], yTd[k].rearrange("(c p) t -> p c t", p=128)[:, :, t0:t0 + n], reads=rd, writes=[t_ytb[bi]])
                            for k in range(4):
                                pg = k % 2; pp = 2 + k % 2
                                for kc in range(8):
                                    MM(ps[pg][:, 0:n], wg[i2][:, kc, k, :], hT[:, kc, t0:t0 + n], [t_wg[i2]] + t_hT[t0 // 128:(t0 + n) // 128], [tp[pg]],
                                       start=(kc == 0), stop=(kc == 7), sig=(kc == 7))
                                for ec_ in range(4):
                                    MM(ps[pp][:, 0:n], wb[i2][:, k, ec_, :], ytr[:, k, ec_, t0:t0 + n], [t_wb[i2], t_ytb[bi]], [tp[pp]],
                                       start=(ec_ == 0), stop=(ec_ == 3), sig=(ec_ == 3))
                                A(lambda e, pg=pg, k=k, n=n: e.activation(out=sg[k % 2][:, 0:n], in_=ps[pg][:, 0:n], func=AF.Sigmoid), [tp[pg]], [t_sg[k % 2]])
                                if k == 0:
                                    V(lambda e, pp=pp, n=n: e.tensor_tensor(out=accg[:, 0:n], in0=sg[0][:, 0:n], in1=ps[pp][:, 0:n], op=ALU.mult), [t_sg[0], tp[pp]], [t_ag])
                                else:
                                    V(lambda e, pp=pp, k=k, n=n: e.tensor_tensor(out=sg[k % 2][:, 0:n], in0=sg[k % 2][:, 0:n], in1=ps[pp][:, 0:n], op=ALU.mult), [t_sg[k % 2], tp[pp]], [t_sg[k % 2]])
                                    if k < 3:
                                        G(lambda e, k=k, n=n: e.tensor_tensor(out=accg[:, 0:n], in0=accg[:, 0:n], in1=sg[k % 2][:, 0:n], op=ALU.add), [t_ag, t_sg[k % 2]], [t_ag])
                                    else:
                                        G(lambda e, k=k, n=n, dc=dc, t0=t0: e.tensor_tensor(out=gT[:, dc, t0:t0 + n], in0=accg[:, 0:n], in1=sg[k % 2][:, 0:n], op=ALU.add),
                                          [t_ag, t_sg[k % 2]], [t_gT[bi]])
                    fw.barrier()
                with ExitStack() as e2:
                    wo = sb(e2, [128, 8, D], BF16); t_wo = Tok()
                    load_w(wo[:], w_out[l], t_wo)
                    tl, t_ab = {}, Tok()
                    for rname, row in (("b", b), ("c", 2)):
                        t = sb(e2, [128, D])
                        fw.dma("sp", t[:], combd[l, row, 2, :].partition_broadcast(128), reads=[t_comb[l]], writes=[t_ab])
                        tl[rname] = t
                    xb = [sb(e2, [128, D]) for _ in range(2)]; t_xb = [Tok(), Tok()]
                    junk = sb(e2, [128, D], BF16); ss = [sb(e2, [128, 1]) for _ in range(2)]; t_ss = [Tok(), Tok()]
                    t1 = [sb(e2, [128, D]) for _ in range(2)]; t_t1 = [Tok(), Tok()]
                    for ti in range(NT):
                        if l == 1 and ti < 2:
                            continue
                        i2 = ti % 2
                        src, rt = xsrc(l, b, ti)
                        fw.dma("sp", xb[i2][:], src, reads=rt, writes=[t_xb[i2]])
                        for half in range(2):
                            bk = 4 + half
                            for kc in range(8):
                                MM(ps[bk][:, 0:512], gT[:, kc, ti * 128:(ti + 1) * 128], wo[:, kc, half * 512:(half + 1) * 512], [t_gT[ti // 4], t_wo], [tp[bk]],
                                   start=(kc == 0), stop=(kc == 7), sig=(kc == 7))
                        A(lambda e, i2=i2: e.activation(out=junk[:, 0:512], in_=ps[4][:, 0:512], func=AF.Square, accum_out=ss[i2][:]), [tp[4]], [t_ss[i2]])
                        A(lambda e, i2=i2: e.activation(out=junk[:, 512:1024], in_=ps[5][:, 0:512], func=AF.Square, accum_out=t1[i2][:, 0:1]), [tp[5]], [t_t1[i2]])
                        V(lambda e, i2=i2: e.tensor_tensor(out=ss[i2][:], in0=ss[i2][:], in1=t1[i2][:, 0:1], op=ALU.add), [t_ss[i2], t_t1[i2]], [t_ss[i2]])
                        rsqrt_mean(ss[i2][:], D, [t_ss[i2]])
                        Gt = tl["c" if ti < 2 else "b"]
                        for half in range(2):
                            V(lambda e, i2=i2, half=half, Gt=Gt: e.scalar_tensor_tensor(out=t1[i2][:, half * 512:(half + 1) * 512], in0=ps[4 + half][:, 0:512], scalar=ss[i2][:, 0:1],
                                                                                       in1=Gt[:, half * 512:(half + 1) * 512], op0=ALU.mult, op1=ALU.mult),
                              [tp[4 + half], t_ss[i2], t_ab], [t_t1[i2]])
                        G(lambda e, i2=i2: e.tensor_tensor(out=t1[i2][:], in0=t1[i2][:], in1=xb[i2][:], op=ALU.add), [t_t1[i2], t_xb[i2]], [t_t1[i2]])
                        fw.dma("sp", xmid[b, ti * 128:(ti + 1) * 128, :], t1[i2][:], reads=[t_t1[i2]], writes=[tk(t_xmid, (b, ti))])
                    fw.barrier()

        def stage_ffn(l):
            with ExitStack() as es:
                w1 = sb(es, [128, 8, FFH], BF16); w3 = sb(es, [128, 8, FFH], BF16); w2 = sb(es, [128, 22, D], BF16); t_w = Tok()
                for c in range(0, FFH, 704):
                    load_w(w1[:, :, c:c + 704], ffn_w1[l, :, c:c + 704], t_w)
                    load_w(w3[:, :, c:c + 704], ffn_w3[l, :, c:c + 704], t_w)
                for c in range(0, 22, 11):
                    fw.dma("pool", w2[:, c:c + 11, :], ffn_w2[l, c * 128:(c + 11) * 128, :].rearrange("(k p) n -> p k n", p=128), writes=[t_w])
                xb = [sb(es, [128, D]) for _ in range(2)]; t_xb = [Tok(), Tok()]
                wk1 = (sb(es, [128, D], BF16), sb(es, [128, 1]), sb(es, [128, D]), sb(es, [128, D], BF16), Tok()); wks = [wk1, wk1]
                h2T = [sb(es, [128, 8, 128], BF16) for _ in range(2)]; t_h2T = [Tok(), Tok()]
                sl = [sb(es, [128, 512]) for _ in range(2)]; t_sl = [Tok(), Tok()]
                u = sb(es, [128, FFH], BF16); t_u = Tok()
                uT = sb(es, [128, 22, 128], BF16); t_uT = Tok()
                junk = wk1[0]; ss = sb(es, [128, 2]); t_ss = Tok()
                o11 = sb(es, [128, D]); t_o11 = Tok()
                tlc = [sb(es, [128, D]) for _ in range(3)]
                nchunks = [(c, min(512, FFH - c)) for c in range(0, FFH, 512)]
                for b in range(2):
                    tiles = [ti for ti in range(NT) if not (l == 1 and ti < 2)]
                    rowof = lambda ti: 2 if ti < 2 else b
                    state = {"row": None, "t_ab": None}

                    def E1(ti):
                        i2 = ti % 2
                        if state["row"] != rowof(ti):
                            state["row"] = rowof(ti)
                            state["t_ab"] = state["t_ab"] or Tok()
                            for j in range(3):
                                fw.dma("sp", tlc[j][:], combd[l, rowof(ti), 3 + j, :].partition_broadcast(128), reads=[t_comb[l]], writes=[state["t_ab"]])
                        fw.dma("sp", xb[i2][:], xmid[b, ti * 128:(ti + 1) * 128, :], reads=[tk(t_xmid, (b, ti))], writes=[t_xb[i2]])
                        norm_mod_T(es, xb[i2][:], t_xb[i2], tlc[0][:], tlc[1][:], state["t_ab"], h2T[i2][:], t_h2T[i2], wk1)

                    def E2(ti):
                        i2 = ti % 2
                        for ci, (c0, n) in enumerate(nchunks):
                            pa = 0 + 2 * (ci % 2); pb_ = 1 + 2 * (ci % 2)
                            for kc in range(8):
                                MM(ps[pa][:, 0:n], h2T[i2][:, kc, :], w1[:, kc, c0:c0 + n], [t_h2T[i2], t_w], [tp[pa]], start=(kc == 0), stop=(kc == 7), sig=(kc == 7))
                            for kc in range(8):
                                MM(ps[pb_][:, 0:n], h2T[i2][:, kc, :], w3[:, kc, c0:c0 + n], [t_h2T[i2], t_w], [tp[pb_]], start=(kc == 0), stop=(kc == 7), sig=(kc == 7))
                            A(lambda e, ci=ci, pa=pa, n=n: e.activation(out=sl[ci % 2][:, 0:n], in_=ps[pa][:, 0:n], func=AF.Silu), [tp[pa]], [t_sl[ci % 2]])
                            V(lambda e, ci=ci, pb_=pb_, n=n, c0=c0: e.tensor_tensor(out=u[:, c0:c0 + n], in0=sl[ci % 2][:, 0:n], in1=ps[pb_][:, 0:n], op=ALU.mult), [t_sl[ci % 2], tp[pb_]], [t_u])
                        for grp in range(3):
                            c_lo = grp * 8; c_hi = min(22, c_lo + 8)
                            for c in range(c_lo, c_hi):
                                TR(psb[7][:, (c - c_lo) * 128:(c - c_lo + 1) * 128], u[:, c * 128:(c + 1) * 128], ident[:], [t_u, t_c], [tp[7]], sig=(c == c_hi - 1))
                            V(lambda e, c_lo=c_lo, c_hi=c_hi: e.tensor_copy(out=uT[:, c_lo:c_hi, :], in_=psb[7][:, 0:(c_hi - c_lo) * 128].rearrange("p (c t) -> p c t", t=128)), [tp[7]], [t_uT])

                    def E3(ti):
                        i2 = ti % 2
                        t_ab = state["t_ab"]
                        for half in range(2):
                            bk = 4 + half
                            for c in range(22):
                                MM(ps[bk][:, 0:512], uT[:, c, :], w2[:, c, half * 512:(half + 1) * 512], [t_uT, t_w], [tp[bk]], start=(c == 0), stop=(c == 21), sig=(c == 21))
                        A(lambda e: e.activation(out=junk[:, 0:512], in_=ps[4][:, 0:512], func=AF.Square, accum_out=ss[:, 0:1]), [tp[4]], [t_ss])
                        A(lambda e: e.activation(out=junk[:, 512:1024], in_=ps[5][:, 0:512], func=AF.Square, accum_out=ss[:, 1:2]), [tp[5]], [t_ss])
                        V(lambda e: e.tensor_tensor(out=ss[:, 0:1], in0=ss[:, 0:1], in1=ss[:, 1:2], op=ALU.add), [t_ss], [t_ss])
                        rsqrt_mean(ss[:, 0:1], D, [t_ss])
                        for half in range(2):
                            V(lambda e, half=half: e.scalar_tensor_tensor(out=o11[:, half * 512:(half + 1) * 512], in0=ps[4 + half][:, 0:512], scalar=ss[:, 0:1],
                                                                          in1=tlc[2][:, half * 512:(half + 1) * 512], op0=ALU.mult, op1=ALU.mult),
                              [tp[4 + half], t_ss, t_ab], [t_o11])
                        G(lambda e: e.tensor_tensor(out=o11[:], in0=o11[:], in1=xb[i2][:], op=ALU.add), [t_o11, t_xb[i2]], [t_o11])
                        if l == 0:
                            fw.dma("sp", xres[b, ti * 128:(ti + 1) * 128, :], o11[:], reads=[t_o11], writes=[tk(t_xres, (b, ti))])
                        else:
                            fw.dma("sp", out[b, (ti - 2) * 128:(ti - 1) * 128, :], o11[:], reads=[t_o11])

                    E1(tiles[0])
                    for idx, ti in enumerate(tiles):
                        nxt = tiles[idx + 1] if idx + 1 < len(tiles) else None
                        pre = nxt is not None and rowof(nxt) == rowof(ti)
                        if pre:
                            E1(nxt)
                        E2(ti)
                        E3(ti)
                        if nxt is not None and not pre:
                            E1(nxt)
                    fw.barrier()
                fw.barrier()

        for l in layers:
            if "mod" in ST:
                stage_mod(l)
            for b in range(2):
                with ExitStack() as eh:
                    hT = sb(eh, [128, 8, NTOK], BF16, "hT")
                    t_hT = [Tok(f"hT{ti}") for ti in range(NT)]
                    if "norm1" in ST:
                        stage_norm1(l, b, hT, t_hT)
                    if "gqa" in ST:
                        stage_gqa(l, b, hT, t_hT)
                    if "mla" in ST:
                        stage_mla(l, b, hT, t_hT)
                    if "na" in ST:
                        stage_na(l, b, hT, t_hT)
                    if "ssm" in ST:
                        stage_ssm(l, b, hT, t_hT)
                    if "merge" in ST:
                        stage_merge(l, b, hT, t_hT)
                    fw.barrier()
            if "ffn" in ST:
                stage_ffn(l)
        fw.barrier()
        fw.replay()
        print("instructions:", fw.n_instr)
    return nc


def _host_consts():
    i = np.arange(128)
    c = np.zeros((128, 14, 128), np.float32)
    c[:, 0, :] = np.eye(128)
    trif = (i[:, None] <= i[None, :]).astype(np.float32)
    trib = (i[:, None] >= i[None, :]).astype(np.float32)
    c[:, 1, :] = trif
    c[:, 2, :] = trib
    for a in range(4):
        c[:, 3 + a, :] = np.where(i[:, None] <= i[None, :], 0.0, NEG)
        c[:, 7 + a, :] = np.where(i[:, None] >= i[None, :], 0.0, NEG)
    c[:, 11, :] = 1.0
    c[:, 12, :] = -trif
    c[:, 13, :] = -trib
    return c


def _rope_table(n, dim):
    t = np.arange(n)
    row = (t // 64).astype(np.float32)
    col = (t % 64).astype(np.float32)
    quarter = dim // 4
    inv = (np.float32(10000.0) ** (-np.arange(quarter, dtype=np.float32) / np.float32(quarter))).astype(np.float32)
    ang = np.concatenate([row[:, None] * inv, col[:, None] * inv], axis=-1).astype(np.float32)
    cs = np.stack([np.cos(ang), np.sin(ang)], axis=0).astype(np.float32)
    return np.ascontiguousarray(cs.reshape(2, 16, 128, dim // 2).transpose(2, 0, 1, 3))


def _na_bias_table(rpb):
    L = rpb.shape[0]
    tab = np.full((L, 5, 128, 8, 5, 128), NEG, np.float32)
    k = np.arange(128)
    q = np.arange(128)
    for cls, t in enumerate((5, 0, 1, 14, 15)):
        _, chunks = na_chunks(t)
        r = 2 * t + q // 64
        qc = q % 64
        rs = np.clip(r - 4, 0, 24)
        cs = np.clip(qc - 8, 0, 48)
        for slot, c in enumerate(chunks):
            kr = 2 * c + k // 64
            kc = k % 64
            ok = ((kr[:, None] >= rs[None, :]) & (kr[:, None] <= rs[None, :] + 7) &
                  (kc[:, None] >= cs[None, :]) & (kc[:, None] <= cs[None, :] + 15))
            ri = np.clip(kr[:, None] - r[None, :] + 7, 0, 14)
            ci = np.clip(kc[:, None] - qc[None, :] + 15, 0, 30)
            g = rpb[:, :, ri, ci]
            g = np.where(ok[None, None], g, np.float32(NEG))
            tab[:, cls, :, :, slot, :] = g.transpose(0, 2, 1, 3)
    return tab


_PROG = {}


def _get_prog(debug=False, layers=(0, 1)):
    key = (debug, tuple(layers))
    if key not in _PROG:
        _PROG[key] = build_program(debug=debug, layers=layers)
    return _PROG[key]


def make_in_maps(inputs):
    f = lambda a: np.ascontiguousarray(np.asarray(a, dtype=np.float32))
    shared = {
        "w_ada": f(inputs["w_ada"]), "b_ada": f(inputs["b_ada"]),
        "g4": f(np.stack([inputs["g_pre1"], inputs["g_post1"], inputs["g_pre2"], inputs["g_post2"]], axis=1)),
        "w_in": f(inputs["w_in"]), "nab": _na_bias_table(f(inputs["na_rpb"])),
        "mla_g_q": f(inputs["mla_g_q"]), "mla_g_kv": f(inputs["mla_g_kv"]),
        "mla_w_uq": f(inputs["mla_w_uq"]), "mla_w_ukv": f(inputs["mla_w_ukv"]),
        "gqa_g": f(np.stack([inputs["gqa_g_q"], inputs["gqa_g_k"]], axis=1)),
        "conv_wT": f(np.transpose(np.asarray(inputs["ssm_conv_w"]), (0, 2, 1))),
        "conv_b": f(inputs["ssm_conv_b"]),
        "ssm_small": f(np.concatenate([np.asarray(inputs["ssm_a_log"]).reshape(2, 16), np.asarray(inputs["ssm_dt_bias"]).reshape(2, 16),
                                       np.asarray(inputs["ssm_d"]).reshape(2, 8)], axis=1)),
        "ssm_g_norm": f(inputs["ssm_g_norm"]),
        "w_branch": f(inputs["w_branch"]), "w_out": f(inputs["w_out"]),
        "ffn_w1": f(inputs["ffn_w1"]), "ffn_w3": f(inputs["ffn_w3"]), "ffn_w2": f(inputs["ffn_w2"]),
        "consts": _host_consts(), "ropeg": _rope_table(2048, 64), "ropem": _rope_table(2048, 32),
    }
    x = f(inputs["x"]); c = f(inputs["c"]); ctx = f(inputs["ctx"]); c_ctx = f(inputs["c_ctx"])
    maps = []
    for i in range(8):
        m = dict(shared)
        m["x"] = x[2 * i:2 * i + 2]
        m["ctx"] = ctx[2 * i:2 * i + 2]
        m["cvec"] = np.ascontiguousarray(np.concatenate([c[2 * i:2 * i + 2], c_ctx[None, :]], axis=0))
        maps.append(m)
    return maps


def kernel(**inputs):
    nc = _get_prog()
    maps = make_in_maps(inputs)
    res = run_bass_kernel_spmd(nc, maps, core_ids=list(range(8)))
    return np.concatenate([np.asarray(r["out"], dtype=np.float32) for r in res.results], axis=0)
```
